# Optimizing a Trainium2 kernel written in Bass

```python
import jax, jax.numpy as jnp
from jax import lax
import numpy as np

D_MODEL = 2048
BATCH = 4
SEQ = 4096
DEPTH = 2

GRID_W = 64
CTX_LEN = 256
HEAD_DIM = 64
RWKV_HEADS = D_MODEL // (2 * HEAD_DIM)
RWKV_DIM = RWKV_HEADS * HEAD_DIM
DECAY_LORA = 64
ICLR_LORA = 64
GATE_LORA = 160
RWKV_COLS = 3 * RWKV_DIM + 2 * DECAY_LORA + 2 * ICLR_LORA + GATE_LORA
GQA_Q_HEADS = D_MODEL // (2 * HEAD_DIM)
GQA_KV_HEADS = GQA_Q_HEADS // 4
GQA_COLS = (GQA_Q_HEADS + 2 * GQA_KV_HEADS) * HEAD_DIM
IN_COLS_AB = GQA_COLS + RWKV_COLS
OUT_COLS_AB = GQA_Q_HEADS * HEAD_DIM + RWKV_DIM
NA_HEADS = D_MODEL // HEAD_DIM
NA_ROWS_MAX = 8
NA_COLS = 16
D_FF = 4 * D_MODEL
Q_BLOCK = 128
ROPE_THETA = 10000.0
ROPE_PAIRS_PER_AXIS = HEAD_DIM // 4
NORM_EPS = 1e-6
LNX_EPS = 64e-5

kernel_name = 'hybrid_dit_rwkv7_gqa_natten'


def _split(t, sizes):
    return jnp.split(t, [int(s) for s in np.cumsum(sizes)[:-1]], axis=-1)


def _heads(t, n_heads):
    return t.reshape(t.shape[:-1] + (n_heads, t.shape[-1] // n_heads))


def _rms_norm(x, g):
    xf = x.astype(jnp.float32)
    y = xf * lax.rsqrt(jnp.mean(xf * xf, axis=-1, keepdims=True) + NORM_EPS)
    return (y * g.astype(jnp.float32)).astype(x.dtype)


def _modulate(h, shift, scale):
    return h * (1.0 + scale) + shift


def _sq_relu_mlp(h, w1, w2):
    return jnp.square(jax.nn.relu(h @ w1)) @ w2


def _qshift_grid(p):
    b, t, ch = p.shape
    rows = t // GRID_W
    p4 = p.reshape(b, rows, GRID_W, ch // 4, 4)
    from_left = jnp.pad(p4[..., 0], ((0, 0), (0, 0), (1, 0), (0, 0)))[:, :, :-1]
    from_right = jnp.pad(p4[..., 1], ((0, 0), (0, 0), (0, 1), (0, 0)))[:, :, 1:]
    from_up = jnp.pad(p4[..., 2], ((0, 0), (1, 0), (0, 0), (0, 0)))[:, :-1]
    from_down = jnp.pad(p4[..., 3], ((0, 0), (0, 1), (0, 0), (0, 0)))[:, 1:]
    return jnp.stack([from_left, from_right, from_up, from_down], axis=-1).reshape(b, t, ch)


def _shift_seq(p):
    b, t, ch = p.shape
    p2 = p.reshape(b, t, ch // 2, 2)
    prev = jnp.pad(p2[..., 0], ((0, 0), (1, 0), (0, 0)))[:, :-1]
    nxt = jnp.pad(p2[..., 1], ((0, 0), (0, 1), (0, 0)))[:, 1:]
    return jnp.stack([prev, nxt], axis=-1).reshape(b, t, ch)


def _axial_rope(n):
    t = jnp.arange(n, dtype=jnp.int32)
    row = (t // GRID_W).astype(jnp.float32)
    col = (t % GRID_W).astype(jnp.float32)
    inv = ROPE_THETA ** (-jnp.arange(ROPE_PAIRS_PER_AXIS, dtype=jnp.float32) / ROPE_PAIRS_PER_AXIS)
    ang = jnp.concatenate([row[:, None] * inv, col[:, None] * inv], axis=-1)
    return jnp.cos(ang), jnp.sin(ang)


def _rope(x, cos, sin):
    half = x.shape[-1] // 2
    xf = x.astype(jnp.float32)
    x1, x2 = xf[..., :half], xf[..., half:]
    cs, sn = cos[None, :, None, :], sin[None, :, None, :]
    return jnp.concatenate([x1 * cs - x2 * sn, x2 * cs + x1 * sn], axis=-1).astype(x.dtype)


def _gqa_dense(q5, k, v):
    s = jnp.einsum('bqkgd,bskd->bkgqs', q5, k).astype(jnp.float32)
    p = jax.nn.softmax(s, axis=-1).astype(v.dtype)
    return jnp.einsum('bkgqs,bskd->bqkgd', p, v)


def _gqa_blocks(q, k, v):
    b, t, hq, dh = q.shape
    hkv = k.shape[2]
    nb = t // Q_BLOCK
    qb = q.reshape(b, nb, Q_BLOCK, hkv, hq // hkv, dh).transpose(1, 0, 2, 3, 4, 5)
    o = lax.map(lambda qi: _gqa_dense(qi, k, v), qb)
    return o.transpose(1, 0, 2, 3, 4, 5).reshape(b, t, hq * dh)


def _rwkv7_scan(r, decay, k, v, a, b, s0, reverse):
    def step(s, inp):
        r_t, w_t, k_t, v_t, a_t, b_t = inp
        sa = jnp.einsum('bhij,bhj->bhi', s, a_t)
        s = s * w_t[:, :, None, :] + sa[..., None] * b_t[:, :, None, :] + v_t[..., None] * k_t[:, :, None, :]
        return s, jnp.einsum('bhij,bhj->bhi', s, r_t)
    xs = tuple(jnp.moveaxis(t, 1, 0) for t in (r, decay, k, v, a, b))
    s_final, ys = lax.scan(step, s0, xs, reverse=reverse)
    return jnp.moveaxis(ys, 0, 1), s_final


def _rwkv7_prep(xm, w0_f, w0_b, ww2_f, ww2_b, a0_f, a0_b, wa2_f, wa2_b, wg2, k_k, k_a, r_k):
    f32 = jnp.float32
    r, k, v, xw_f, xw_b, xa_f, xa_b, xg = _split(
        xm, (RWKV_DIM,) * 3 + (DECAY_LORA,) * 2 + (ICLR_LORA,) * 2 + (GATE_LORA,))
    kk = _heads((k * k_k).astype(f32), RWKV_HEADS)
    kk = kk * lax.rsqrt(jnp.maximum(jnp.sum(kk * kk, axis=-1, keepdims=True), 1e-12))
    rh = _heads(r.astype(f32), RWKV_HEADS)
    vh = _heads(v.astype(f32), RWKV_HEADS)
    dirs = []
    for w0, ww2, a0, wa2, xw, xa in ((w0_f, ww2_f, a0_f, wa2_f, xw_f, xa_f),
                                     (w0_b, ww2_b, a0_b, wa2_b, xw_b, xa_b)):
        log_w = -jax.nn.softplus(-(w0 + jnp.tanh(xw) @ ww2).astype(f32)) - 0.5
        decay = jnp.exp(-jnp.exp(log_w))
        iclr = jax.nn.sigmoid((a0 + xa @ wa2).astype(f32))
        k_dir = _heads(k.astype(f32) * (1.0 + (iclr - 1.0) * k_a), RWKV_HEADS)
        iclr = _heads(iclr, RWKV_HEADS)
        bonus = jnp.sum(rh * k_dir * r_k, axis=-1, keepdims=True) * vh
        dirs.append((_heads(decay, RWKV_HEADS), k_dir, -kk, kk * iclr, bonus))
    gate = jax.nn.sigmoid(xg) @ wg2
    return rh, vh, dirs, gate


def _rwkv7_bidirectional(xm_lat, xm_ctx, w0_f, w0_b, ww2_f, ww2_b, a0_f, a0_b, wa2_f, wa2_b,
                         wg2, k_k, k_a, r_k, lnx_g, lnx_b):
    r_l, v_l, dirs_l, gate_l = _rwkv7_prep(xm_lat, w0_f, w0_b, ww2_f, ww2_b, a0_f, a0_b, wa2_f, wa2_b, wg2, k_k, k_a, r_k)
    r_c, v_c, dirs_c, gate_c = _rwkv7_prep(xm_ctx, w0_f, w0_b, ww2_f, ww2_b, a0_f, a0_b, wa2_f, wa2_b, wg2, k_k, k_a, r_k)
    s0 = jnp.zeros(r_c.shape[:1] + (RWKV_HEADS, HEAD_DIM, HEAD_DIM), jnp.float32)
    ys_l = []
    ys_c = []
    for dl, dc, reverse in zip(dirs_l, dirs_c, (False, True)):
        y_c, s_ctx = _rwkv7_scan(r_c, dc[0], dc[1], v_c, dc[2], dc[3], s0, reverse)
        y_l, _ = _rwkv7_scan(r_l, dl[0], dl[1], v_l, dl[2], dl[3], s_ctx, reverse)
        ys_l.append(y_l)
        ys_c.append(y_c)

    def finish(ys, dirs, gate, dtype):
        y = ys[0] + ys[1]
        mu = jnp.mean(y, axis=-1, keepdims=True)
        var = jnp.mean(jnp.square(y - mu), axis=-1, keepdims=True)
        yn = ((y - mu) * lax.rsqrt(var + LNX_EPS)).reshape(y.shape[:2] + (RWKV_DIM,))
        bonus = (dirs[0][4] + dirs[1][4]).reshape(yn.shape)
        return ((yn * lnx_g + lnx_b + bonus) * gate).astype(dtype)

    return finish(ys_l, dirs_l, gate_l, xm_lat.dtype), finish(ys_c, dirs_c, gate_c, xm_ctx.dtype)


def _mixer_rwkv7_gqa(h_lat, h_ctx, w_in, shift_mu, w0_f, w0_b, ww2_f, ww2_b, a0_f, a0_b, wa2_f, wa2_b,
                     wg2, k_k, k_a, r_k, lnx_g, lnx_b, q_norm, k_norm, w_out, need_ctx):
    p_lat = h_lat @ w_in
    p_ctx = h_ctx @ w_in
    qkv_sizes = (GQA_Q_HEADS * HEAD_DIM, GQA_KV_HEADS * HEAD_DIM, GQA_KV_HEADS * HEAD_DIM)
    q_l, k_l, v_l = _split(p_lat[..., :GQA_COLS], qkv_sizes)
    q_c, k_c, v_c = _split(p_ctx[..., :GQA_COLS], qkv_sizes)
    scale = HEAD_DIM ** -0.5
    cos, sin = _axial_rope(h_lat.shape[1])
    q_l = _rope(_rms_norm(_heads(q_l, GQA_Q_HEADS), q_norm), cos, sin) * scale
    k_l = _rope(_rms_norm(_heads(k_l, GQA_KV_HEADS), k_norm), cos, sin)
    q_c = _rms_norm(_heads(q_c, GQA_Q_HEADS), q_norm) * scale
    k_c = _rms_norm(_heads(k_c, GQA_KV_HEADS), k_norm)
    v_l = _heads(v_l, GQA_KV_HEADS)
    v_c = _heads(v_c, GQA_KV_HEADS)
    k_all = jnp.concatenate([k_l, k_c], axis=1)
    v_all = jnp.concatenate([v_l, v_c], axis=1)
    o_gqa_l = _gqa_blocks(q_l, k_all, v_all)

    rw_l = p_lat[..., GQA_COLS:]
    rw_c = p_ctx[..., GQA_COLS:]
    rw_l = rw_l + shift_mu * (_qshift_grid(rw_l) - rw_l)
    rw_c = rw_c + shift_mu * (_shift_seq(rw_c) - rw_c)
    o_rwkv_l, o_rwkv_c = _rwkv7_bidirectional(rw_l, rw_c, w0_f, w0_b, ww2_f, ww2_b, a0_f, a0_b, wa2_f, wa2_b,
                                              wg2, k_k, k_a, r_k, lnx_g, lnx_b)
    out_l = jnp.concatenate([o_gqa_l, o_rwkv_l], axis=-1) @ w_out
    if not need_ctx:
        return out_l, None
    b, n = q_c.shape[:2]
    q_c5 = q_c.reshape(b, n, GQA_KV_HEADS, GQA_Q_HEADS // GQA_KV_HEADS, HEAD_DIM)
    o_gqa_c = _gqa_dense(q_c5, k_c, v_c).reshape(b, n, GQA_Q_HEADS * HEAD_DIM)
    out_c = jnp.concatenate([o_gqa_c, o_rwkv_c], axis=-1) @ w_out
    return out_l, out_c


def _neighbourhood_attention(q, k, v, k_ctx, v_ctx, rpb):
    b, t, h, dh = q.shape
    rows = t // GRID_W
    kr = min(NA_ROWS_MAX, rows)
    kc = NA_COLS
    nk = kr * kc
    q_rows = q.reshape(b, rows, GRID_W, h, dh).transpose(1, 0, 2, 3, 4)
    col = jnp.arange(GRID_W, dtype=jnp.int32)
    col_start = jnp.clip(col - kc // 2, 0, GRID_W - kc)
    key_cols = col_start[:, None] + jnp.arange(kc, dtype=jnp.int32)[None, :]
    dcol = key_cols - col[:, None] + (NA_COLS - 1)

    def row_step(args):
        i, qi = args
        row_start = jnp.clip(i - kr // 2, 0, rows - kr)
        key_rows = row_start + jnp.arange(kr, dtype=jnp.int32)
        idx = (key_rows[None, :, None] * GRID_W + key_cols[:, None, :]).reshape(GRID_W, nk)
        kg = jnp.take(k, idx, axis=1)
        vg = jnp.take(v, idx, axis=1)
        drow = key_rows - i + (NA_ROWS_MAX - 1)
        bias = rpb[:, drow][:, :, dcol]
        bias = bias.transpose(0, 2, 1, 3).reshape(h, GRID_W, nk).astype(jnp.float32)
        s_win = jnp.einsum('bqhd,bqnhd->bhqn', qi, kg).astype(jnp.float32) + bias
        s_ctx = jnp.einsum('bqhd,bchd->bhqc', qi, k_ctx).astype(jnp.float32)
        p = jax.nn.softmax(jnp.concatenate([s_win, s_ctx], axis=-1), axis=-1).astype(v.dtype)
        return (jnp.einsum('bhqn,bqnhd->bqhd', p[..., :nk], vg)
                + jnp.einsum('bhqc,bchd->bqhd', p[..., nk:], v_ctx))

    o = lax.map(row_step, (jnp.arange(rows, dtype=jnp.int32), q_rows))
    return o.transpose(1, 0, 2, 3, 4).reshape(b, t, h * dh)


def _mixer_neighbourhood(h_lat, h_ctx, w_qkv, rpb, w_out, need_ctx):
    width = NA_HEADS * HEAD_DIM
    scale = HEAD_DIM ** -0.5
    q_l, k_l, v_l = [_heads(t, NA_HEADS) for t in _split(h_lat @ w_qkv, (width, width, width))]
    if need_ctx:
        q_c, k_c, v_c = [_heads(t, NA_HEADS) for t in _split(h_ctx @ w_qkv, (width, width, width))]
    else:
        k_c, v_c = [_heads(t, NA_HEADS) for t in _split(h_ctx @ w_qkv[:, width:], (width, width))]
    out_l = _neighbourhood_attention(q_l * scale, k_l, v_l, k_c, v_c, rpb) @ w_out
    if not need_ctx:
        return out_l, None
    b, n = q_c.shape[:2]
    o_c = _gqa_dense((q_c * scale)[:, :, :, None, :], k_c, v_c).reshape(b, n, width)
    return out_l, o_c @ w_out


def setup_inputs(seed: int = 0) -> dict:
    key = jax.random.key(seed)
    keys = jax.random.split(key, 64)
    counter = [0]

    def nxt():
        kk = keys[counter[0]]
        counter[0] += 1
        return kk

    def nrm(shape, scale):
        return jax.random.normal(nxt(), shape, jnp.float32) * scale

    def gain(shape):
        return 1.0 + nrm(shape, 0.1)

    def unif(shape, lo, hi):
        return jax.random.uniform(nxt(), shape, jnp.float32, lo, hi)

    d = D_MODEL
    inp = {}
    inp['x'] = nrm((BATCH, SEQ, d), 1.0)
    inp['c'] = nrm((BATCH, d), 1.0)
    inp['ctx'] = nrm((BATCH, CTX_LEN, d), 1.0)
    inp['c_ctx'] = nrm((d,), 1.0)
    inp['l0_norm1'] = gain((d,))
    inp['l0_norm2'] = gain((d,))
    inp['l0_ada_w'] = nrm((d, 6 * d), 0.5 * d ** -0.5)
    inp['l0_ada_b'] = nrm((6 * d,), 0.01)
    inp['l0_w_in'] = nrm((d, IN_COLS_AB), d ** -0.5)
    inp['l0_shift_mu'] = unif((RWKV_COLS,), 0.0, 1.0)
    inp['l0_w0_f'] = unif((RWKV_DIM,), -4.0, 1.0)
    inp['l0_w0_b'] = unif((RWKV_DIM,), -4.0, 1.0)
    inp['l0_ww2_f'] = nrm((DECAY_LORA, RWKV_DIM), 0.1)
    inp['l0_ww2_b'] = nrm((DECAY_LORA, RWKV_DIM), 0.1)
    inp['l0_a0_f'] = nrm((RWKV_DIM,), 0.1)
    inp['l0_a0_b'] = nrm((RWKV_DIM,), 0.1)
    inp['l0_wa2_f'] = nrm((ICLR_LORA, RWKV_DIM), 0.1)
    inp['l0_wa2_b'] = nrm((ICLR_LORA, RWKV_DIM), 0.1)
    inp['l0_wg2'] = nrm((GATE_LORA, RWKV_DIM), GATE_LORA ** -0.5)
    inp['l0_k_k'] = 0.85 + nrm((RWKV_DIM,), 0.1)
    inp['l0_k_a'] = gain((RWKV_DIM,))
    inp['l0_r_k'] = nrm((RWKV_HEADS, HEAD_DIM), 0.1)
    inp['l0_lnx_g'] = gain((RWKV_DIM,))
    inp['l0_lnx_b'] = nrm((RWKV_DIM,), 0.01)
    inp['l0_q_norm'] = gain((HEAD_DIM,))
    inp['l0_k_norm'] = gain((HEAD_DIM,))
    inp['l0_w_out'] = nrm((OUT_COLS_AB, d), OUT_COLS_AB ** -0.5)
    inp['l0_mlp_w1'] = nrm((d, D_FF), d ** -0.5)
    inp['l0_mlp_w2'] = nrm((D_FF, d), D_FF ** -0.5)
    inp['l1_norm1'] = gain((d,))
    inp['l1_norm2'] = gain((d,))
    inp['l1_ada_w'] = nrm((d, 6 * d), 0.5 * d ** -0.5)
    inp['l1_ada_b'] = nrm((6 * d,), 0.01)
    inp['l1_w_qkv'] = nrm((d, 3 * NA_HEADS * HEAD_DIM), d ** -0.5)
    inp['l1_rpb'] = nrm((NA_HEADS, 2 * NA_ROWS_MAX - 1, 2 * NA_COLS - 1), 0.02)
    inp['l1_w_out'] = nrm((NA_HEADS * HEAD_DIM, d), (NA_HEADS * HEAD_DIM) ** -0.5)
    inp['l1_mlp_w1'] = nrm((d, D_FF), d ** -0.5)
    inp['l1_mlp_w2'] = nrm((D_FF, d), D_FF ** -0.5)
    inp['final_norm'] = gain((d,))
    return inp


def reference(x, c, ctx, c_ctx,
              l0_norm1, l0_norm2, l0_ada_w, l0_ada_b, l0_w_in, l0_shift_mu,
              l0_w0_f, l0_w0_b, l0_ww2_f, l0_ww2_b, l0_a0_f, l0_a0_b, l0_wa2_f, l0_wa2_b,
              l0_wg2, l0_k_k, l0_k_a, l0_r_k, l0_lnx_g, l0_lnx_b, l0_q_norm, l0_k_norm,
              l0_w_out, l0_mlp_w1, l0_mlp_w2,
              l1_norm1, l1_norm2, l1_ada_w, l1_ada_b, l1_w_qkv, l1_rpb, l1_w_out,
              l1_mlp_w1, l1_mlp_w2, final_norm):
    norm1 = (l0_norm1, l1_norm1)
    norm2 = (l0_norm2, l1_norm2)
    ada_w = (l0_ada_w, l1_ada_w)
    ada_b = (l0_ada_b, l1_ada_b)
    mlp_w1 = (l0_mlp_w1, l1_mlp_w1)
    mlp_w2 = (l0_mlp_w2, l1_mlp_w2)
    for layer in range(DEPTH):
        last = layer == DEPTH - 1
        mod = jax.nn.silu(c) @ ada_w[layer] + ada_b[layer]
        mod_c = jax.nn.silu(c_ctx) @ ada_w[layer] + ada_b[layer]
        sh1, sc1, g1, sh2, sc2, g2 = jnp.split(mod[:, None, :], 6, axis=-1)
        csh1, csc1, cg1, csh2, csc2, cg2 = jnp.split(mod_c, 6, axis=-1)
        h = _modulate(_rms_norm(x, norm1[layer]), sh1, sc1)
        hc = _modulate(_rms_norm(ctx, norm1[layer]), csh1, csc1)
        if layer % 2 == 0:
            o, oc = _mixer_rwkv7_gqa(h, hc, l0_w_in, l0_shift_mu, l0_w0_f, l0_w0_b, l0_ww2_f, l0_ww2_b,
                                     l0_a0_f, l0_a0_b, l0_wa2_f, l0_wa2_b, l0_wg2, l0_k_k, l0_k_a, l0_r_k,
                                     l0_lnx_g, l0_lnx_b, l0_q_norm, l0_k_norm, l0_w_out, not last)
        else:
            o, oc = _mixer_neighbourhood(h, hc, l1_w_qkv, l1_rpb, l1_w_out, not last)
        x = x + g1 * o
        x = x + g2 * _sq_relu_mlp(_modulate(_rms_norm(x, norm2[layer]), sh2, sc2), mlp_w1[layer], mlp_w2[layer])
        if not last:
            ctx = ctx + cg1 * oc
            ctx = ctx + cg2 * _sq_relu_mlp(_modulate(_rms_norm(ctx, norm2[layer]), csh2, csc2),
                                           mlp_w1[layer], mlp_w2[layer])
    return _rms_norm(x, final_norm)
```

```python
import os
import numpy as np
import concourse.bass as bass
import concourse.mybir as mybir
from concourse.bass_utils import run_bass_kernel_spmd
from contextlib import ExitStack

F32 = mybir.dt.float32
BF16 = mybir.dt.bfloat16
AF = mybir.ActivationFunctionType
ALU = mybir.AluOpType

D = 2048
KC = 16
T = 4096
CT = 256
S0 = T + CT
NE = 2304
NX = NE + CT
NOWN = 2048
RW = 3488
GQ = 1536
NCOL0 = GQ + RW
EPS = 1e-6


class Ctx:
    COMPUTE = ('pe', 'act', 'dve', 'pool')

    def __init__(self, nc, n_dma_sems=20):
        self.nc = nc
        self.eng = {'pe': nc.tensor, 'act': nc.scalar, 'dve': nc.vector, 'pool': nc.gpsimd, 'sp': nc.sync}
        self.sem = {e: nc.alloc_semaphore('c_' + e) for e in self.COMPUTE}
        self.cnt = {e: 0 for e in self.COMPUTE}
        self.dq = {}
        for q in ('sp', 'pool', 'act'):
            self.dq[q] = dict(sems=[nc.alloc_semaphore(f'd_{q}{i}') for i in range(n_dma_sems)],
                              val=[0] * n_dma_sems, nxt=0)
        self.seen = {}
        self.lastw = {}
        self.lastr = {}
        self.psn = 0
        self.nrot = 8
        self.ps_tiles = [nc.alloc_psum_tensor(f"ps{i}", [128, 512], F32) for i in range(8)]
        self.rr = {}

    def _semobj(self, key):
        if isinstance(key, str):
            return self.sem[key]
        q, i = key
        return self.dq[q]['sems'][i]

    def _wait(self, eng, tok):
        key, val = tok
        if eng == 'pe' and key == 'pe':
            return
        if self.seen.get((eng, key), 0) >= val:
            return
        self.eng[eng].wait_ge(self._semobj(key), val)
        self.seen[(eng, key)] = val

    def _deps(self, eng, reads, writes):
        for k in reads:
            t = self.lastw.get(k)
            if t is not None:
                self._wait(eng, t)
        for k in writes:
            t = self.lastw.get(k)
            if t is not None:
                self._wait(eng, t)
            for t in self.lastr.get(k, {}).values():
                self._wait(eng, t)

    def _record(self, tok, reads, writes):
        for k in reads:
            self.lastr.setdefault(k, {})[tok[0]] = tok
        for k in writes:
            self.lastw[k] = tok
            self.lastr[k] = {}

    def op(self, eng, fn, reads=(), writes=(), signal=True):
        self._deps(eng, reads, writes)
        ins = fn(self.eng[eng])
        if signal:
            self.cnt[eng] += 1
            ins.then_inc(self.sem[eng], 1)
            tok = (eng, self.cnt[eng])
        else:
            tok = (eng, self.cnt[eng] + 1)
        self._record(tok, reads, writes)
        return ins

    def dma(self, q, out, in_, reads=(), writes=(), **kw):
        d = self.dq[q]
        i = d['nxt']
        d['nxt'] = (i + 1) % len(d['sems'])
        if d['val'][i] > 0:
            self._wait(q, ((q, i), d['val'][i]))
        self._deps(q, reads, writes)
        ins = self.eng[q].dma_start(out=out, in_=in_, **kw)
        d['val'][i] += 16
        ins.then_inc(d['sems'][i], 16)
        tok = ((q, i), d['val'][i])
        self._record(tok, reads, writes)
        return ins

    def barrier(self):
        toks = [(e, self.cnt[e]) for e in self.COMPUTE if self.cnt[e] > 0]
        for q, d in self.dq.items():
            for i, v in enumerate(d['val']):
                if v > 0:
                    toks.append(((q, i), v))
        for e in ('pe', 'act', 'dve', 'pool', 'sp'):
            for t in toks:
                if t[0] == e:
                    continue
                self._wait(e, t)
        self.lastw = {}
        self.lastr = {}

    def finish(self, q='sp'):
        for e in self.COMPUTE:
            if self.cnt[e] > 0:
                self._wait(q, (e, self.cnt[e]))
        for qq, d in self.dq.items():
            for i, v in enumerate(d['val']):
                if v > 0:
                    self._wait(q, ((qq, i), v))

    def ps(self):
        i = self.psn % self.nrot
        self.psn = (i + 1) % self.nrot
        return self.ps_tiles[i], f'ps{i}'

    def psh(self):
        i = getattr(self, '_hn', 0) % len(self.hbanks)
        self._hn = (i + 1) % len(self.hbanks)
        b = self.hbanks[i]
        return self.ps_tiles[b], f'ps{b}'

    def pick(self, name, choices):
        i = self.rr.get(name, 0)
        self.rr[name] = i + 1
        return choices[i % len(choices)]


class Pool:
    def __init__(self, P, name, shape, dtype, n):
        self.t = [P.sb(f"{name}{i}", shape, dtype) for i in range(n)]
        self.k = [f"{name}{i}" for i in range(n)]
        self.i = 0

    def get(self):
        i = self.i
        self.i = (i + 1) % len(self.t)
        return self.t[i], self.k[i]


class Prog:
    def __init__(self, debug=()):
        self.debug = set(debug)
        nc = self.nc = bass.Bass("TRN2", target_bir_lowering=False)
        self.C = Ctx(nc)
        self.inp = {}
        self.outs = {}
        self.scr = {}
        self.gstack = ExitStack()
        self.stack = self.gstack

    def begin_phase(self):
        self.C.barrier()
        if self.stack is not self.gstack:
            self.stack.close()
        self.stack = ExitStack()

    def sub_begin(self):
        self._saved = self.stack
        self.stack = ExitStack()

    def sub_end(self):
        self.C.barrier()
        self.stack.close()
        self.stack = self._saved

    def din(self, name, shape, dt=F32):
        self.inp[name] = self.nc.dram_tensor(name, list(shape), dt, kind="ExternalInput").ap()
        return self.inp[name]

    def dout(self, name, shape, dt=F32):
        self.outs[name] = self.nc.dram_tensor(name, list(shape), dt, kind="ExternalOutput").ap()
        return self.outs[name]

    def dscr(self, name, shape, dt=F32):
        self.scr[name] = self.nc.dram_tensor(name, list(shape), dt, kind="Internal").ap()
        return self.scr[name]

    def sb(self, name, shape, dt=F32):
        self._uid = getattr(self, '_uid', 0) + 1
        return self.stack.enter_context(self.nc.sbuf_tensor(f"{name}_u{self._uid}", list(shape), dt))

    def load_consts(self):
        C = self.C
        self.din('ident', [128, 128])
        self.din('ones_blk', [128, 128])
        self.ident = self.sb('ident_sb', [128, 128])
        self.ones = self.sb('ones_sb', [128, 128])
        self.ones_blk = self.sb('ones_blk_sb', [128, 128])
        C.dma('sp', self.ident[:], self.inp['ident'][:, :], writes=['ident'])
        C.dma('sp', self.ones_blk[:], self.inp['ones_blk'][:, :], writes=['ones_blk'])
        C.op('dve', lambda e: e.memset(self.ones[:], 1.0), writes=['ones'])
        self.din('ccol', [128, 32])
        self.din('ada_b', [128, 192])
        self.din('nrm', [128, 80])
        self.ccol = self.sb('ccol_sb', [128, 32])
        self.ada_b = self.sb('ada_b_sb', [128, 192])
        self.nrm = self.sb('nrm_sb', [128, 80])
        C.dma('sp', self.ccol[:], self.inp['ccol'][:, :], writes=['ccol'])
        C.dma('sp', self.ada_b[:], self.inp['ada_b'][:, :], writes=['ada_b'])
        C.dma('sp', self.nrm[:], self.inp['nrm'][:, :], writes=['nrm'])

    def phase_mod(self):
        C, nc = self.C, self.nc
        self.din('ada_w0', [D, 6 * D])
        self.din('ada_w1', [D, 6 * D])
        self.mod = self.sb('mod', [128, 2 * 96 * 2])
        modv = self.mod[:].rearrange("p (l j v) -> p l j v", l=2, j=96)
        self.Acol = self.sb('Acol', [128, 2 * 2 * 16 * 2])
        Av = self.Acol[:].rearrange("p (l w k v) -> p l w k v", l=2, w=2, k=16)
        self.begin_phase()
        s = self.sb('silu_c', [128, 32])
        C.op('act', lambda e: e.activation(out=s[:], in_=self.ccol[:], func=AF.Silu), reads=['ccol'], writes=['silu_c'])
        NCB = 768
        wp = Pool(self, 'adaw', [128, KC * NCB], F32, 2)
        for L in range(2):
            W = self.inp[f'ada_w{L}'].rearrange("(k p) c -> p k c", p=128)
            for cb in range(6 * D // NCB):
                wt, wk = wp.get()
                wv = wt[:].rearrange("p (k c) -> p k c", k=KC)
                for kq in range(4):
                    C.dma('sp', wv[:, kq * 4:(kq + 1) * 4, :], W[:, kq * 4:(kq + 1) * 4, cb * NCB:(cb + 1) * NCB], writes=[wk])
                pt, pk = C.ps()
                for j in range(NCB // 128):
                    for kc in range(KC):
                        C.op('pe', lambda e, j=j, kc=kc: e.matmul(pt[:, j * 2:j * 2 + 2], lhsT=wv[:, kc, j * 128:(j + 1) * 128],
                                                                 rhs=s[:, kc * 2:kc * 2 + 2], start=(kc == 0), stop=(kc == KC - 1)),
                             reads=[wk, 'silu_c'], writes=[pk], signal=(kc == KC - 1 and j == NCB // 128 - 1))
                for j in range(NCB // 128):
                    jg = cb * (NCB // 128) + j
                    C.op('dve', lambda e, j=j, jg=jg: e.tensor_scalar(out=modv[:, L, jg, :], in0=pt[:, j * 2:j * 2 + 2],
                                                                     scalar1=self.ada_b[:, L * 96 + jg:L * 96 + jg + 1], scalar2=None, op0=ALU.add),
                         reads=[pk, 'ada_b'], writes=['mod'])
            for w, (sci, nidx) in enumerate(((1, 2 * L), (4, 2 * L + 1))):
                for v in range(2):
                    C.op('dve', lambda e, w=w, sci=sci, nidx=nidx, v=v: e.scalar_tensor_tensor(
                        out=Av[:, L, w, :, v], in0=modv[:, L, sci * 16:(sci + 1) * 16, v], scalar=1.0,
                        in1=self.nrm[:, nidx * 16:(nidx + 1) * 16], op0=ALU.add, op1=ALU.mult),
                         reads=['mod', 'nrm'], writes=['Acol'])
        self.modv, self.Av = modv, Av

    def mcol(self, L, s, kc, v):
        return self.modv[:, L, s * 16 + kc, v:v + 1]

    def norm_mod(self, xt, xk, n, h, hk, Afn, shfn, tmpp, sqp):
        C = self.C
        pt, pk = C.ps()
        for kc in range(KC):
            sq, sqk = sqp.get()
            C.op('act', lambda e, kc=kc, sq=sq: e.activation(out=sq[:, 0:n], in_=xt[:, kc, :], func=AF.Square), reads=[xk], writes=[sqk])
            C.op('pe', lambda e, kc=kc, sq=sq: e.matmul(pt[:, 0:n], lhsT=self.ones[:], rhs=sq[:, 0:n], start=(kc == 0), stop=(kc == KC - 1)),
                 reads=[sqk, 'ones'], writes=[pk])
        rs, rsk = sqp.get()
        C.op('act', lambda e: e.activation(out=rs[:, 0:n], in_=pt[:, 0:n], func=AF.Sqrt, bias=EPS, scale=1.0 / D), reads=[pk], writes=[rsk])
        C.op('dve', lambda e: e.reciprocal(out=rs[:, 0:n], in_=rs[:, 0:n]), reads=[rsk], writes=[rsk])
        for kc in range(KC):
            tm, tmk = tmpp.get()
            eng = 'dve'
            C.op(eng, lambda e, kc=kc, tm=tm: e.tensor_tensor(out=tm[:, 0:n], in0=xt[:, kc, :], in1=rs[:, 0:n], op=ALU.mult),
                 reads=[xk, rsk], writes=[tmk])
            if shfn is not None:
                C.op('act', lambda e, kc=kc, tm=tm: e.activation(out=h[:, kc, :], in_=tm[:, 0:n], func=AF.Identity, bias=shfn(kc), scale=Afn(kc)),
                     reads=[tmk, 'mod', 'Acol'], writes=[hk])
            else:
                C.op('act', lambda e, kc=kc, tm=tm: e.activation(out=h[:, kc, :], in_=tm[:, 0:n], func=AF.Copy, scale=Afn(kc)),
                     reads=[tmk, 'nrm'], writes=[hk])

    def wload_init(self, WB=256, nst=2, nwb=3):
        self.WB = WB
        self.wst = Pool(self, 'wst', [128, KC * WB], F32, nst)
        self.wbp = Pool(self, 'wbf', [128, KC * WB], BF16, nwb)

    def wstream(self, blocks, pf=1):
        self._wq = list(blocks)
        self._wi = 0
        self._wissued = []
        self._wpf = pf

    def wnext(self):
        while len(self._wissued) <= self._wi + self._wpf and len(self._wissued) < len(self._wq):
            b = self._wq[len(self._wissued)]
            self._wissued.append(self.wload(*b))
        r = self._wissued[self._wi]
        self._wi += 1
        return r

    def wload(self, Wv, c0, ncols, kchunks=KC, k0=0):
        C = self.C
        st, sk = self.wst.get()
        sv = st[:].rearrange("p (k c) -> p k c", k=KC)[:, 0:kchunks, 0:ncols]
        wb, wk = self.wbp.get()
        wv = wb[:].rearrange("p (k c) -> p k c", k=KC)[:, 0:kchunks, 0:ncols]
        step = max(1, kchunks // 4)
        for kq in range(0, kchunks, step):
            ke = min(kchunks, kq + step)
            C.dma('sp', sv[:, kq:ke, :], Wv[:, k0 + kq:k0 + ke, c0:c0 + ncols], writes=[sk])
        for kq in range(0, kchunks, step):
            ke = min(kchunks, kq + step)
            eng = C.pick('wcast', ['pool', 'dve', 'pool', 'act'])
            if eng == 'act':
                C.op('act', lambda e: e.activation(out=wv[:, kq:ke, :], in_=sv[:, kq:ke, :], func=AF.Copy), reads=[sk], writes=[wk])
            else:
                C.op(eng, lambda e: e.tensor_copy(out=wv[:, kq:ke, :], in_=sv[:, kq:ke, :]), reads=[sk], writes=[wk])
        return wv, wk

    def phase_proj0(self):
        C, nc = self.C, self.nc
        xT = self.din('xT', [D, S0]).rearrange("(k p) t -> p k t", p=128)
        W = self.din('w_in', [D, NCOL0]).rearrange("(k p) c -> p k c", p=128)
        pT = self.dscr('pT', [NCOL0, S0])
        vtok = self.dscr('vtok', [S0, 256], BF16)
        xp = Pool(self, 'p1x', [128, KC * 512], F32, 1)
        hp = Pool(self, 'p1h', [128, KC * 512], BF16, 2)
        sqp = Pool(self, 'p1sq', [128, 512], F32, 3)
        tmpp = Pool(self, 'p1tm', [128, 512], F32, 4)
        WB = 256
        self.wload_init(WB, 3, 4)
        evp = Pool(self, 'p1ev', [128, 512], F32, 4)
        vp = Pool(self, 'p1v', [128, 256], BF16, 3)
        tiles = [(i * 512, 512, 0) for i in range(8)] + [(T, CT, 1)]
        VC0, VC1 = 1280, 1536
        fm_blocks = [(c, min(WB, NCOL0 - c)) for c in list(range(0, VC0, WB)) + list(range(VC1, NCOL0, WB))]
        wl = []
        for _ in tiles:
            wl += [(W, c0, nc_, KC, 0) for (c0, nc_) in fm_blocks] + [(W, VC0, 256, KC, 0)]
        self.wstream(wl, 2)
        for (t0, n, v) in tiles:
            xt, xk = xp.get()
            xv = xt[:].rearrange("p (k t) -> p k t", k=KC)[:, :, 0:n]
            for kq in range(4):
                C.dma('sp', xv[:, kq * 4:(kq + 1) * 4, :], xT[:, kq * 4:(kq + 1) * 4, t0:t0 + n], writes=[xk])
            ht, hk = hp.get()
            hv = ht[:].rearrange("p (k t) -> p k t", k=KC)[:, :, 0:n]
            self.norm_mod(xv, xk, n, hv, hk, lambda kc: self.Av[:, 0, 0, kc, v:v + 1], lambda kc: self.mcol(0, 0, kc, v), tmpp, sqp)
            if 'h0' in self.debug:
                self._dbg_h(hv, hk, t0, n)
            for (c0, nc_) in fm_blocks:
                wv, wk = self.wnext()
                for m0 in range(0, nc_, 128):
                    m = min(128, nc_ - m0)
                    pt, pk = C.ps()
                    for kc in range(KC):
                        C.op('pe', lambda e, kc=kc, m0=m0, m=m: e.matmul(pt[0:m, 0:n], lhsT=wv[:, kc, m0:m0 + m], rhs=hv[:, kc, :],
                                                                        start=(kc == 0), stop=(kc == KC - 1)),
                             reads=[wk, hk], writes=[pk], signal=(kc == KC - 1))
                    ev, evk = evp.get()
                    eng = C.pick('p1ev', ['act', 'dve'])
                    if eng == 'act':
                        C.op('act', lambda e, m=m: e.activation(out=ev[0:m, 0:n], in_=pt[0:m, 0:n], func=AF.Copy), reads=[pk], writes=[evk])
                    else:
                        C.op('dve', lambda e, m=m: e.tensor_copy(out=ev[0:m, 0:n], in_=pt[0:m, 0:n]), reads=[pk], writes=[evk])
                    C.dma('sp', pT[c0 + m0:c0 + m0 + m, t0:t0 + n], ev[0:m, 0:n], reads=[evk], writes=['pT'])
            wv, wk = self.wnext()
            for s0 in range(0, n, 128):
                pt, pk = C.ps()
                for kc in range(KC):
                    C.op('pe', lambda e, kc=kc, s0=s0: e.matmul(pt[:, 0:256], lhsT=hv[:, kc, s0:s0 + 128], rhs=wv[:, kc, :],
                                                                start=(kc == 0), stop=(kc == KC - 1)),
                         reads=[wk, hk], writes=[pk], signal=(kc == KC - 1))
                vt, vk = vp.get()
                C.op('act', lambda e: e.activation(out=vt[:], in_=pt[:, 0:256], func=AF.Copy), reads=[pk], writes=[vk])
                C.dma('sp', vtok[t0 + s0:t0 + s0 + 128, :], vt[:], reads=[vk], writes=['vtok'])


    def headnorm(self, raw, rawk, n, colscalar, tp, sqp):
        C = self.C
        sq, sqk = sqp.get()
        C.op('act', lambda e: e.activation(out=sq[:, 0:n], in_=raw, func=AF.Square), reads=[rawk], writes=[sqk])
        pt, pk = C.ps()
        C.op('pe', lambda e: e.matmul(pt[:, 0:n], lhsT=self.ones_blk[:], rhs=sq[:, 0:n], start=True, stop=True), reads=[sqk, 'ones_blk'], writes=[pk])
        rs, rsk = sqp.get()
        C.op('act', lambda e: e.activation(out=rs[:, 0:n], in_=pt[:, 0:n], func=AF.Sqrt, bias=EPS, scale=1.0 / 64), reads=[pk], writes=[rsk])
        C.op('dve', lambda e: e.reciprocal(out=rs[:, 0:n], in_=rs[:, 0:n]), reads=[rsk], writes=[rsk])
        kn, knk = tp.get()
        C.op('dve', lambda e: e.scalar_tensor_tensor(out=kn[:, 0:n], in0=raw, scalar=colscalar, in1=rs[:, 0:n], op0=ALU.mult, op1=ALU.mult),
             reads=[rawk, rsk, 'qkn'], writes=[knk])
        return kn, knk

    def rope(self, kn, knk, n, cs0, out, outk, tp):
        C = self.C
        pt, pk = C.ps()
        C.op('pe', lambda e: e.matmul(pt[:, 0:n], lhsT=self.rotm[:], rhs=kn[:, 0:n], start=True, stop=True), reads=[knk, 'rotm'], writes=[pk])
        t1, t1k = tp.get()
        C.op('dve', lambda e: e.tensor_tensor(out=t1[:, 0:n], in0=kn[:, 0:n], in1=self.cos[:, cs0:cs0 + n], op=ALU.mult), reads=[knk, 'cos'], writes=[t1k])
        t2, t2k = tp.get()
        C.op('dve', lambda e: e.tensor_tensor(out=t2[:, 0:n], in0=pt[:, 0:n], in1=self.sin[:, cs0:cs0 + n], op=ALU.mult), reads=[pk, 'sin'], writes=[t2k])
        C.op('dve', lambda e: e.tensor_tensor(out=out, in0=t1[:, 0:n], in1=t2[:, 0:n], op=ALU.add), reads=[t1k, t2k], writes=[outk])

    def phase_gqa(self):
        C, nc = self.C, self.nc
        pT = self.scr['pT']
        vtok = self.scr['vtok']
        attnT = self.dscr('attnT', [D, NX], BF16)
        self.din('qkn', [128, 2])
        self.din('rotm', [128, 128])
        self.din('rope_cos', [128, T])
        self.din('rope_sin', [128, T])
        KT = [self.sb(f'KT{i}', [128, S0], BF16) for i in range(2)]
        QT = [self.sb(f'QT{i}', [128, NX], BF16) for i in range(8)]
        Vaug = self.sb('Vaug', [128, 34 * 4 * 128], BF16)
        Vv = Vaug[:].rearrange("p (c h d) -> p c h d", c=34, h=4)
        qkn = self.sb('qkn_sb', [128, 2])
        self.rotm = self.sb('rotm_sb', [128, 128])
        C.dma('sp', qkn[:], self.inp['qkn'][:, :], writes=['qkn'])
        C.dma('sp', self.rotm[:], self.inp['rotm'][:, :], writes=['rotm'])
        self.sub_begin()
        self.cos = self.sb('cos_sb', [128, T])
        self.sin = self.sb('sin_sb', [128, T])
        for q4 in range(4):
            C.dma('sp', self.cos[:, q4 * 1024:(q4 + 1) * 1024], self.inp['rope_cos'][:, q4 * 1024:(q4 + 1) * 1024], writes=['cos'])
            C.dma('sp', self.sin[:, q4 * 1024:(q4 + 1) * 1024], self.inp['rope_sin'][:, q4 * 1024:(q4 + 1) * 1024], writes=['sin'])
        rawp = Pool(self, 'g_raw', [128, S0], F32, 2)
        sqp = Pool(self, 'g_sq', [128, 512], F32, 4)
        tp = Pool(self, 'g_tp', [128, 512], F32, 6)
        vst = self.sb('g_vst', [128, 34 * 256], BF16)
        vsv = vst[:].rearrange("p (c x) -> p c x", c=34)
        vsrc = vtok.rearrange("(c p) x -> p c x", p=128)
        for c4 in range(0, 34, 6):
            c5 = min(34, c4 + 6)
            C.dma('sp', vsv[:, c4:c5, :], vsrc[:, c4:c5, :], writes=['g_vst'])
        C.op('dve', lambda e: e.memset(Vaug[:], 1.0), writes=['Vaug'])
        for h in range(4):
            C.op('dve', lambda e, h=h: e.tensor_copy(out=Vv[:, :, h, 0:64], in_=vsv[:, :, h * 64:(h + 1) * 64]), reads=['g_vst'], writes=['Vaug'])
        ktiles = [(i * 512, 512, True) for i in range(8)] + [(T, CT, False)]
        for kt in range(2):
            raw, rawk = rawp.get()
            for q4 in range(0, S0, 1088):
                C.dma('sp', raw[:, q4:q4 + 1088], pT[1024 + kt * 128:1024 + (kt + 1) * 128, q4:q4 + 1088], writes=[rawk])
            for (t0, n, lat) in ktiles:
                kn, knk = self.headnorm(raw[:, t0:t0 + n], rawk, n, qkn[:, 1:2], tp, sqp)
                if lat:
                    self.rope(kn, knk, n, t0, KT[kt][:, t0:t0 + n], f'KT{kt}', tp)
                else:
                    C.op('act', lambda e: e.activation(out=KT[kt][:, t0:t0 + n], in_=kn[:, 0:n], func=AF.Copy), reads=[knk], writes=[f'KT{kt}'])
        self.qpairs = [(0, 4), (1, 5), (2, 6), (3, 7), (8, 12), (9, 13), (10, 14), (11, 15)]
        qtiles = [(i * 512, i * 512, 512, True) for i in range(4)] + [(2048, 2048, 256, True), (T, NE, CT, False)]
        for j, (a, b) in enumerate(self.qpairs):
            raw, rawk = rawp.get()
            for hh, hq in enumerate((a, b)):
                C.dma('sp', raw[hh * 64:(hh + 1) * 64, 0:NE], pT[hq * 64:(hq + 1) * 64, 0:NE], writes=[rawk])
                C.dma('sp', raw[hh * 64:(hh + 1) * 64, NE:NX], pT[hq * 64:(hq + 1) * 64, T:S0], writes=[rawk])
            for (src0, x0, n, lat) in qtiles:
                kn, knk = self.headnorm(raw[:, x0:x0 + n], rawk, n, qkn[:, 0:1], tp, sqp)
                if lat:
                    self.rope(kn, knk, n, src0, QT[j][:, x0:x0 + n], f'QT{j}', tp)
                else:
                    C.op('act', lambda e: e.activation(out=QT[j][:, x0:x0 + n], in_=kn[:, 0:n], func=AF.Copy), reads=[knk], writes=[f'QT{j}'])
        if 'qk' in self.debug:
            o = self.dout('dbg_KT', [256, S0], BF16)
            for i in range(2):
                C.dma('sp', o[i * 128:(i + 1) * 128, :], KT[i][:], reads=[f'KT{i}'])
            o = self.dout('dbg_QT', [1024, NX], BF16)
            for i in range(8):
                C.dma('sp', o[i * 128:(i + 1) * 128, :], QT[i][:], reads=[f'QT{i}'])
        self.sub_end()
        C.nrot = 6
        ptp = Pool(self, 'g_pt', [128, 512], BF16, 6)
        osp = Pool(self, 'g_os', [128, 512], F32, 2)
        rsp = Pool(self, 'g_rs', [64, 512], F32, 2)
        onp = Pool(self, 'g_on', [128, 512], BF16, 2)
        acc_i = 0
        for hq in range(16):
            kvh = hq // 4
            base = (kvh % 2) * 64
            j = [i for i, pr in enumerate(self.qpairs) if hq in pr][0]
            jobs = [(i * 512, 512, list(range(34))) for i in range(4)] + [(2048, 256, list(range(34))), (NE, CT, [32, 33])]
            for (q0, n, chunks) in jobs:
                po = C.ps_tiles[6 + acc_i % 2]
                pok = f'ps{6 + acc_i % 2}'
                acc_i += 1
                pend = []

                def pv(item):
                    pb, pbk, c, ci = item
                    C.op('pe', lambda e: e.matmul(po[:, 0:n], lhsT=Vv[:, c, kvh, :], rhs=pb[:, 0:n],
                                                  start=(ci == 0), stop=(ci == len(chunks) - 1)),
                         reads=['Vaug', pbk], writes=[pok])
                for ci, c in enumerate(chunks):
                    pt, pk = C.ps()
                    C.op('pe', lambda e, c=c: e.matmul(pt[:, 0:n], lhsT=KT[kvh // 2][base:base + 64, c * 128:(c + 1) * 128],
                                                      rhs=QT[j][base:base + 64, q0:q0 + n], start=True, stop=True),
                         reads=[f'KT{kvh // 2}', f'QT{j}'], writes=[pk])
                    pb, pbk = ptp.get()
                    C.op('act', lambda e: e.activation(out=pb[:, 0:n], in_=pt[:, 0:n], func=AF.Exp), reads=[pk], writes=[pbk])
                    pend.append((pb, pbk, c, ci))
                    if len(pend) > 2:
                        pv(pend.pop(0))
                while pend:
                    pv(pend.pop(0))
                osb, osk = osp.get()
                C.op('dve', lambda e: e.tensor_copy(out=osb[:, 0:n], in_=po[:, 0:n]), reads=[pok], writes=[osk])
                rs, rsk = rsp.get()
                C.dma('sp', rs[:, 0:n], osb[64:128, 0:n], reads=[osk], writes=[rsk])
                C.op('dve', lambda e: e.reciprocal(out=rs[:, 0:n], in_=rs[:, 0:n]), reads=[rsk], writes=[rsk])
                on, onk = onp.get()
                C.op('dve', lambda e: e.tensor_tensor(out=on[0:64, 0:n], in0=osb[0:64, 0:n], in1=rs[:, 0:n], op=ALU.mult), reads=[osk, rsk], writes=[onk])
                C.dma('sp', attnT[hq * 64:(hq + 1) * 64, q0:q0 + n], on[0:64, 0:n], reads=[onk], writes=['attnT'])
        C.nrot = 8

    def phase_rwkv_shift(self):
        C = self.C
        pT = self.scr['pT']
        rwT = self.dscr('rwT', [RW, S0])
        self.din('mu_col', [128, 28])
        mu = self.sb('mu_sb', [128, 28])
        omu = self.sb('omu_sb', [128, 28])
        C.dma('sp', mu[:], self.inp['mu_col'][:, :], writes=['mu'])
        C.op('dve', lambda e: e.tensor_scalar(out=omu[:], in0=mu[:], scalar1=-1.0, scalar2=1.0, op0=ALU.mult, op1=ALU.add), reads=['mu'], writes=['omu'])
        rawp = Pool(self, 'rs_raw', [128, S0], F32, 2)
        outp = Pool(self, 'rs_out', [128, S0], F32, 2)
        for g in range(4):
            for j in range(7):
                rows = min(128, 872 - j * 128)
                r0 = g * 872 + j * 128
                ci = g * 7 + j
                raw, rk = rawp.get()
                for q4 in range(0, S0, 1088):
                    C.dma('sp', raw[0:rows, q4:q4 + 1088], pT[GQ + r0:GQ + r0 + rows, q4:q4 + 1088], writes=[rk])
                o, ok = outp.get()
                C.op('act', lambda e: e.activation(out=o[0:rows, :], in_=raw[0:rows, :], func=AF.Copy, scale=omu[0:rows, ci:ci + 1]), reads=[rk, 'omu'], writes=[ok])
                m = mu[0:rows, ci:ci + 1]
                rv = raw[0:rows, 0:T].rearrange("p (r c) -> p r c", c=64)
                ov = o[0:rows, 0:T].rearrange("p (r c) -> p r c", c=64)
                if g == 0:
                    src, dst = rv[:, :, 0:63], ov[:, :, 1:64]
                elif g == 1:
                    src, dst = rv[:, :, 1:64], ov[:, :, 0:63]
                elif g == 2:
                    src, dst = raw[0:rows, 0:T - 64], o[0:rows, 64:T]
                else:
                    src, dst = raw[0:rows, 64:T], o[0:rows, 0:T - 64]
                C.op('dve', lambda e: e.scalar_tensor_tensor(out=dst, in0=src, scalar=m, in1=dst, op0=ALU.mult, op1=ALU.add), reads=[rk, ok, 'mu'], writes=[ok])
                if g in (0, 2):
                    src, dst = raw[0:rows, T:S0 - 1], o[0:rows, T + 1:S0]
                else:
                    src, dst = raw[0:rows, T + 1:S0], o[0:rows, T:S0 - 1]
                C.op('dve', lambda e: e.scalar_tensor_tensor(out=dst, in0=src, scalar=m, in1=dst, op0=ALU.mult, op1=ALU.add), reads=[rk, ok, 'mu'], writes=[ok])
                for q4 in range(0, S0, 1088):
                    C.dma('sp', rwT[r0:r0 + rows, q4:q4 + 1088], o[0:rows, q4:q4 + 1088], reads=[ok], writes=['rwT'])

    def rw_rows(self, i0, hd=None, n16=16):
        return [(g * 872 + i0, n16) for g in range(4)]

    def phase_rwkv_scan(self):
        C, nc = self.C, self.nc
        rwT = self.scr['rwT']
        attnT = self.scr['attnT']
        yfT = self.dscr('yfT', [1024, NX])
        self.din('rw_cols', [64, 16 * 5])
        self.din('lora_w', [65, 4 * 1024])
        self.din('wg2', [160, 1024])
        self.din('scan_masks', [64, 6 * 64])
        self.din('chunk_rst', [64, 512])
        rwc = self.sb('rwc', [64, 80])
        oka = self.sb('oka', [64, 16])
        lw_sb = self.sb('lora_sb', [65, 4096])
        wg_a = self.sb('wg_a', [128, 1024])
        wg_b = self.sb('wg_b', [32, 1024])
        msk = self.sb('scan_msk', [64, 384])
        rst = self.sb('chunk_rst_sb', [64, 512])
        C.dma('sp', rwc[:], self.inp['rw_cols'][:, :], writes=['rwc'])
        for q in range(4):
            C.dma('sp', lw_sb[:, q * 1024:(q + 1) * 1024], self.inp['lora_w'][:, q * 1024:(q + 1) * 1024], writes=['lora'])
        C.dma('sp', wg_a[:], self.inp['wg2'][0:128, :], writes=['wg'])
        C.dma('sp', wg_b[:], self.inp['wg2'][128:160, :], writes=['wg'])
        C.dma('sp', msk[:], self.inp['scan_masks'][:, :], writes=['msk'])
        C.dma('sp', rst[:], self.inp['chunk_rst'][:, :], writes=['rst'])
        rwcv = rwc[:].rearrange("p (h q) -> p h q", q=5)
        C.op('dve', lambda e: e.tensor_scalar(out=oka[:], in0=rwcv[:, :, 1], scalar1=-1.0, scalar2=1.0, op0=ALU.mult, op1=ALU.add), reads=['rwc'], writes=['oka'])
        ones64 = self.ones[0:64, 0:64]
        id64 = self.ident[0:64, 0:64]
        Z = [self.sb(f'Z{h}', [64, 64]) for h in range(16)]
        Zn = [0] * 16
        inp = Pool(self, 'rk_in', [64, 512], F32, 12)
        lop = Pool(self, 'rk_lo', [65, 512], F32, 4)
        tp = Pool(self, 'rk_t', [64, 512], F32, 16)
        arp = Pool(self, 'rk_ar', [64, 8 * 128], F32, 4)
        bkp = Pool(self, 'rk_bk', [64, 8 * 128], F32, 4)
        bhp = Pool(self, 'rk_bh', [64, 8 * 128], F32, 4)
        wcp = Pool(self, 'rk_wc', [64, 8], F32, 8)
        GRP = 4
        smr = [Pool(self, f'rk_sm{i}_', [64, 128], F32, 7) for i in range(GRP)]
        fixb = []
        for i in range(GRP):
            fixb.append({'vbk': (self.sb(f'rk_vbk{i}', [64, 192]), f'rk_vbk{i}'), 'NB': (self.sb(f'rk_NB{i}', [64, 128]), f'rk_NB{i}'),
                         'NK': (self.sb(f'rk_NK{i}', [64, 128]), f'rk_NK{i}'), 'A': (self.sb(f'rk_A{i}', [64, 64]), f'rk_A{i}')})
        tfp = Pool(self, 'rk_tf', [64, 512], F32, 6)
        ysbp = Pool(self, 'rk_ys', [64, 512], F32, 6)
        xgp = Pool(self, 'rk_xg', [128, 512], F32, 2)
        xg2p = Pool(self, 'rk_xg2', [32, 512], F32, 2)
        outp = Pool(self, 'rk_o', [64, 512], BF16, 2)
        C.nrot = 1
        C.psn = 0
        C.hbanks = [1, 2, 3, 4, 5, 6, 7]

        def xcols(kind, c0):
            return (NE + c0 * 64) if kind == 'ctx' else c0 * 64

        def scols(kind, c0):
            return (T + c0 * 64) if kind == 'ctx' else c0 * 64

        def load_rows(dst, dk, i0, n, s0, rows16=16, base=0):
            for g in range(4):
                C.dma('sp', dst[base + g * rows16:base + (g + 1) * rows16, 0:n], rwT[g * 872 + i0:g * 872 + i0 + rows16, s0:s0 + n], writes=[dk])

        def lora_in(i0, n, s0, func):
            t, k = lop.get()
            load_rows(t, k, i0, n, s0)
            C.op('act', lambda e: e.activation(out=t[0:64, 0:n], in_=t[0:64, 0:n], func=func), reads=[k], writes=[k])
            C.op('dve', lambda e: e.memset(t[64:65, 0:n], 1.0), writes=[k])
            return t, k

        def ew(eng, fn, reads, n=None):
            t, k = tp.get()
            C.op(eng, lambda e: fn(e, t), reads=reads, writes=[k])
            return t, k

        for d in range(2):
            if d == 0:
                blocks = [('ctx', 0, 4, True)] + [('lat', c, 8, True) for c in (0, 8, 16, 24)] + [('lat', 32, 4, True)]
            else:
                blocks = [('ctx', 0, 4, True)] + [('lat', c, 8, False) for c in (56, 48, 40)] + [('lat', 36, 4, False), ('lat', 32, 4, True)] + \
                         [('lat', c, 8, True) for c in (24, 16, 8, 0)]
            for h in range(16):
                C.op('dve', lambda e, h=h: e.memset(Z[h][:], 0.0), writes=[f'Z{h}'])
            mS = msk[:, d * 192:d * 192 + 128]
            mST = msk[:, d * 192 + 128:d * 192 + 192]
            for (kind, c0, nch, outs) in blocks:
                n = nch * 64
                s0 = scols(kind, c0)
                x0 = xcols(kind, c0)
                xw, xwk = lora_in(768 + 16 * d, n, s0, AF.Tanh)
                xa, xak = lora_in(800 + 16 * d, n, s0, AF.Copy)
                if outs and d == 1:
                    xa0, xa0k = lora_in(800, n, s0, AF.Copy)
                    xg, xgk = xgp.get()
                    xg2, xg2k = xg2p.get()
                    for g in range(4):
                        C.dma('sp', xg[g * 32:(g + 1) * 32, 0:n], rwT[g * 872 + 832:g * 872 + 864, s0:s0 + n], writes=[xgk])
                        C.dma('sp', xg2[g * 8:(g + 1) * 8, 0:n], rwT[g * 872 + 864:g * 872 + 872, s0:s0 + n], writes=[xg2k])
                    C.op('act', lambda e: e.activation(out=xg[:, 0:n], in_=xg[:, 0:n], func=AF.Sigmoid), reads=[xgk], writes=[xgk])
                    C.op('act', lambda e: e.activation(out=xg2[:, 0:n], in_=xg2[:, 0:n], func=AF.Sigmoid), reads=[xg2k], writes=[xg2k])
                for g0 in range(0, 16, GRP):
                  HS = {}
                  for h in range(g0, g0 + GRP):
                    hc = slice(h * 64, (h + 1) * 64)
                    kkc, kac, rkc, lgc, lbc = [rwcv[:, h, q:q + 1] for q in range(5)]
                    kt, kk_ = inp.get(); load_rows(kt, kk_, 256 + h * 16, n, s0)
                    vt, vk_ = inp.get(); load_rows(vt, vk_, 512 + h * 16, n, s0)
                    if outs:
                        rt, rk_ = inp.get(); load_rows(rt, rk_, h * 16, n, s0)
                    pz, pzk = C.ps()
                    C.op('pe', lambda e: e.matmul(pz[0:64, 0:n], lhsT=lw_sb[0:65, d * 1024 + h * 64:d * 1024 + (h + 1) * 64], rhs=xw[0:65, 0:n], start=True, stop=True),
                         reads=['lora', xwk], writes=[pzk])
                    sg, sgk = ew('act', lambda e, t: e.activation(out=t[:, 0:n], in_=pz[0:64, 0:n], func=AF.Sigmoid), [pzk])
                    lw, lwk = ew('dve', lambda e, t: e.tensor_scalar(out=t[:, 0:n], in0=sg[:, 0:n], scalar1=-0.6065306597126334, scalar2=None, op0=ALU.mult), [sgk])
                    Pp, Ppk = ew('dve', lambda e, t: e.tensor_tensor_scan(out=t[:, 0:n], data0=rst[:, 0:n], data1=lw[:, 0:n], initial=0.0, op0=ALU.mult, op1=ALU.add), [lwk, 'rst'])
                    Ee, Eek = ew('dve', lambda e, t: e.tensor_tensor(out=t[:, 0:n], in0=Pp[:, 0:n], in1=lw[:, 0:n], op=ALU.subtract), [Ppk, lwk])
                    P3 = Pp[:, 0:n].rearrange("p (c t) -> p c t", t=64)
                    Qq, Qqk = ew('dve', lambda e, t: e.tensor_tensor(out=t[:, 0:n].rearrange("p (c t) -> p c t", t=64), in0=P3[:, :, 63:64].to_broadcast([64, nch, 64]), in1=P3, op=ALU.subtract), [Ppk])
                    if d == 0:
                        Lin, Link, Lex, Lexk, Lh, Lhk = Pp, Ppk, Ee, Eek, Qq, Qqk
                    else:
                        Lin, Link = ew('dve', lambda e, t: e.tensor_tensor(out=t[:, 0:n], in0=Qq[:, 0:n], in1=lw[:, 0:n], op=ALU.add), [Qqk, lwk])
                        Lex, Lexk, Lh, Lhk = Qq, Qqk, Ee, Eek
                    wc, wck = wcp.get()
                    C.op('act', lambda e: e.activation(out=wc[:, 0:nch], in_=P3[:, :, 63], func=AF.Exp), reads=[Ppk], writes=[wck])
                    eLex, eLexk = ew('act', lambda e, t: e.activation(out=t[:, 0:n], in_=Lex[:, 0:n], func=AF.Exp), [Lexk])
                    eNeg, eNegk = ew('act', lambda e, t: e.activation(out=t[:, 0:n], in_=Lin[:, 0:n], func=AF.Exp, scale=-1.0), [Link])
                    eH, eHk = ew('act', lambda e, t: e.activation(out=t[:, 0:n], in_=Lh[:, 0:n], func=AF.Exp), [Lhk])
                    kk, kkk = ew('dve', lambda e, t: e.tensor_scalar(out=t[:, 0:n], in0=kt[:, 0:n], scalar1=kkc, scalar2=None, op0=ALU.mult), [kk_, 'rwc'])
                    sq, sqk = ew('act', lambda e, t: e.activation(out=t[:, 0:n], in_=kk[:, 0:n], func=AF.Square), [kkk])
                    pss, pssk = C.ps()
                    C.op('pe', lambda e: e.matmul(pss[0:64, 0:n], lhsT=ones64, rhs=sq[:, 0:n], start=True, stop=True), reads=[sqk, 'ones'], writes=[pssk])
                    rn, rnk = ew('act', lambda e, t: e.activation(out=t[:, 0:n], in_=pss[0:64, 0:n], func=AF.Sqrt), [pssk])
                    C.op('dve', lambda e: e.tensor_scalar(out=rn[:, 0:n], in0=rn[:, 0:n], scalar1=1e-6, scalar2=None, op0=ALU.max), reads=[rnk], writes=[rnk])
                    C.op('dve', lambda e: e.reciprocal(out=rn[:, 0:n], in_=rn[:, 0:n]), reads=[rnk], writes=[rnk])
                    kkn, kknk = ew('dve', lambda e, t: e.tensor_tensor(out=t[:, 0:n], in0=kk[:, 0:n], in1=rn[:, 0:n], op=ALU.mult), [kkk, rnk])
                    pa, pak = C.ps()
                    C.op('pe', lambda e: e.matmul(pa[0:64, 0:n], lhsT=lw_sb[0:65, (2 + d) * 1024 + h * 64:(2 + d) * 1024 + (h + 1) * 64], rhs=xa[0:65, 0:n], start=True, stop=True),
                         reads=['lora', xak], writes=[pak])
                    ic, ick = ew('act', lambda e, t: e.activation(out=t[:, 0:n], in_=pa[0:64, 0:n], func=AF.Sigmoid), [pak])
                    bb, bbk = ew('dve', lambda e, t: e.tensor_tensor(out=t[:, 0:n], in0=kkn[:, 0:n], in1=ic[:, 0:n], op=ALU.mult), [kknk, ick])
                    tf, tfk = tfp.get()
                    C.op('dve', lambda e: e.tensor_scalar(out=tf[:, 0:n], in0=ic[:, 0:n], scalar1=kac, scalar2=oka[:, h:h + 1], op0=ALU.mult, op1=ALU.add), reads=[ick, 'rwc', 'oka'], writes=[tfk])
                    kd, kdk = ew('dve', lambda e, t: e.tensor_tensor(out=t[:, 0:n], in0=kt[:, 0:n], in1=tf[:, 0:n], op=ALU.mult), [kk_, tfk])
                    ar, ark = arp.get(); arv = ar[:, 0:nch * 128].rearrange("p (c x) -> p c x", x=128)
                    bk, bkk = bkp.get(); bkv = bk[:, 0:nch * 128].rearrange("p (c x) -> p c x", x=128)
                    bh, bhk = bhp.get(); bhv = bh[:, 0:nch * 128].rearrange("p (c x) -> p c x", x=128)
                    v3 = lambda t: t[:, 0:n].rearrange("p (c t) -> p c t", t=64)
                    C.op('dve', lambda e: e.scalar_tensor_tensor(out=arv[:, :, 0:64], in0=v3(kkn), scalar=-1.0, in1=v3(eLex), op0=ALU.mult, op1=ALU.mult), reads=[kknk, eLexk], writes=[ark])
                    if outs:
                        eLin, eLink = ew('act', lambda e, t: e.activation(out=t[:, 0:n], in_=Lin[:, 0:n], func=AF.Exp), [Link])
                        C.op('dve', lambda e: e.tensor_tensor(out=arv[:, :, 64:128], in0=v3(rt), in1=v3(eLin), op=ALU.mult), reads=[rk_, eLink], writes=[ark])
                    C.op('dve', lambda e: e.tensor_tensor(out=bkv[:, :, 0:64], in0=v3(bb), in1=v3(eNeg), op=ALU.mult), reads=[bbk, eNegk], writes=[bkk])
                    C.op('dve', lambda e: e.tensor_tensor(out=bkv[:, :, 64:128], in0=v3(kd), in1=v3(eNeg), op=ALU.mult), reads=[kdk, eNegk], writes=[bkk])
                    C.op('dve', lambda e: e.tensor_tensor(out=bhv[:, :, 0:64], in0=v3(bb), in1=v3(eH), op=ALU.mult), reads=[bbk, eHk], writes=[bhk])
                    C.op('dve', lambda e: e.tensor_tensor(out=bhv[:, :, 64:128], in0=v3(kd), in1=v3(eH), op=ALU.mult), reads=[kdk, eHk], writes=[bhk])
                    ysb_, ysbk_ = ysbp.get()
                    HS[h] = dict(kt=kt, kk_=kk_, vt=vt, vk_=vk_, rt=(rt if outs else None), rk_=(rk_ if outs else None), arv=arv, ark=ark, bkv=bkv, bkk=bkk,
                                 bhv=bhv, bhk=bhk, wc=wc, wck=wck, tf=tf, tfk=tfk, ysb=ysb_, ysbk=ysbk_)
                  MRG = 'p'

                  def chunk_gen(h, c, slot):
                    Hh = HS[h]
                    vt, vk_, arv, ark, bkv, bkk, bhv, bhk, wc, wck = (Hh[k] for k in ('vt', 'vk_', 'arv', 'ark', 'bkv', 'bkk', 'bhv', 'bhk', 'wc', 'wck'))
                    sm = smr[slot]
                    nw = 128 if outs else 64
                    aT = arv[:, c, 0:64]
                    bT = bkv[:, c, 0:64]
                    kT = bkv[:, c, 64:128]
                    zk = f'Z{h}'
                    pt1, pt1k = C.psh()
                    C.op('pe', lambda e: e.transpose(pt1[0:64, 0:64], vt[:, c * 64:(c + 1) * 64], id64), reads=[vk_, 'ident'], writes=[pt1k])
                    C.op('pe', lambda e: e.transpose(pt1[0:64, 64:128], bhv[:, c, 0:64], id64), reads=[bhk, 'ident'], writes=[pt1k])
                    C.op('pe', lambda e: e.transpose(pt1[0:64, 128:192], bhv[:, c, 64:128], id64), reads=[bhk, 'ident'], writes=[pt1k])
                    vbk, vbkk = fixb[slot]['vbk']
                    C.op('act', lambda e: e.activation(out=vbk[:, 0:192], in_=pt1[0:64, 0:192], func=AF.Copy), reads=[pt1k], writes=[vbkk])
                    yield
                    Vt, Bh, Kh = vbk[:, 0:64], vbk[:, 64:128], vbk[:, 128:192]
                    p1, p1k = C.psh()
                    C.op('pe', lambda e: e.matmul(p1[0:64, 0:nw], lhsT=bT, rhs=arv[:, c, 0:nw], start=True, stop=True), reads=[bkk, ark], writes=[p1k])
                    NB, NBk = fixb[slot]['NB']
                    C.op('dve', lambda e: e.tensor_tensor(out=NB[:, 0:nw], in0=p1[0:64, 0:nw], in1=mS[:, 0:nw], op=ALU.mult), reads=[p1k, 'msk'], writes=[NBk])
                    p2, p2k = C.psh()
                    C.op('pe', lambda e: e.matmul(p2[0:64, 0:nw], lhsT=kT, rhs=arv[:, c, 0:nw], start=True, stop=True), reads=[bkk, ark], writes=[p2k])
                    NK, NKk = fixb[slot]['NK']
                    C.op('dve', lambda e: e.tensor_tensor(out=NK[:, 0:nw], in0=p2[0:64, 0:nw], in1=mS[:, 0:nw], op=ALU.mult), reads=[p2k, 'msk'], writes=[NKk])
                    p3, p3k = C.psh()
                    C.op('pe', lambda e: e.matmul(p3[0:64, 0:64], lhsT=aT, rhs=bT, start=True, stop=True), reads=[bkk, ark], writes=[p3k])
                    A, Ak = fixb[slot]['A']
                    C.op('dve', lambda e: e.tensor_tensor(out=A[:, 0:64], in0=p3[0:64, 0:64], in1=mST, op=ALU.mult), reads=[p3k, 'msk'], writes=[Ak])
                    yield
                    px, pxk = C.psh()
                    C.op('pe', lambda e: e.transpose(px[0:64, 0:64], aT, id64), reads=[ark, 'ident'], writes=[pxk])
                    C.op('pe', lambda e: e.matmul(px[0:64, 64:128], lhsT=NK[:, 0:64], rhs=Vt, start=True, stop=True), reads=[NKk, vbkk], writes=[pxk])
                    X, Xk = sm.get()
                    C.op('act', lambda e: e.activation(out=X[:, 0:128], in_=px[0:64, 0:128], func=AF.Copy), reads=[pxk], writes=[Xk])
                    yield
                    Nc, Nck, Ac, Ack = NB[:, 0:64], NBk, A[:, 0:64], Ak
                    for it in range(6):
                        pq, pqk = C.psh()
                        C.op('pe', lambda e: e.matmul(pq[0:64, 0:128], lhsT=Nc, rhs=X[:, 0:128], start=True, stop=True), reads=[Nck, Xk], writes=[pqk])
                        if 'q' in MRG:
                            if it < 5:
                                C.op('pe', lambda e: e.matmul(pq[0:64, 128:192], lhsT=Ac, rhs=Nc, start=True, stop=True), reads=[Ack, Nck], writes=[pqk])
                                C.op('pe', lambda e: e.matmul(pq[0:64, 192:256], lhsT=Nc, rhs=Ac, start=True, stop=True), reads=[Ack, Nck], writes=[pqk])
                            X2, X2k = sm.get()
                            C.op('dve', lambda e: e.tensor_tensor(out=X2[:, 0:128], in0=pq[0:64, 0:128], in1=X[:, 0:128], op=ALU.add), reads=[pqk, Xk], writes=[X2k])
                            if it < 5:
                                NA, NAk = sm.get()
                                C.op('act', lambda e: e.activation(out=NA[:, 0:128], in_=pq[0:64, 128:256], func=AF.Copy), reads=[pqk], writes=[NAk])
                                Nc, Nck = NA[:, 0:64], NAk
                                Ac, Ack = NA[:, 64:128], NAk
                            X, Xk = X2, X2k
                            yield
                        elif 'p' in MRG:
                            if it < 5:
                                pn, pnk = C.psh()
                                C.op('pe', lambda e: e.matmul(pn[0:64, 0:64], lhsT=Ac, rhs=Nc, start=True, stop=True), reads=[Ack, Nck], writes=[pnk])
                                C.op('pe', lambda e: e.matmul(pn[0:64, 64:128], lhsT=Nc, rhs=Ac, start=True, stop=True), reads=[Ack, Nck], writes=[pnk])
                            X2, X2k = sm.get()
                            C.op('dve', lambda e: e.tensor_tensor(out=X2[:, 0:128], in0=pq[0:64, 0:128], in1=X[:, 0:128], op=ALU.add), reads=[pqk, Xk], writes=[X2k])
                            if it < 5:
                                NA, NAk = sm.get()
                                C.op('act', lambda e: e.activation(out=NA[:, 0:128], in_=pn[0:64, 0:128], func=AF.Copy), reads=[pnk], writes=[NAk])
                                Nc, Nck = NA[:, 0:64], NAk
                                Ac, Ack = NA[:, 64:128], NAk
                            X, Xk = X2, X2k
                            yield
                        else:
                            X2, X2k = sm.get()
                            C.op('dve', lambda e: e.tensor_tensor(out=X2[:, 0:128], in0=pq[0:64, 0:128], in1=X[:, 0:128], op=ALU.add), reads=[pqk, Xk], writes=[X2k])
                            yield
                            X, Xk = X2, X2k
                            if it < 5:
                                pn, pnk = C.psh()
                                C.op('pe', lambda e: e.matmul(pn[0:64, 0:64], lhsT=Ac, rhs=Nc, start=True, stop=True), reads=[Ack, Nck], writes=[pnk])
                                C.op('pe', lambda e: e.matmul(pn[0:64, 64:128], lhsT=Nc, rhs=Ac, start=True, stop=True), reads=[Ack, Nck], writes=[pnk])
                                NA, NAk = sm.get()
                                C.op('act', lambda e: e.activation(out=NA[:, 0:128], in_=pn[0:64, 0:128], func=AF.Copy), reads=[pnk], writes=[NAk])
                                yield
                                Nc, Nck = NA[:, 0:64], NAk
                                Ac, Ack = NA[:, 64:128], NAk
                    Ap, U0 = X[:, 0:64], X[:, 64:128]
                    pm, pmk = C.psh()
                    C.op('pe', lambda e: e.matmul(pm[0:64, 0:64], lhsT=Ap, rhs=Bh, start=True, stop=True), reads=[Xk, vbkk], writes=[pmk])
                    C.op('pe', lambda e: e.matmul(pm[0:64, 64:128], lhsT=Bh, rhs=U0, start=True, stop=False), reads=[Xk, vbkk], writes=[pmk])
                    C.op('pe', lambda e: e.matmul(pm[0:64, 64:128], lhsT=Kh, rhs=Vt, start=False, stop=True), reads=[vbkk], writes=[pmk])
                    if outs and 'm' in MRG:
                        C.op('pe', lambda e: e.matmul(pm[0:64, 128:192], lhsT=Ap, rhs=NB[:, 64:128], start=True, stop=True), reads=[Xk, NBk], writes=[pmk])
                    MS, MSk = sm.get()
                    C.op('dve', lambda e: e.scalar_tensor_tensor(out=MS[:, 0:64], in0=id64, scalar=wc[:, c:c + 1], in1=pm[0:64, 0:64], op0=ALU.mult, op1=ALU.add),
                         reads=[pmk, wck, 'ident'], writes=[MSk])
                    C.op('act', lambda e: e.activation(out=MS[:, 64:128], in_=pm[0:64, 64:128], func=AF.Copy), reads=[pmk], writes=[MSk])
                    if outs and 'm' in MRG:
                        Rp, Rpk = sm.get()
                        C.op('dve', lambda e: e.tensor_tensor(out=Rp[:, 0:64], in0=pm[0:64, 128:192], in1=arv[:, c, 64:128], op=ALU.add), reads=[pmk, ark], writes=[Rpk])
                    yield
                    if outs and 'm' not in MRG:
                        pr, prk = C.psh()
                        C.op('pe', lambda e: e.matmul(pr[0:64, 0:64], lhsT=Ap, rhs=NB[:, 64:128], start=True, stop=True), reads=[Xk, NBk], writes=[prk])
                        Rp, Rpk = sm.get()
                        C.op('dve', lambda e: e.tensor_tensor(out=Rp[:, 0:64], in0=pr[0:64, 0:64], in1=arv[:, c, 64:128], op=ALU.add), reads=[prk, ark], writes=[Rpk])
                        yield
                    if 'z' in MRG:
                        pz2, pz2k = C.psh()
                        if outs:
                            yc = pz2[0:64, 0:64]
                            C.op('pe', lambda e: e.matmul(yc, lhsT=Z[h][:], rhs=Rp[:, 0:64], start=True, stop=False), reads=[zk, Rpk], writes=[pz2k])
                            C.op('pe', lambda e: e.matmul(yc, lhsT=U0, rhs=NB[:, 64:128], start=False, stop=False), reads=[Xk, NBk], writes=[pz2k])
                            C.op('pe', lambda e: e.matmul(yc, lhsT=Vt, rhs=NK[:, 64:128], start=False, stop=True), reads=[vbkk, NKk], writes=[pz2k])
                        C.op('pe', lambda e: e.matmul(pz2[0:64, 64:128], lhsT=MS[:, 0:64], rhs=Z[h][:], start=True, stop=True), reads=[MSk, zk], writes=[pz2k])
                        if outs:
                            C.op('act', lambda e: e.activation(out=Hh['ysb'][:, c * 64:(c + 1) * 64], in_=pz2[0:64, 0:64], func=AF.Copy), reads=[pz2k], writes=[Hh['ysbk']])
                        C.op('dve', lambda e: e.tensor_tensor(out=Z[h][:], in0=pz2[0:64, 64:128], in1=MS[:, 64:128], op=ALU.add), reads=[pz2k, MSk], writes=[zk])
                        yield
                    else:
                        if outs:
                            pyc, pyk = C.psh()
                            yc = pyc[0:64, 0:64]
                            C.op('pe', lambda e: e.matmul(yc, lhsT=Z[h][:], rhs=Rp[:, 0:64], start=True, stop=False), reads=[zk, Rpk], writes=[pyk])
                            C.op('pe', lambda e: e.matmul(yc, lhsT=U0, rhs=NB[:, 64:128], start=False, stop=False), reads=[Xk, NBk], writes=[pyk])
                            C.op('pe', lambda e: e.matmul(yc, lhsT=Vt, rhs=NK[:, 64:128], start=False, stop=True), reads=[vbkk, NKk], writes=[pyk])
                            C.op('act', lambda e: e.activation(out=Hh['ysb'][:, c * 64:(c + 1) * 64], in_=yc, func=AF.Copy), reads=[pyk], writes=[Hh['ysbk']])
                            yield
                        pzz, pzzk = C.psh()
                        C.op('pe', lambda e: e.matmul(pzz[0:64, 0:64], lhsT=MS[:, 0:64], rhs=Z[h][:], start=True, stop=True), reads=[MSk, zk], writes=[pzzk])
                        C.op('dve', lambda e: e.tensor_tensor(out=Z[h][:], in0=pzz[0:64, 0:64], in1=MS[:, 64:128], op=ALU.add), reads=[pzzk, MSk], writes=[zk])
                        yield

                  order = range(nch) if d == 0 else range(nch - 1, -1, -1)
                  for c in order:
                      active = [chunk_gen(h, c, h - g0) for h in range(g0, g0 + GRP)]
                      while active:
                          for gnr in list(active):
                              try:
                                  next(gnr)
                              except StopIteration:
                                  active.remove(gnr)
                  for h in range(g0, g0 + GRP):
                    Hh = HS[h]
                    hc = slice(h * 64, (h + 1) * 64)
                    kkc, kac, rkc, lgc, lbc = [rwcv[:, h, q:q + 1] for q in range(5)]
                    kt, kk_, vt, vk_, rt, rk_, tf, tfk = (Hh[k] for k in ('kt', 'kk_', 'vt', 'vk_', 'rt', 'rk_', 'tf', 'tfk'))
                    pyv, pyk = Hh['ysb'], Hh['ysbk']
                    if not outs:
                        continue
                    if d == 0:
                        ysb, ysk = ew('act', lambda e, t: e.activation(out=t[:, 0:n], in_=pyv[:, 0:n], func=AF.Copy), [pyk])
                        C.dma('sp', yfT[hc, x0:x0 + n], ysb[:, 0:n], reads=[ysk], writes=['yfT'])
                        continue
                    yf, yfk = tp.get()
                    C.dma('sp', yf[:, 0:n], yfT[hc, x0:x0 + n], reads=['yfT'], writes=[yfk])
                    ys, ysk = ew('dve', lambda e, t: e.tensor_tensor(out=t[:, 0:n], in0=pyv[:, 0:n], in1=yf[:, 0:n], op=ALU.add), [pyk, yfk])
                    pmn, pmnk = C.ps()
                    C.op('pe', lambda e: e.matmul(pmn[0:64, 0:n], lhsT=ones64, rhs=ys[:, 0:n], start=True, stop=True), reads=[ysk, 'ones'], writes=[pmnk])
                    ycn, ycnk = ew('dve', lambda e, t: e.scalar_tensor_tensor(out=t[:, 0:n], in0=pmn[0:64, 0:n], scalar=-1.0 / 64, in1=ys[:, 0:n], op0=ALU.mult, op1=ALU.add), [pmnk, ysk])
                    sq2, sq2k = ew('act', lambda e, t: e.activation(out=t[:, 0:n], in_=ycn[:, 0:n], func=AF.Square), [ycnk])
                    pvr, pvrk = C.ps()
                    C.op('pe', lambda e: e.matmul(pvr[0:64, 0:n], lhsT=ones64, rhs=sq2[:, 0:n], start=True, stop=True), reads=[sq2k, 'ones'], writes=[pvrk])
                    rsd, rsdk = ew('act', lambda e, t: e.activation(out=t[:, 0:n], in_=pvr[0:64, 0:n], func=AF.Sqrt, bias=64e-5, scale=1.0 / 64), [pvrk])
                    C.op('dve', lambda e: e.reciprocal(out=rsd[:, 0:n], in_=rsd[:, 0:n]), reads=[rsdk], writes=[rsdk])
                    yn, ynk = ew('dve', lambda e, t: e.tensor_tensor(out=t[:, 0:n], in0=ycn[:, 0:n], in1=rsd[:, 0:n], op=ALU.mult), [ycnk, rsdk])
                    o1, o1k = ew('act', lambda e, t: e.activation(out=t[:, 0:n], in_=yn[:, 0:n], func=AF.Identity, bias=lbc, scale=lgc), [ynk, 'rwc'])
                    pa0, pa0k = C.ps()
                    C.op('pe', lambda e: e.matmul(pa0[0:64, 0:n], lhsT=lw_sb[0:65, 2 * 1024 + h * 64:2 * 1024 + (h + 1) * 64], rhs=xa0[0:65, 0:n], start=True, stop=True),
                         reads=['lora', xa0k], writes=[pa0k])
                    ic0, ic0k = ew('act', lambda e, t: e.activation(out=t[:, 0:n], in_=pa0[0:64, 0:n], func=AF.Sigmoid), [pa0k])
                    C.op('dve', lambda e: e.tensor_scalar(out=ic0[:, 0:n], in0=ic0[:, 0:n], scalar1=kac, scalar2=oka[:, h:h + 1], op0=ALU.mult, op1=ALU.add), reads=[ic0k, 'rwc', 'oka'], writes=[ic0k])
                    C.op('dve', lambda e: e.tensor_tensor(out=ic0[:, 0:n], in0=ic0[:, 0:n], in1=tf[:, 0:n], op=ALU.add), reads=[ic0k, tfk], writes=[ic0k])
                    C.op('dve', lambda e: e.tensor_tensor(out=ic0[:, 0:n], in0=ic0[:, 0:n], in1=kt[:, 0:n], op=ALU.mult), reads=[ic0k, kk_], writes=[ic0k])
                    rk2, rk2k = ew('dve', lambda e, t: e.scalar_tensor_tensor(out=t[:, 0:n], in0=rt[:, 0:n], scalar=rkc, in1=ic0[:, 0:n], op0=ALU.mult, op1=ALU.mult), [rk_, ic0k, 'rwc'])
                    pb, pbk = C.ps()
                    C.op('pe', lambda e: e.matmul(pb[0:64, 0:n], lhsT=ones64, rhs=rk2[:, 0:n], start=True, stop=True), reads=[rk2k, 'ones'], writes=[pbk])
                    bon, bonk = ew('dve', lambda e, t: e.tensor_tensor(out=t[:, 0:n], in0=pb[0:64, 0:n], in1=vt[:, 0:n], op=ALU.mult), [pbk, vk_])
                    C.op('dve', lambda e: e.tensor_tensor(out=o1[:, 0:n], in0=o1[:, 0:n], in1=bon[:, 0:n], op=ALU.add), reads=[o1k, bonk], writes=[o1k])
                    pg, pgk = C.ps()
                    C.op('pe', lambda e: e.matmul(pg[0:64, 0:n], lhsT=wg_a[:, hc], rhs=xg[:, 0:n], start=True, stop=False), reads=['wg', xgk], writes=[pgk])
                    C.op('pe', lambda e: e.matmul(pg[0:64, 0:n], lhsT=wg_b[:, hc], rhs=xg2[:, 0:n], start=False, stop=True), reads=['wg', xg2k], writes=[pgk])
                    ob, obk = outp.get()
                    C.op('dve', lambda e: e.tensor_tensor(out=ob[:, 0:n], in0=pg[0:64, 0:n], in1=o1[:, 0:n], op=ALU.mult), reads=[pgk, o1k], writes=[obk])
                    C.dma('sp', attnT[1024 + h * 64:1024 + (h + 1) * 64, x0:x0 + n], ob[:, 0:n], reads=[obk], writes=['attnT'])
        C.nrot = 8

    def phase_mlp(self, L, tiles, attn, xsrc, out_fn, final=False):
        C = self.C
        Wo = self.din(f'w_out{L}', [D, D]).rearrange("(k p) c -> p k c", p=128)
        W1 = self.din(f'mlp_w1_{L}', [D, 4 * D]).rearrange("(k p) c -> p k c", p=128)
        W2 = self.din(f'mlp_w2_{L}', [4 * D, D]).rearrange("(k p) c -> p k c", p=128)
        self.wload_init(256, 3, 3)
        xp = Pool(self, 'm_x', [128, KC * 512], F32, 1)
        ap_ = Pool(self, 'm_a', [128, KC * 512], BF16, 1)
        hp = ap_
        hid = self.sb('m_hid', [128, 64 * 512], BF16)
        sqp = Pool(self, 'm_sq', [128, 512], F32, 2)
        tmpp = Pool(self, 'm_tm', [128, 512], F32, 3)
        ofp = Pool(self, 'm_of', [128, 512], F32, 2) if final else None
        av_src = attn.rearrange("(k p) t -> p k t", p=128)
        xv_src = xsrc.rearrange("(k p) t -> p k t", p=128)
        wl = []
        for _ in tiles:
            wl += [(Wo, c0, 256, KC, 0) for c0 in range(0, D, 256)]
            wl += [(W1, c0, 256, KC, 0) for c0 in range(0, 4 * D, 256)]
            wl += [(W2, c0, 256, 16, k0) for c0 in range(0, D, 256) for k0 in range(0, 64, 16)]
        self.wstream(wl, 2)
        for (a0, n, v, xs0) in tiles:
            xt, xk = xp.get()
            xv = xt[:].rearrange("p (k t) -> p k t", k=KC)[:, :, 0:n]
            at, ak = ap_.get()
            av = at[:].rearrange("p (k t) -> p k t", k=KC)[:, :, 0:n]
            for kq in range(4):
                C.dma('sp', xv[:, kq * 4:(kq + 1) * 4, :], xv_src[:, kq * 4:(kq + 1) * 4, xs0:xs0 + n], reads=['resT'], writes=[xk])
                C.dma('sp', av[:, kq * 4:(kq + 1) * 4, :], av_src[:, kq * 4:(kq + 1) * 4, a0:a0 + n], reads=['attnT'], writes=[ak])
            for c0 in range(0, D, 256):
                wv, wk = self.wnext()
                for m0 in (0, 128):
                    dt = (c0 + m0) // 128
                    pt, pk = C.ps()
                    for kc in range(KC):
                        C.op('pe', lambda e, kc=kc: e.matmul(pt[:, 0:n], lhsT=wv[:, kc, m0:m0 + 128], rhs=av[:, kc, :], start=(kc == 0), stop=(kc == KC - 1)),
                             reads=[wk, ak], writes=[pk], signal=(kc == KC - 1))
                    C.op('dve', lambda e: e.scalar_tensor_tensor(out=xv[:, dt, :], in0=pt[:, 0:n], scalar=self.mcol(L, 2, dt, v), in1=xv[:, dt, :], op0=ALU.mult, op1=ALU.add),
                         reads=[pk, xk, 'mod'], writes=[xk])
            if f'xmid{L}' in self.debug:
                o = self.outs.get(f'dbg_xmid{L}') or self.dout(f'dbg_xmid{L}', [D, NX])
                ov = o.rearrange("(k p) t -> p k t", p=128)
                for kq in range(4):
                    C.dma('sp', ov[:, kq * 4:(kq + 1) * 4, a0:a0 + n], xv[:, kq * 4:(kq + 1) * 4, :], reads=[xk])
            ht, hk = hp.get()
            hv = ht[:].rearrange("p (k t) -> p k t", k=KC)[:, :, 0:n]
            self.norm_mod(xv, xk, n, hv, hk, lambda kc: self.Av[:, L, 1, kc, v:v + 1], lambda kc: self.mcol(L, 3, kc, v), tmpp, sqp)
            hidv = hid[:].rearrange("p (k t) -> p k t", k=64)[:, :, 0:n]
            for c0 in range(0, 4 * D, 256):
                wv, wk = self.wnext()
                for m0 in (0, 128):
                    ht_i = (c0 + m0) // 128
                    pt, pk = C.ps()
                    for kc in range(KC):
                        C.op('pe', lambda e, kc=kc: e.matmul(pt[:, 0:n], lhsT=wv[:, kc, m0:m0 + 128], rhs=hv[:, kc, :], start=(kc == 0), stop=(kc == KC - 1)),
                             reads=[wk, hk], writes=[pk], signal=(kc == KC - 1))
                    tm, tmk = tmpp.get()
                    C.op('act', lambda e: e.activation(out=tm[:, 0:n], in_=pt[:, 0:n], func=AF.Relu), reads=[pk], writes=[tmk])
                    eng = C.pick('m_sq', ['pool', 'dve'])
                    C.op(eng, lambda e: e.tensor_tensor(out=hidv[:, ht_i, :], in0=tm[:, 0:n], in1=tm[:, 0:n], op=ALU.mult), reads=[tmk], writes=['hid'])
            for c0 in range(0, D, 256):
                pts = [C.ps(), C.ps()]
                for k0 in range(0, 64, 16):
                    wv, wk = self.wnext()
                    for mi, m0 in enumerate((0, 128)):
                        pt, pk = pts[mi]
                        for kc in range(16):
                            C.op('pe', lambda e, kc=kc: e.matmul(pt[:, 0:n], lhsT=wv[:, kc, m0:m0 + 128], rhs=hidv[:, k0 + kc, :],
                                                                start=(k0 + kc == 0), stop=(k0 + kc == 63)),
                                 reads=[wk, 'hid'], writes=[pk], signal=(kc == 15))
                for mi, m0 in enumerate((0, 128)):
                    dt = (c0 + m0) // 128
                    pt, pk = pts[mi]
                    C.op('dve', lambda e: e.scalar_tensor_tensor(out=xv[:, dt, :], in0=pt[:, 0:n], scalar=self.mcol(L, 5, dt, v), in1=xv[:, dt, :], op0=ALU.mult, op1=ALU.add),
                         reads=[pk, xk, 'mod'], writes=[xk])
            if not final:
                out_fn(xv, xk, a0, n)
            else:
                pt, pk = C.ps()
                for kc in range(KC):
                    sq, sqk = sqp.get()
                    C.op('act', lambda e, kc=kc: e.activation(out=sq[:, 0:n], in_=xv[:, kc, :], func=AF.Square), reads=[xk], writes=[sqk])
                    C.op('pe', lambda e, kc=kc: e.matmul(pt[:, 0:n], lhsT=self.ones[:], rhs=sq[:, 0:n], start=(kc == 0), stop=(kc == KC - 1)), reads=[sqk, 'ones'], writes=[pk])
                rs, rsk = sqp.get()
                C.op('act', lambda e: e.activation(out=rs[:, 0:n], in_=pt[:, 0:n], func=AF.Sqrt, bias=EPS, scale=1.0 / D), reads=[pk], writes=[rsk])
                C.op('dve', lambda e: e.reciprocal(out=rs[:, 0:n], in_=rs[:, 0:n]), reads=[rsk], writes=[rsk])
                for kc in range(KC):
                    of, ofk = ofp.get()
                    C.op('dve', lambda e, kc=kc: e.scalar_tensor_tensor(out=of[:, 0:n], in0=xv[:, kc, :], scalar=self.nrm[:, 64 + kc:64 + kc + 1], in1=rs[:, 0:n], op0=ALU.mult, op1=ALU.mult),
                         reads=[xk, rsk, 'nrm'], writes=[ofk])
                    out_fn(of, ofk, kc, a0, n)

    def phase_proj1(self):
        C = self.C
        resT = self.scr['resT']
        xsrc = resT.rearrange("(k p) t -> p k t", p=128)
        W = self.din('w_qkv', [D, 3 * D]).rearrange("(k p) c -> p k c", p=128)
        q1T = self.dscr('q1T', [D, NOWN], BF16)
        k1T = self.dscr('k1T', [D, NX], BF16)
        v1 = self.dscr('v1', [NX, D], BF16)
        self.wload_init(256, 3, 4)
        xp = Pool(self, 'q_x', [128, KC * 512], F32, 2)
        hp = Pool(self, 'q_h', [128, KC * 512], BF16, 2)
        sqp = Pool(self, 'q_sq', [128, 512], F32, 3)
        tmpp = Pool(self, 'q_tm', [128, 512], F32, 3)
        evp = Pool(self, 'q_ev', [128, 512], BF16, 4)
        tiles = [(i * 512, 512, 0, True) for i in range(4)] + [(2048, 256, 0, False), (NE, CT, 1, False)]
        wl = []
        for (t0, n, v, own) in tiles:
            wl += [(W, c0, 256, KC, 0) for c0 in range(0 if own else D, 3 * D, 256)]
        self.wstream(wl, 2)
        for (t0, n, v, own) in tiles:
            xt, xk = xp.get()
            xv = xt[:].rearrange("p (k t) -> p k t", k=KC)[:, :, 0:n]
            for kq in range(4):
                C.dma('sp', xv[:, kq * 4:(kq + 1) * 4, :], xsrc[:, kq * 4:(kq + 1) * 4, t0:t0 + n], reads=['resT'], writes=[xk])
            ht, hk = hp.get()
            hv = ht[:].rearrange("p (k t) -> p k t", k=KC)[:, :, 0:n]
            self.norm_mod(xv, xk, n, hv, hk, lambda kc: self.Av[:, 1, 0, kc, v:v + 1], lambda kc: self.mcol(1, 0, kc, v), tmpp, sqp)
            for c0 in range(0 if own else D, 2 * D, 256):
                wv, wk = self.wnext()
                for m0 in (0, 128):
                    pt, pk = C.ps()
                    for kc in range(KC):
                        C.op('pe', lambda e, kc=kc: e.matmul(pt[:, 0:n], lhsT=wv[:, kc, m0:m0 + 128], rhs=hv[:, kc, :], start=(kc == 0), stop=(kc == KC - 1)),
                             reads=[wk, hk], writes=[pk], signal=(kc == KC - 1))
                    ev, evk = evp.get()
                    isq = c0 < D
                    C.op('act', lambda e: e.activation(out=ev[:, 0:n], in_=pt[:, 0:n], func=AF.Copy, scale=(0.125 if isq else 1.0)), reads=[pk], writes=[evk])
                    cc = c0 + m0
                    if isq:
                        C.dma('sp', q1T[cc:cc + 128, t0:t0 + n], ev[:, 0:n], reads=[evk], writes=['q1T'])
                    else:
                        C.dma('sp', k1T[cc - D:cc - D + 128, t0:t0 + n], ev[:, 0:n], reads=[evk], writes=['k1T'])
            for c0 in range(2 * D, 3 * D, 256):
                wv, wk = self.wnext()
                for s0 in range(0, n, 128):
                    pt, pk = C.ps()
                    for kc in range(KC):
                        C.op('pe', lambda e, kc=kc: e.matmul(pt[:, 0:256], lhsT=hv[:, kc, s0:s0 + 128], rhs=wv[:, kc, :], start=(kc == 0), stop=(kc == KC - 1)),
                             reads=[wk, hk], writes=[pk], signal=(kc == KC - 1))
                    ev, evk = evp.get()
                    C.op('act', lambda e: e.activation(out=ev[:, 0:256], in_=pt[:, 0:256], func=AF.Copy), reads=[pk], writes=[evk])
                    C.dma('sp', v1[t0 + s0:t0 + s0 + 128, c0 - 2 * D:c0 - 2 * D + 256], ev[:, 0:256], reads=[evk], writes=['v1'])

    def phase_na(self):
        C = self.C
        q1T, k1T, v1 = self.scr['q1T'], self.scr['k1T'], self.scr['v1']
        a1T = self.dscr('attn1T', [D, NOWN], BF16)
        nab = self.din('na_bias', [32, 128, 15 * 128])
        qp = Pool(self, 'n_q', [128, NOWN], BF16, 2)
        kp = Pool(self, 'n_k', [128, NX], BF16, 2)
        vsp = Pool(self, 'n_vs', [128, 20 * 128], BF16, 2)
        vap = Pool(self, 'n_va', [128, 20 * 2 * 128], BF16, 2)
        bp = Pool(self, 'n_b', [128, 15 * 128], F32, 2)
        sbp = Pool(self, 'n_sb', [128, 128], F32, 6)
        ptp = Pool(self, 'n_pt', [128, 128], BF16, 8)
        osp = Pool(self, 'n_os', [128, 512], F32, 2)
        rsp = Pool(self, 'n_rs', [64, 512], F32, 2)
        onp = Pool(self, 'n_on', [128, 512], BF16, 2)
        C.nrot = 6
        acc_i = 0
        vsrc = v1.rearrange("(c p) x -> p c x", p=128)
        for tp_ in range(16):
            qt, qk = qp.get()
            kt, kk = kp.get()
            C.dma('sp', qt[:], q1T[tp_ * 128:(tp_ + 1) * 128, :], reads=['q1T'], writes=[qk])
            C.dma('sp', kt[:], k1T[tp_ * 128:(tp_ + 1) * 128, :], reads=['k1T'], writes=[kk])
            vs, vsk = vsp.get()
            vsv = vs[:].rearrange("p (c x) -> p c x", c=20)
            for c4 in range(0, 20, 5):
                C.dma('sp', vsv[:, c4:c4 + 5, :], vsrc[:, c4:c4 + 5, tp_ * 128:(tp_ + 1) * 128], reads=['v1'], writes=[vsk])
            va, vak = vap.get()
            vav = va[:].rearrange("p (c h d) -> p c h d", c=20, h=2)
            C.op('dve', lambda e: e.memset(va[:], 1.0), writes=[vak])
            for hh in range(2):
                C.op('dve', lambda e, hh=hh: e.tensor_copy(out=vav[:, :, hh, 0:64], in_=vsv[:, :, hh * 64:(hh + 1) * 64]), reads=[vsk], writes=[vak])
            for hh in range(2):
                h = tp_ * 2 + hh
                base = hh * 64
                bt, bk = bp.get()
                for q5 in range(0, 15, 5):
                    C.dma('sp', bt[:, q5 * 128:(q5 + 5) * 128], nab[h, :, q5 * 128:(q5 + 5) * 128], writes=[bk])
                for qg in range(4):
                    po = C.ps_tiles[6 + acc_i % 2]
                    pok = f'ps{6 + acc_i % 2}'
                    acc_i += 1
                    pend = []

                    def pv(item):
                        pb, pbk, kc, ci, qi, nchk = item
                        C.op('pe', lambda e: e.matmul(po[:, qi * 128:(qi + 1) * 128], lhsT=vav[:, kc, hh, :], rhs=pb[:], start=(ci == 0), stop=(ci == nchk - 1)),
                             reads=[vak, pbk], writes=[pok])
                    for qi in range(4):
                        qb = qg * 4 + qi
                        cls = min(qb, 2)
                        cs = max(qb - 2, 0)
                        chunks = [(cs + j, cls * 5 + j) for j in range(5)] + [(18, None), (19, None)]
                        for ci, (kc, var) in enumerate(chunks):
                            pt, pk = C.ps()
                            C.op('pe', lambda e: e.matmul(pt[:, 0:128], lhsT=kt[base:base + 64, kc * 128:(kc + 1) * 128], rhs=qt[base:base + 64, qb * 128:(qb + 1) * 128], start=True, stop=True),
                                 reads=[kk, qk], writes=[pk])
                            pb, pbk = ptp.get()
                            if var is not None:
                                sb_, sbk = sbp.get()
                                C.op('dve', lambda e: e.tensor_tensor(out=sb_[:], in0=pt[:, 0:128], in1=bt[:, var * 128:(var + 1) * 128], op=ALU.add), reads=[pk, bk], writes=[sbk])
                                C.op('act', lambda e: e.activation(out=pb[:], in_=sb_[:], func=AF.Exp), reads=[sbk], writes=[pbk])
                            else:
                                C.op('act', lambda e: e.activation(out=pb[:], in_=pt[:, 0:128], func=AF.Exp), reads=[pk], writes=[pbk])
                            pend.append((pb, pbk, kc, ci, qi, len(chunks)))
                            if len(pend) > 3:
                                pv(pend.pop(0))
                    while pend:
                        pv(pend.pop(0))
                    osb, osk = osp.get()
                    C.op('dve', lambda e: e.tensor_copy(out=osb[:], in_=po[:, 0:512]), reads=[pok], writes=[osk])
                    rs, rsk = rsp.get()
                    C.dma('sp', rs[:], osb[64:128, :], reads=[osk], writes=[rsk])
                    C.op('dve', lambda e: e.reciprocal(out=rs[:], in_=rs[:]), reads=[rsk], writes=[rsk])
                    on, onk = onp.get()
                    C.op('dve', lambda e: e.tensor_tensor(out=on[0:64, :], in0=osb[0:64, :], in1=rs[:], op=ALU.mult), reads=[osk, rsk], writes=[onk])
                    C.dma('sp', a1T[h * 64:(h + 1) * 64, qg * 512:(qg + 1) * 512], on[0:64, :], reads=[onk], writes=['attn1T'])
        C.nrot = 8

    def _dbg_h(self, hv, hk, t0, n):
        C = self.C
        if 'h0' not in self.outs:
            self.dout('h0', [D, S0], BF16)
        o = self.outs['h0'].rearrange("(k p) t -> p k t", p=128)
        for kq in range(4):
            C.dma('sp', o[:, kq * 4:(kq + 1) * 4, t0:t0 + n], hv[:, kq * 4:(kq + 1) * 4, :], reads=[hk])

    def dbg_copy_scr(self, name, src, rows, cols, dt=F32):
        C = self.C
        o = self.dout('dbg_' + name, [rows, cols], dt)
        for r0 in range(0, rows, 128):
            r1 = min(rows, r0 + 128)
            C.dma('sp', o[r0:r1, :], src[r0:r1, :], reads=[name])


def build(debug=(), upto=99):
    P = Prog(debug)
    C = P.C
    P.load_consts()
    P.phase_mod()
    if 'mod' in P.debug:
        o = P.dout('dbg_mod', [128, 384])
        C.dma('sp', o[:, :], P.mod[:], reads=['mod'])
    if upto >= 1:
        P.begin_phase()
        P.phase_proj0()
        if 'pT' in P.debug:
            C.barrier()
            P.dbg_copy_scr('pT', P.scr['pT'], NCOL0, S0)
            P.dbg_copy_scr('vtok', P.scr['vtok'], S0, 256, BF16)
    if upto >= 2:
        P.begin_phase()
        P.phase_gqa()
    if upto >= 3:
        P.begin_phase()
        P.phase_rwkv_shift()
        if 'rwT' in P.debug:
            C.barrier()
            P.dbg_copy_scr('rwT', P.scr['rwT'], RW, S0)
    if upto >= 4:
        P.begin_phase()
        P.phase_rwkv_scan()
    if upto >= 2 and 'attnT' in P.debug:
        C.barrier()
        P.dbg_copy_scr('attnT', P.scr['attnT'], D if upto >= 4 else 1024, NX, BF16)
    if upto >= 5:
        P.begin_phase()
        resT = P.dscr('resT', [D, NX])
        rv = resT.rearrange("(k p) t -> p k t", p=128)
        tiles0 = [(i * 512, 512, 0, i * 512) for i in range(4)] + [(2048, 256, 0, 2048), (NE, CT, 1, T)]

        def out0(xv, xk, a0, n):
            for kq in range(4):
                C.dma('sp', rv[:, kq * 4:(kq + 1) * 4, a0:a0 + n], xv[:, kq * 4:(kq + 1) * 4, :], reads=[xk], writes=['resT'])
        P.phase_mlp(0, tiles0, P.scr['attnT'], P.inp['xT'], out0)
        if 'resT' in P.debug:
            C.barrier()
            P.dbg_copy_scr('resT', resT, D, NX)
    if upto >= 6:
        P.begin_phase()
        P.phase_proj1()
    if upto >= 7:
        P.begin_phase()
        P.phase_na()
        if 'attn1T' in P.debug:
            C.barrier()
            P.dbg_copy_scr('attn1T', P.scr['attn1T'], D, NOWN, BF16)
    if upto >= 8:
        P.begin_phase()
        outT = P.dout('outT', [D, NOWN])
        tiles1 = [(i * 512, 512, 0, i * 512) for i in range(4)]

        def out1(of, ofk, kc, a0, n):
            C.dma('sp', outT[kc * 128:(kc + 1) * 128, a0:a0 + n], of[:, 0:n], reads=[ofk], writes=['outT'])
        P.phase_mlp(1, tiles1, P.scr['attn1T'], P.scr['resT'], out1, final=True)
    C.finish('sp')
    return P


def pk(v):
    v = np.asarray(v, np.float32)
    return np.ascontiguousarray(v.reshape(-1, 128).T)


def rw_perm(flip):
    grp = [1, 0, 3, 2] if flip else [0, 1, 2, 3]
    ii = np.arange(872)
    if flip:
        ii = np.concatenate([ii[:768], ii[784:800], ii[768:784], ii[816:832], ii[800:816], ii[832:]])
    return np.concatenate([4 * ii + g for g in grp])


def head_perm(flip, n=16):
    grp = [1, 0, 3, 2] if flip else [0, 1, 2, 3]
    return np.concatenate([4 * np.arange(n) + g for g in grp])


def na_bias_tables(rpb, flip):
    out = np.full((32, 128, 15, 128), -30000.0, np.float32)
    kc = np.arange(64)
    qc = np.arange(64)
    for cls, qb in enumerate((0, 1, 4)):
        cs = max(2 * qb - 4, 0)
        for j in range(5):
            var = cls * 5 + j
            for a in range(2):
                for m_ in range(2):
                    kr = cs + 2 * j + a
                    qr = 2 * qb + m_
                    if flip:
                        kro, qro = 63 - kr, 63 - qr
                        kco, qco = 63 - kc, 63 - qc
                    else:
                        kro, qro, kco, qco = kr, qr, kc, qc
                    rs = min(max(qro - 4, 0), 56)
                    if not (rs <= kro < rs + 8):
                        continue
                    cstart = np.clip(qco - 8, 0, 48)
                    valid = (kco[:, None] >= cstart[None, :]) & (kco[:, None] < cstart[None, :] + 16)
                    dcol = kco[:, None] - qco[None, :] + 15
                    drow = kro - qro + 7
                    vals = rpb[:, drow, :][:, np.clip(dcol, 0, 30)]
                    blk = np.where(valid[None], vals, np.float32(-30000.0))
                    out[:, a * 64:(a + 1) * 64, var, m_ * 64:(m_ + 1) * 64] = blk
    return np.ascontiguousarray(out.reshape(32, 128, 15 * 128))


_SHARED = {}


def host_inputs(I, b, half):
    flip = (half == 1)
    m = {}
    x = I['x'][b]
    cx = I['ctx'][b]
    if flip:
        x = x[::-1]
        cx = cx[::-1]
    m['xT'] = np.ascontiguousarray(np.concatenate([x, cx], 0).T)
    cc = np.stack([pk(I['c'][b]), pk(I['c_ctx'])], -1).reshape(128, 32)
    m['ccol'] = np.ascontiguousarray(cc)
    key = ('shared', flip)
    if key in _SHARED:
        m.update(_SHARED[key])
        return m
    sh = {}
    sh['ident'] = np.eye(128, dtype=np.float32)
    ob = np.zeros((128, 128), np.float32)
    ob[:64, :64] = 1
    ob[64:, 64:] = 1
    sh['ones_blk'] = ob
    sh['ada_b'] = np.ascontiguousarray(np.concatenate([pk(I['l0_ada_b']), pk(I['l1_ada_b'])], 1))
    sh['nrm'] = np.ascontiguousarray(np.concatenate([pk(I[k]) for k in ('l0_norm1', 'l0_norm2', 'l1_norm1', 'l1_norm2', 'final_norm')], 1))
    sh['ada_w0'] = I['l0_ada_w']
    sh['ada_w1'] = I['l1_ada_w']
    w_in = I['l0_w_in']
    rwp = rw_perm(flip)
    perm = np.concatenate([np.arange(GQ), GQ + rwp])
    sh['w_in'] = np.ascontiguousarray(w_in[:, perm])
    qn = np.tile(I['l0_q_norm'], 2) * np.float32(64 ** -0.5)
    kn = np.tile(I['l0_k_norm'], 2)
    sh['qkn'] = np.ascontiguousarray(np.stack([qn, kn], 1).astype(np.float32))
    rot = np.zeros((128, 128), np.float32)
    for mm in range(128):
        if mm % 64 < 32:
            rot[mm + 32, mm] = -1.0
        else:
            rot[mm - 32, mm] = 1.0
    sh['rotm'] = rot
    tt = np.arange(T)
    if flip:
        tt = tt[::-1]
    row = (tt // 64).astype(np.float32)
    col = (tt % 64).astype(np.float32)
    inv = (np.float32(10000.0) ** (-np.arange(16, dtype=np.float32) / np.float32(16))).astype(np.float32)
    ang = np.concatenate([row[:, None] * inv, col[:, None] * inv], -1).astype(np.float32)
    cs = np.cos(ang).astype(np.float32)
    sn = np.sin(ang).astype(np.float32)
    idx = np.arange(128) % 32
    sh['rope_cos'] = np.ascontiguousarray(cs[:, idx].T)
    sh['rope_sin'] = np.ascontiguousarray(sn[:, idx].T)
    mu = I['l0_shift_mu'][rwp]
    mc = np.zeros((128, 28), np.float32)
    for g in range(4):
        for j in range(7):
            rows = min(128, 872 - j * 128)
            mc[:rows, g * 7 + j] = mu[g * 872 + j * 128:g * 872 + j * 128 + rows]
    sh['mu_col'] = mc
    hp = head_perm(flip)
    chan = (np.arange(16)[:, None] * 64 + hp[None, :]).reshape(-1)
    rwc = np.zeros((64, 16, 5), np.float32)
    for q, nm in enumerate(('l0_k_k', 'l0_k_a', None, 'l0_lnx_g', 'l0_lnx_b')):
        vec = I['l0_r_k'].reshape(-1) if nm is None else I[nm]
        rwc[:, :, q] = vec[chan].reshape(16, 64).T
    sh['rw_cols'] = np.ascontiguousarray(rwc.reshape(64, 80))
    dn = ('b', 'f') if flip else ('f', 'b')
    lw = np.zeros((65, 4, 1024), np.float32)
    for d in range(2):
        lw[:64, d] = I[f'l0_ww2_{dn[d]}'][hp][:, chan]
        lw[64, d] = I[f'l0_w0_{dn[d]}'][chan]
        lw[:64, 2 + d] = I[f'l0_wa2_{dn[d]}'][hp][:, chan]
        lw[64, 2 + d] = I[f'l0_a0_{dn[d]}'][chan]
    sh['lora_w'] = np.ascontiguousarray(lw.reshape(65, 4096))
    grp = [1, 0, 3, 2] if flip else [0, 1, 2, 3]
    gp = np.concatenate([4 * np.arange(32) + g for g in grp] + [4 * np.arange(32, 40) + g for g in grp])
    sh['wg2'] = np.ascontiguousarray(I['l0_wg2'][gp][:, chan])
    s_ = np.arange(64)
    Ms = (s_[:, None] < s_[None, :]).astype(np.float32)
    Mi = (s_[:, None] <= s_[None, :]).astype(np.float32)
    sh['scan_masks'] = np.ascontiguousarray(np.concatenate([Ms, Mi, Ms.T, Ms.T, Mi.T, Ms], 1))
    rst = np.ones((64, 512), np.float32)
    rst[:, ::64] = 0
    sh['chunk_rst'] = rst
    wo = I['l0_w_out']
    sh['w_out0'] = np.ascontiguousarray(np.concatenate([wo[:1024], wo[1024 + chan]], 0))
    sh['mlp_w1_0'] = I['l0_mlp_w1']
    sh['mlp_w2_0'] = I['l0_mlp_w2']
    sh['w_qkv'] = I['l1_w_qkv']
    sh['na_bias'] = na_bias_tables(I['l1_rpb'], flip)
    sh['w_out1'] = I['l1_w_out']
    sh['mlp_w1_1'] = I['l1_mlp_w1']
    sh['mlp_w2_1'] = I['l1_mlp_w2']
    _SHARED[key] = sh
    m.update(sh)
    return m


_PROG = {}


def kernel(**inputs):
    I = {k: np.asarray(v) for k, v in inputs.items()}
    if 'p' not in _PROG:
        _PROG['p'] = build()
    P = _PROG['p']
    _SHARED.clear()
    in_maps = []
    for b in range(4):
        for half in range(2):
            m = host_inputs(I, b, half)
            in_maps.append({k: v for k, v in m.items() if k in P.inp})
    res = run_bass_kernel_spmd(P.nc, in_maps, core_ids=list(range(8)))
    out = np.empty((4, T, D), np.float32)
    for b in range(4):
        o0 = np.asarray(res.results[2 * b]['outT'])
        o1 = np.asarray(res.results[2 * b + 1]['outT'])
        out[b, :NOWN] = o0.T
        out[b, NOWN:] = o1.T[::-1]
    _SHARED.clear()
    return out
```

```python
import os
import numpy as np
import concourse.bass as bass
import concourse.mybir as mybir
from concourse.bass_utils import run_bass_kernel_spmd
from contextlib import ExitStack

F32 = mybir.dt.float32
BF16 = mybir.dt.bfloat16
F32R = mybir.dt.float32r
AF = mybir.ActivationFunctionType
ALU = mybir.AluOpType

D = 2048
KC = 16
T = 4096
CT = 256
S0 = T + CT
NE = 2304
NX = NE + CT
NOWN = 2048
RW = 3488
GQ = 1536
NCOL0 = GQ + RW
EPS = 1e-6


class Ctx:
    COMPUTE = ('pe', 'act', 'dve', 'pool')

    def __init__(self, nc, n_dma_sems=20):
        self.nc = nc
        self.eng = {'pe': nc.tensor, 'act': nc.scalar, 'dve': nc.vector, 'pool': nc.gpsimd, 'sp': nc.sync}
        self.sem = {e: nc.alloc_semaphore('c_' + e) for e in self.COMPUTE}
        self.cnt = {e: 0 for e in self.COMPUTE}
        self.dq = {}
        for q in ('sp', 'pool', 'act'):
            self.dq[q] = dict(sems=[nc.alloc_semaphore(f'd_{q}{i}') for i in range(n_dma_sems)],
                              val=[0] * n_dma_sems, nxt=0)
        self.seen = {}
        self.lastw = {}
        self.lastr = {}
        self.psn = 0
        self.nrot = 8
        self.ps_tiles = [nc.alloc_psum_tensor(f"ps{i}", [128, 512], F32) for i in range(8)]
        self.rr = {}

    def _semobj(self, key):
        if isinstance(key, str):
            return self.sem[key]
        q, i = key
        return self.dq[q]['sems'][i]

    def _wait(self, eng, tok):
        key, val = tok
        if eng == 'pe' and key == 'pe':
            return
        if self.seen.get((eng, key), 0) >= val:
            return
        self.eng[eng].wait_ge(self._semobj(key), val)
        self.seen[(eng, key)] = val

    def _deps(self, eng, reads, writes):
        for k in reads:
            t = self.lastw.get(k)
            if t is not None:
                self._wait(eng, t)
        for k in writes:
            t = self.lastw.get(k)
            if t is not None:
                self._wait(eng, t)
            for t in self.lastr.get(k, {}).values():
                self._wait(eng, t)

    def _record(self, tok, reads, writes):
        for k in reads:
            self.lastr.setdefault(k, {})[tok[0]] = tok
        for k in writes:
            self.lastw[k] = tok
            self.lastr[k] = {}

    def op(self, eng, fn, reads=(), writes=(), signal=True):
        self._deps(eng, reads, writes)
        ins = fn(self.eng[eng])
        if signal:
            self.cnt[eng] += 1
            ins.then_inc(self.sem[eng], 1)
            tok = (eng, self.cnt[eng])
        else:
            tok = (eng, self.cnt[eng] + 1)
        self._record(tok, reads, writes)
        return ins

    def dma(self, q, out, in_, reads=(), writes=(), **kw):
        d = self.dq[q]
        i = d['nxt']
        d['nxt'] = (i + 1) % len(d['sems'])
        if d['val'][i] > 0:
            self._wait(q, ((q, i), d['val'][i]))
        self._deps(q, reads, writes)
        ins = self.eng[q].dma_start(out=out, in_=in_, **kw)
        d['val'][i] += 16
        ins.then_inc(d['sems'][i], 16)
        tok = ((q, i), d['val'][i])
        self._record(tok, reads, writes)
        return ins

    def barrier(self):
        toks = [(e, self.cnt[e]) for e in self.COMPUTE if self.cnt[e] > 0]
        for q, d in self.dq.items():
            for i, v in enumerate(d['val']):
                if v > 0:
                    toks.append(((q, i), v))
        for e in ('pe', 'act', 'dve', 'pool', 'sp'):
            for t in toks:
                if t[0] == e:
                    continue
                self._wait(e, t)
        self.lastw = {}
        self.lastr = {}

    def finish(self, q='sp'):
        for e in self.COMPUTE:
            if self.cnt[e] > 0:
                self._wait(q, (e, self.cnt[e]))
        for qq, d in self.dq.items():
            for i, v in enumerate(d['val']):
                if v > 0:
                    self._wait(q, ((qq, i), v))

    def ps(self):
        i = self.psn % self.nrot
        self.psn = (i + 1) % self.nrot
        return self.ps_tiles[i], f'ps{i}'

    def psh(self):
        i = getattr(self, '_hn', 0) % len(self.hbanks)
        self._hn = (i + 1) % len(self.hbanks)
        b = self.hbanks[i]
        return self.ps_tiles[b], f'ps{b}'

    def pick(self, name, choices):
        i = self.rr.get(name, 0)
        self.rr[name] = i + 1
        return choices[i % len(choices)]


class Pool:
    def __init__(self, P, name, shape, dtype, n):
        self.t = [P.sb(f"{name}{i}", shape, dtype) for i in range(n)]
        self.k = [f"{name}{i}" for i in range(n)]
        self.i = 0

    def get(self):
        i = self.i
        self.i = (i + 1) % len(self.t)
        return self.t[i], self.k[i]


class Prog:
    def __init__(self, debug=()):
        self.debug = set(debug)
        nc = self.nc = bass.Bass("TRN2", target_bir_lowering=False)
        self.C = Ctx(nc)
        self.inp = {}
        self.outs = {}
        self.scr = {}
        self.gstack = ExitStack()
        self.stack = self.gstack

    def begin_phase(self):
        self.C.barrier()
        if self.stack is not self.gstack:
            self.stack.close()
        self.stack = ExitStack()

    def sub_begin(self):
        self._saved = self.stack
        self.stack = ExitStack()

    def sub_end(self):
        self.C.barrier()
        self.stack.close()
        self.stack = self._saved

    def din(self, name, shape, dt=F32):
        self.inp[name] = self.nc.dram_tensor(name, list(shape), dt, kind="ExternalInput").ap()
        return self.inp[name]

    def dout(self, name, shape, dt=F32):
        self.outs[name] = self.nc.dram_tensor(name, list(shape), dt, kind="ExternalOutput").ap()
        return self.outs[name]

    def dscr(self, name, shape, dt=F32):
        self.scr[name] = self.nc.dram_tensor(name, list(shape), dt, kind="Internal").ap()
        return self.scr[name]

    def sb(self, name, shape, dt=F32):
        self._uid = getattr(self, '_uid', 0) + 1
        return self.stack.enter_context(self.nc.sbuf_tensor(f"{name}_u{self._uid}", list(shape), dt))

    def load_consts(self):
        C = self.C
        self.din('ident', [128, 128])
        self.din('ones_blk', [128, 128])
        self.ident = self.sb('ident_sb', [128, 128])
        self.ones = self.sb('ones_sb', [128, 128])
        self.ones_blk = self.sb('ones_blk_sb', [128, 128])
        C.dma('sp', self.ident[:], self.inp['ident'][:, :], writes=['ident'])
        C.dma('sp', self.ones_blk[:], self.inp['ones_blk'][:, :], writes=['ones_blk'])
        C.op('dve', lambda e: e.memset(self.ones[:], 1.0), writes=['ones'])
        self.din('ccol', [128, 32])
        self.din('ada_b', [128, 192])
        self.din('nrm', [128, 80])
        self.ccol = self.sb('ccol_sb', [128, 32])
        self.ada_b = self.sb('ada_b_sb', [128, 192])
        self.nrm = self.sb('nrm_sb', [128, 80])
        C.dma('sp', self.ccol[:], self.inp['ccol'][:, :], writes=['ccol'])
        C.dma('sp', self.ada_b[:], self.inp['ada_b'][:, :], writes=['ada_b'])
        C.dma('sp', self.nrm[:], self.inp['nrm'][:, :], writes=['nrm'])

    def phase_mod(self):
        C, nc = self.C, self.nc
        self.din('ada_w0', [D, 6 * D])
        self.din('ada_w1', [D, 6 * D])
        self.mod = self.sb('mod', [128, 2 * 96 * 2])
        modv = self.mod[:].rearrange("p (l j v) -> p l j v", l=2, j=96)
        self.Acol = self.sb('Acol', [128, 2 * 2 * 16 * 2])
        Av = self.Acol[:].rearrange("p (l w k v) -> p l w k v", l=2, w=2, k=16)
        self.begin_phase()
        s = self.sb('silu_c', [128, 32])
        C.op('act', lambda e: e.activation(out=s[:], in_=self.ccol[:], func=AF.Silu), reads=['ccol'], writes=['silu_c'])
        NCB = 768
        wp = Pool(self, 'adaw', [128, KC * NCB], F32, 2)
        for L in range(2):
            W = self.inp[f'ada_w{L}'].rearrange("(k p) c -> p k c", p=128)
            for cb in range(6 * D // NCB):
                wt, wk = wp.get()
                wv = wt[:].rearrange("p (k c) -> p k c", k=KC)
                for kq in range(4):
                    C.dma('sp', wv[:, kq * 4:(kq + 1) * 4, :], W[:, kq * 4:(kq + 1) * 4, cb * NCB:(cb + 1) * NCB], writes=[wk])
                pt, pk = C.ps()
                for j in range(NCB // 128):
                    for kc in range(KC):
                        C.op('pe', lambda e, j=j, kc=kc: e.matmul(pt[:, j * 2:j * 2 + 2], lhsT=wv[:, kc, j * 128:(j + 1) * 128],
                                                                 rhs=s[:, kc * 2:kc * 2 + 2], start=(kc == 0), stop=(kc == KC - 1)),
                             reads=[wk, 'silu_c'], writes=[pk], signal=(kc == KC - 1 and j == NCB // 128 - 1))
                for j in range(NCB // 128):
                    jg = cb * (NCB // 128) + j
                    C.op('dve', lambda e, j=j, jg=jg: e.tensor_scalar(out=modv[:, L, jg, :], in0=pt[:, j * 2:j * 2 + 2],
                                                                     scalar1=self.ada_b[:, L * 96 + jg:L * 96 + jg + 1], scalar2=None, op0=ALU.add),
                         reads=[pk, 'ada_b'], writes=['mod'])
            for w, (sci, nidx) in enumerate(((1, 2 * L), (4, 2 * L + 1))):
                for v in range(2):
                    C.op('dve', lambda e, w=w, sci=sci, nidx=nidx, v=v: e.scalar_tensor_tensor(
                        out=Av[:, L, w, :, v], in0=modv[:, L, sci * 16:(sci + 1) * 16, v], scalar=1.0,
                        in1=self.nrm[:, nidx * 16:(nidx + 1) * 16], op0=ALU.add, op1=ALU.mult),
                         reads=['mod', 'nrm'], writes=['Acol'])
        self.modv, self.Av = modv, Av

    def mcol(self, L, s, kc, v):
        return self.modv[:, L, s * 16 + kc, v:v + 1]

    def norm_mod(self, xt, xk, n, h, hk, Afn, shfn, tmpp, sqp):
        C = self.C
        pt, pk = C.ps()
        for kc in range(KC):
            sq, sqk = sqp.get()
            C.op('act', lambda e, kc=kc, sq=sq: e.activation(out=sq[:, 0:n], in_=xt[:, kc, :], func=AF.Square), reads=[xk], writes=[sqk])
            C.op('pe', lambda e, kc=kc, sq=sq: e.matmul(pt[:, 0:n], lhsT=self.ones[:], rhs=sq[:, 0:n], start=(kc == 0), stop=(kc == KC - 1)),
                 reads=[sqk, 'ones'], writes=[pk])
        rs, rsk = sqp.get()
        C.op('act', lambda e: e.activation(out=rs[:, 0:n], in_=pt[:, 0:n], func=AF.Sqrt, bias=EPS, scale=1.0 / D), reads=[pk], writes=[rsk])
        C.op('dve', lambda e: e.reciprocal(out=rs[:, 0:n], in_=rs[:, 0:n]), reads=[rsk], writes=[rsk])
        for kc in range(KC):
            tm, tmk = tmpp.get()
            eng = 'dve'
            C.op(eng, lambda e, kc=kc, tm=tm: e.tensor_tensor(out=tm[:, 0:n], in0=xt[:, kc, :], in1=rs[:, 0:n], op=ALU.mult),
                 reads=[xk, rsk], writes=[tmk])
            if shfn is not None:
                C.op('act', lambda e, kc=kc, tm=tm: e.activation(out=h[:, kc, :], in_=tm[:, 0:n], func=AF.Identity, bias=shfn(kc), scale=Afn(kc)),
                     reads=[tmk, 'mod', 'Acol'], writes=[hk])
            else:
                C.op('act', lambda e, kc=kc, tm=tm: e.activation(out=h[:, kc, :], in_=tm[:, 0:n], func=AF.Copy, scale=Afn(kc)),
                     reads=[tmk, 'nrm'], writes=[hk])

    def wload_init(self, WB=256, nst=2, nwb=3):
        self.WB = WB
        self.wst = Pool(self, 'wst', [128, KC * WB], F32, nst)
        self.wbp = Pool(self, 'wbf', [128, KC * WB], BF16, nwb)

    def wstream(self, blocks, pf=1):
        self._wq = list(blocks)
        self._wi = 0
        self._wissued = []
        self._wpf = pf

    def wnext(self):
        while len(self._wissued) <= self._wi + self._wpf and len(self._wissued) < len(self._wq):
            b = self._wq[len(self._wissued)]
            self._wissued.append(self.wload(*b))
        r = self._wissued[self._wi]
        self._wi += 1
        return r

    def wload(self, Wv, c0, ncols, kchunks=KC, k0=0):
        C = self.C
        st, sk = self.wst.get()
        sv = st[:].rearrange("p (k c) -> p k c", k=KC)[:, 0:kchunks, 0:ncols]
        wb, wk = self.wbp.get()
        wv = wb[:].rearrange("p (k c) -> p k c", k=KC)[:, 0:kchunks, 0:ncols]
        step = max(1, kchunks // 4)
        for kq in range(0, kchunks, step):
            ke = min(kchunks, kq + step)
            C.dma('sp', sv[:, kq:ke, :], Wv[:, k0 + kq:k0 + ke, c0:c0 + ncols], writes=[sk])
        for kq in range(0, kchunks, step):
            ke = min(kchunks, kq + step)
            eng = C.pick('wcast', ['pool', 'dve', 'pool', 'act'])
            if eng == 'act':
                C.op('act', lambda e: e.activation(out=wv[:, kq:ke, :], in_=sv[:, kq:ke, :], func=AF.Copy), reads=[sk], writes=[wk])
            else:
                C.op(eng, lambda e: e.tensor_copy(out=wv[:, kq:ke, :], in_=sv[:, kq:ke, :]), reads=[sk], writes=[wk])
        return wv, wk

    def phase_proj0(self):
        C, nc = self.C, self.nc
        xT = self.din('xT', [D, S0]).rearrange("(k p) t -> p k t", p=128)
        W = self.din('w_in', [D, NCOL0]).rearrange("(k p) c -> p k c", p=128)
        pT = self.dscr('pT', [NCOL0, S0])
        vtok = self.dscr('vtok', [S0, 256], BF16)
        xp = Pool(self, 'p1x', [128, KC * 512], F32, 2)
        hp = Pool(self, 'p1h', [128, KC * 512], BF16, 2)
        sqp = Pool(self, 'p1sq', [128, 512], F32, 3)
        tmpp = Pool(self, 'p1tm', [128, 512], F32, 4)
        WB = 256
        self.wload_init(WB)
        evp = Pool(self, 'p1ev', [128, 512], F32, 4)
        vp = Pool(self, 'p1v', [128, 256], BF16, 3)
        tiles = [(i * 512, 512, 0) for i in range(8)] + [(T, CT, 1)]
        VC0, VC1 = 1280, 1536
        fm_blocks = [(c, min(WB, NCOL0 - c)) for c in list(range(0, VC0, WB)) + list(range(VC1, NCOL0, WB))]
        wl = []
        for _ in tiles:
            wl += [(W, c0, nc_, KC, 0) for (c0, nc_) in fm_blocks] + [(W, VC0, 256, KC, 0)]
        self.wstream(wl, 1)
        for (t0, n, v) in tiles:
            xt, xk = xp.get()
            xv = xt[:].rearrange("p (k t) -> p k t", k=KC)[:, :, 0:n]
            for kq in range(4):
                C.dma('sp', xv[:, kq * 4:(kq + 1) * 4, :], xT[:, kq * 4:(kq + 1) * 4, t0:t0 + n], writes=[xk])
            ht, hk = hp.get()
            hv = ht[:].rearrange("p (k t) -> p k t", k=KC)[:, :, 0:n]
            self.norm_mod(xv, xk, n, hv, hk, lambda kc: self.Av[:, 0, 0, kc, v:v + 1], lambda kc: self.mcol(0, 0, kc, v), tmpp, sqp)
            if 'h0' in self.debug:
                self._dbg_h(hv, hk, t0, n)
            for (c0, nc_) in fm_blocks:
                wv, wk = self.wnext()
                for m0 in range(0, nc_, 128):
                    m = min(128, nc_ - m0)
                    pt, pk = C.ps()
                    for kc in range(KC):
                        C.op('pe', lambda e, kc=kc, m0=m0, m=m: e.matmul(pt[0:m, 0:n], lhsT=wv[:, kc, m0:m0 + m], rhs=hv[:, kc, :],
                                                                        start=(kc == 0), stop=(kc == KC - 1)),
                             reads=[wk, hk], writes=[pk], signal=(kc == KC - 1))
                    ev, evk = evp.get()
                    eng = C.pick('p1ev', ['act', 'dve'])
                    if eng == 'act':
                        C.op('act', lambda e, m=m: e.activation(out=ev[0:m, 0:n], in_=pt[0:m, 0:n], func=AF.Copy), reads=[pk], writes=[evk])
                    else:
                        C.op('dve', lambda e, m=m: e.tensor_copy(out=ev[0:m, 0:n], in_=pt[0:m, 0:n]), reads=[pk], writes=[evk])
                    C.dma('sp', pT[c0 + m0:c0 + m0 + m, t0:t0 + n], ev[0:m, 0:n], reads=[evk], writes=['pT'])
            wv, wk = self.wnext()
            for s0 in range(0, n, 128):
                pt, pk = C.ps()
                for kc in range(KC):
                    C.op('pe', lambda e, kc=kc, s0=s0: e.matmul(pt[:, 0:256], lhsT=hv[:, kc, s0:s0 + 128], rhs=wv[:, kc, :],
                                                                start=(kc == 0), stop=(kc == KC - 1)),
                         reads=[wk, hk], writes=[pk], signal=(kc == KC - 1))
                vt, vk = vp.get()
                C.op('act', lambda e: e.activation(out=vt[:], in_=pt[:, 0:256], func=AF.Copy), reads=[pk], writes=[vk])
                C.dma('sp', vtok[t0 + s0:t0 + s0 + 128, :], vt[:], reads=[vk], writes=['vtok'])


    def headnorm(self, raw, rawk, n, colscalar, tp, sqp):
        C = self.C
        sq, sqk = sqp.get()
        C.op('act', lambda e: e.activation(out=sq[:, 0:n], in_=raw, func=AF.Square), reads=[rawk], writes=[sqk])
        pt, pk = C.ps()
        C.op('pe', lambda e: e.matmul(pt[:, 0:n], lhsT=self.ones_blk[:], rhs=sq[:, 0:n], start=True, stop=True), reads=[sqk, 'ones_blk'], writes=[pk])
        rs, rsk = sqp.get()
        C.op('act', lambda e: e.activation(out=rs[:, 0:n], in_=pt[:, 0:n], func=AF.Sqrt, bias=EPS, scale=1.0 / 64), reads=[pk], writes=[rsk])
        C.op('dve', lambda e: e.reciprocal(out=rs[:, 0:n], in_=rs[:, 0:n]), reads=[rsk], writes=[rsk])
        kn, knk = tp.get()
        C.op('dve', lambda e: e.scalar_tensor_tensor(out=kn[:, 0:n], in0=raw, scalar=colscalar, in1=rs[:, 0:n], op0=ALU.mult, op1=ALU.mult),
             reads=[rawk, rsk, 'qkn'], writes=[knk])
        return kn, knk

    def rope(self, kn, knk, n, cs0, out, outk, tp):
        C = self.C
        pt, pk = C.ps()
        C.op('pe', lambda e: e.matmul(pt[:, 0:n], lhsT=self.rotm[:], rhs=kn[:, 0:n], start=True, stop=True), reads=[knk, 'rotm'], writes=[pk])
        t1, t1k = tp.get()
        C.op('dve', lambda e: e.tensor_tensor(out=t1[:, 0:n], in0=kn[:, 0:n], in1=self.cos[:, cs0:cs0 + n], op=ALU.mult), reads=[knk, 'cos'], writes=[t1k])
        t2, t2k = tp.get()
        C.op('dve', lambda e: e.tensor_tensor(out=t2[:, 0:n], in0=pt[:, 0:n], in1=self.sin[:, cs0:cs0 + n], op=ALU.mult), reads=[pk, 'sin'], writes=[t2k])
        C.op('dve', lambda e: e.tensor_tensor(out=out, in0=t1[:, 0:n], in1=t2[:, 0:n], op=ALU.add), reads=[t1k, t2k], writes=[outk])

    def phase_gqa(self):
        C, nc = self.C, self.nc
        pT = self.scr['pT']
        vtok = self.scr['vtok']
        attnT = self.dscr('attnT', [D, NX], BF16)
        self.din('qkn', [128, 2])
        self.din('rotm', [128, 128])
        self.din('rope_cos', [128, T])
        self.din('rope_sin', [128, T])
        KT = [self.sb(f'KT{i}', [128, S0], BF16) for i in range(2)]
        QT = [self.sb(f'QT{i}', [128, NX], BF16) for i in range(8)]
        Vaug = self.sb('Vaug', [128, 34 * 4 * 128], BF16)
        Vv = Vaug[:].rearrange("p (c h d) -> p c h d", c=34, h=4)
        qkn = self.sb('qkn_sb', [128, 2])
        self.rotm = self.sb('rotm_sb', [128, 128])
        C.dma('sp', qkn[:], self.inp['qkn'][:, :], writes=['qkn'])
        C.dma('sp', self.rotm[:], self.inp['rotm'][:, :], writes=['rotm'])
        self.sub_begin()
        self.cos = self.sb('cos_sb', [128, T])
        self.sin = self.sb('sin_sb', [128, T])
        for q4 in range(4):
            C.dma('sp', self.cos[:, q4 * 1024:(q4 + 1) * 1024], self.inp['rope_cos'][:, q4 * 1024:(q4 + 1) * 1024], writes=['cos'])
            C.dma('sp', self.sin[:, q4 * 1024:(q4 + 1) * 1024], self.inp['rope_sin'][:, q4 * 1024:(q4 + 1) * 1024], writes=['sin'])
        rawp = Pool(self, 'g_raw', [128, S0], F32, 2)
        sqp = Pool(self, 'g_sq', [128, 512], F32, 4)
        tp = Pool(self, 'g_tp', [128, 512], F32, 6)
        vst = self.sb('g_vst', [128, 34 * 256], BF16)
        vsv = vst[:].rearrange("p (c x) -> p c x", c=34)
        vsrc = vtok.rearrange("(c p) x -> p c x", p=128)
        for c4 in range(0, 34, 6):
            c5 = min(34, c4 + 6)
            C.dma('sp', vsv[:, c4:c5, :], vsrc[:, c4:c5, :], writes=['g_vst'])
        C.op('dve', lambda e: e.memset(Vaug[:], 1.0), writes=['Vaug'])
        for h in range(4):
            C.op('dve', lambda e, h=h: e.tensor_copy(out=Vv[:, :, h, 0:64], in_=vsv[:, :, h * 64:(h + 1) * 64]), reads=['g_vst'], writes=['Vaug'])
        ktiles = [(i * 512, 512, True) for i in range(8)] + [(T, CT, False)]
        for kt in range(2):
            raw, rawk = rawp.get()
            for q4 in range(0, S0, 1088):
                C.dma('sp', raw[:, q4:q4 + 1088], pT[1024 + kt * 128:1024 + (kt + 1) * 128, q4:q4 + 1088], writes=[rawk])
            for (t0, n, lat) in ktiles:
                kn, knk = self.headnorm(raw[:, t0:t0 + n], rawk, n, qkn[:, 1:2], tp, sqp)
                if lat:
                    self.rope(kn, knk, n, t0, KT[kt][:, t0:t0 + n], f'KT{kt}', tp)
                else:
                    C.op('act', lambda e: e.activation(out=KT[kt][:, t0:t0 + n], in_=kn[:, 0:n], func=AF.Copy), reads=[knk], writes=[f'KT{kt}'])
        self.qpairs = [(0, 4), (1, 5), (2, 6), (3, 7), (8, 12), (9, 13), (10, 14), (11, 15)]
        qtiles = [(i * 512, i * 512, 512, True) for i in range(4)] + [(2048, 2048, 256, True), (T, NE, CT, False)]
        for j, (a, b) in enumerate(self.qpairs):
            raw, rawk = rawp.get()
            for hh, hq in enumerate((a, b)):
                C.dma('sp', raw[hh * 64:(hh + 1) * 64, 0:NE], pT[hq * 64:(hq + 1) * 64, 0:NE], writes=[rawk])
                C.dma('sp', raw[hh * 64:(hh + 1) * 64, NE:NX], pT[hq * 64:(hq + 1) * 64, T:S0], writes=[rawk])
            for (src0, x0, n, lat) in qtiles:
                kn, knk = self.headnorm(raw[:, x0:x0 + n], rawk, n, qkn[:, 0:1], tp, sqp)
                if lat:
                    self.rope(kn, knk, n, src0, QT[j][:, x0:x0 + n], f'QT{j}', tp)
                else:
                    C.op('act', lambda e: e.activation(out=QT[j][:, x0:x0 + n], in_=kn[:, 0:n], func=AF.Copy), reads=[knk], writes=[f'QT{j}'])
        if 'qk' in self.debug:
            o = self.dout('dbg_KT', [256, S0], BF16)
            for i in range(2):
                C.dma('sp', o[i * 128:(i + 1) * 128, :], KT[i][:], reads=[f'KT{i}'])
            o = self.dout('dbg_QT', [1024, NX], BF16)
            for i in range(8):
                C.dma('sp', o[i * 128:(i + 1) * 128, :], QT[i][:], reads=[f'QT{i}'])
        self.sub_end()
        C.nrot = 6
        ptp = Pool(self, 'g_pt', [128, 512], BF16, 6)
        osp = Pool(self, 'g_os', [128, 512], F32, 2)
        rsp = Pool(self, 'g_rs', [64, 512], F32, 2)
        onp = Pool(self, 'g_on', [128, 512], BF16, 2)
        acc_i = 0
        for hq in range(16):
            kvh = hq // 4
            base = (kvh % 2) * 64
            j = [i for i, pr in enumerate(self.qpairs) if hq in pr][0]
            jobs = [(i * 512, 512, list(range(34))) for i in range(4)] + [(2048, 256, list(range(34))), (NE, CT, [32, 33])]
            for (q0, n, chunks) in jobs:
                po = C.ps_tiles[6 + acc_i % 2]
                pok = f'ps{6 + acc_i % 2}'
                acc_i += 1
                pend = []

                def pv(item):
                    pb, pbk, c, ci = item
                    C.op('pe', lambda e: e.matmul(po[:, 0:n], lhsT=Vv[:, c, kvh, :], rhs=pb[:, 0:n],
                                                  start=(ci == 0), stop=(ci == len(chunks) - 1)),
                         reads=['Vaug', pbk], writes=[pok])
                for ci, c in enumerate(chunks):
                    pt, pk = C.ps()
                    C.op('pe', lambda e, c=c: e.matmul(pt[:, 0:n], lhsT=KT[kvh // 2][base:base + 64, c * 128:(c + 1) * 128],
                                                      rhs=QT[j][base:base + 64, q0:q0 + n], start=True, stop=True),
                         reads=[f'KT{kvh // 2}', f'QT{j}'], writes=[pk])
                    pb, pbk = ptp.get()
                    C.op('act', lambda e: e.activation(out=pb[:, 0:n], in_=pt[:, 0:n], func=AF.Exp), reads=[pk], writes=[pbk])
                    pend.append((pb, pbk, c, ci))
                    if len(pend) > 2:
                        pv(pend.pop(0))
                while pend:
                    pv(pend.pop(0))
                osb, osk = osp.get()
                C.op('dve', lambda e: e.tensor_copy(out=osb[:, 0:n], in_=po[:, 0:n]), reads=[pok], writes=[osk])
                rs, rsk = rsp.get()
                C.dma('sp', rs[:, 0:n], osb[64:128, 0:n], reads=[osk], writes=[rsk])
                C.op('dve', lambda e: e.reciprocal(out=rs[:, 0:n], in_=rs[:, 0:n]), reads=[rsk], writes=[rsk])
                on, onk = onp.get()
                C.op('dve', lambda e: e.tensor_tensor(out=on[0:64, 0:n], in0=osb[0:64, 0:n], in1=rs[:, 0:n], op=ALU.mult), reads=[osk, rsk], writes=[onk])
                C.dma('sp', attnT[hq * 64:(hq + 1) * 64, q0:q0 + n], on[0:64, 0:n], reads=[onk], writes=['attnT'])
        C.nrot = 8

    def phase_rwkv_shift(self):
        C = self.C
        pT = self.scr['pT']
        rwT = self.dscr('rwT', [RW, S0])
        self.din('mu_col', [128, 28])
        mu = self.sb('mu_sb', [128, 28])
        omu = self.sb('omu_sb', [128, 28])
        C.dma('sp', mu[:], self.inp['mu_col'][:, :], writes=['mu'])
        C.op('dve', lambda e: e.tensor_scalar(out=omu[:], in0=mu[:], scalar1=-1.0, scalar2=1.0, op0=ALU.mult, op1=ALU.add), reads=['mu'], writes=['omu'])
        rawp = Pool(self, 'rs_raw', [128, S0], F32, 2)
        outp = Pool(self, 'rs_out', [128, S0], F32, 2)
        for g in range(4):
            for j in range(7):
                rows = min(128, 872 - j * 128)
                r0 = g * 872 + j * 128
                ci = g * 7 + j
                raw, rk = rawp.get()
                for q4 in range(0, S0, 1088):
                    C.dma('sp', raw[0:rows, q4:q4 + 1088], pT[GQ + r0:GQ + r0 + rows, q4:q4 + 1088], writes=[rk])
                o, ok = outp.get()
                C.op('act', lambda e: e.activation(out=o[0:rows, :], in_=raw[0:rows, :], func=AF.Copy, scale=omu[0:rows, ci:ci + 1]), reads=[rk, 'omu'], writes=[ok])
                m = mu[0:rows, ci:ci + 1]
                rv = raw[0:rows, 0:T].rearrange("p (r c) -> p r c", c=64)
                ov = o[0:rows, 0:T].rearrange("p (r c) -> p r c", c=64)
                if g == 0:
                    src, dst = rv[:, :, 0:63], ov[:, :, 1:64]
                elif g == 1:
                    src, dst = rv[:, :, 1:64], ov[:, :, 0:63]
                elif g == 2:
                    src, dst = raw[0:rows, 0:T - 64], o[0:rows, 64:T]
                else:
                    src, dst = raw[0:rows, 64:T], o[0:rows, 0:T - 64]
                C.op('dve', lambda e: e.scalar_tensor_tensor(out=dst, in0=src, scalar=m, in1=dst, op0=ALU.mult, op1=ALU.add), reads=[rk, ok, 'mu'], writes=[ok])
                if g in (0, 2):
                    src, dst = raw[0:rows, T:S0 - 1], o[0:rows, T + 1:S0]
                else:
                    src, dst = raw[0:rows, T + 1:S0], o[0:rows, T:S0 - 1]
                C.op('dve', lambda e: e.scalar_tensor_tensor(out=dst, in0=src, scalar=m, in1=dst, op0=ALU.mult, op1=ALU.add), reads=[rk, ok, 'mu'], writes=[ok])
                for q4 in range(0, S0, 1088):
                    C.dma('sp', rwT[r0:r0 + rows, q4:q4 + 1088], o[0:rows, q4:q4 + 1088], reads=[ok], writes=['rwT'])

    def rw_rows(self, i0, hd=None, n16=16):
        return [(g * 872 + i0, n16) for g in range(4)]

    def phase_rwkv_scan(self):
        C, nc = self.C, self.nc
        rwT = self.scr['rwT']
        attnT = self.scr['attnT']
        yfT = self.dscr('yfT', [1024, NX])
        self.din('rw_cols', [64, 16 * 5])
        self.din('lora_w', [65, 4 * 1024])
        self.din('wg2', [160, 1024])
        self.din('scan_masks', [64, 6 * 64])
        self.din('chunk_rst', [64, 512])
        rwc = self.sb('rwc', [64, 80])
        oka = self.sb('oka', [64, 16])
        lw_sb = self.sb('lora_sb', [65, 4096])
        wg_a = self.sb('wg_a', [128, 1024])
        wg_b = self.sb('wg_b', [32, 1024])
        msk = self.sb('scan_msk', [64, 384])
        rst = self.sb('chunk_rst_sb', [64, 512])
        C.dma('sp', rwc[:], self.inp['rw_cols'][:, :], writes=['rwc'])
        for q in range(4):
            C.dma('sp', lw_sb[:, q * 1024:(q + 1) * 1024], self.inp['lora_w'][:, q * 1024:(q + 1) * 1024], writes=['lora'])
        C.dma('sp', wg_a[:], self.inp['wg2'][0:128, :], writes=['wg'])
        C.dma('sp', wg_b[:], self.inp['wg2'][128:160, :], writes=['wg'])
        C.dma('sp', msk[:], self.inp['scan_masks'][:, :], writes=['msk'])
        C.dma('sp', rst[:], self.inp['chunk_rst'][:, :], writes=['rst'])
        rwcv = rwc[:].rearrange("p (h q) -> p h q", q=5)
        C.op('dve', lambda e: e.tensor_scalar(out=oka[:], in0=rwcv[:, :, 1], scalar1=-1.0, scalar2=1.0, op0=ALU.mult, op1=ALU.add), reads=['rwc'], writes=['oka'])
        ones64 = self.ones[0:64, 0:64]
        id64 = self.ident[0:64, 0:64]
        Z = [self.sb(f'Z{h}', [64, 64], F32R) for h in range(16)]
        Zn = [0] * 16
        inp = Pool(self, 'rk_in', [64, 512], F32, 12)
        lop = Pool(self, 'rk_lo', [65, 512], F32, 4)
        tp = Pool(self, 'rk_t', [64, 512], F32, 16)
        arp = Pool(self, 'rk_ar', [64, 8 * 128], F32R, 4)
        bkp = Pool(self, 'rk_bk', [64, 8 * 128], F32R, 4)
        bhp = Pool(self, 'rk_bh', [64, 8 * 128], F32, 4)
        wcp = Pool(self, 'rk_wc', [64, 8], F32, 8)
        GRP = 4
        smr = [Pool(self, f'rk_sm{i}_', [64, 128], F32R, 7) for i in range(GRP)]
        fixb = []
        for i in range(GRP):
            fixb.append({'vbk': (self.sb(f'rk_vbk{i}', [64, 192], F32R), f'rk_vbk{i}'), 'NB': (self.sb(f'rk_NB{i}', [64, 128], F32R), f'rk_NB{i}'),
                         'NK': (self.sb(f'rk_NK{i}', [64, 128], F32R), f'rk_NK{i}'), 'A': (self.sb(f'rk_A{i}', [64, 64], F32R), f'rk_A{i}')})
        tfp = Pool(self, 'rk_tf', [64, 512], F32, 6)
        ysbp = Pool(self, 'rk_ys', [64, 512], F32, 6)
        xgp = Pool(self, 'rk_xg', [128, 512], F32, 2)
        xg2p = Pool(self, 'rk_xg2', [32, 512], F32, 2)
        outp = Pool(self, 'rk_o', [64, 512], BF16, 2)
        C.nrot = 1
        C.psn = 0
        C.hbanks = [1, 2, 3, 4, 5, 6, 7]

        def xcols(kind, c0):
            return (NE + c0 * 64) if kind == 'ctx' else c0 * 64

        def scols(kind, c0):
            return (T + c0 * 64) if kind == 'ctx' else c0 * 64

        def load_rows(dst, dk, i0, n, s0, rows16=16, base=0):
            for g in range(4):
                C.dma('sp', dst[base + g * rows16:base + (g + 1) * rows16, 0:n], rwT[g * 872 + i0:g * 872 + i0 + rows16, s0:s0 + n], writes=[dk])

        def lora_in(i0, n, s0, func):
            t, k = lop.get()
            load_rows(t, k, i0, n, s0)
            C.op('act', lambda e: e.activation(out=t[0:64, 0:n], in_=t[0:64, 0:n], func=func), reads=[k], writes=[k])
            C.op('dve', lambda e: e.memset(t[64:65, 0:n], 1.0), writes=[k])
            return t, k

        def ew(eng, fn, reads, n=None):
            t, k = tp.get()
            C.op(eng, lambda e: fn(e, t), reads=reads, writes=[k])
            return t, k

        for d in range(2):
            if d == 0:
                blocks = [('ctx', 0, 4, True)] + [('lat', c, 8, True) for c in (0, 8, 16, 24)] + [('lat', 32, 4, True)]
            else:
                blocks = [('ctx', 0, 4, True)] + [('lat', c, 8, False) for c in (56, 48, 40)] + [('lat', 36, 4, False), ('lat', 32, 4, True)] + \
                         [('lat', c, 8, True) for c in (24, 16, 8, 0)]
            for h in range(16):
                C.op('dve', lambda e, h=h: e.tensor_scalar(out=Z[h][:], in0=id64, scalar1=0.0, scalar2=None, op0=ALU.mult), reads=['ident'], writes=[f'Z{h}'])
            mS = msk[:, d * 192:d * 192 + 128]
            mST = msk[:, d * 192 + 128:d * 192 + 192]
            for (kind, c0, nch, outs) in blocks:
                n = nch * 64
                s0 = scols(kind, c0)
                x0 = xcols(kind, c0)
                xw, xwk = lora_in(768 + 16 * d, n, s0, AF.Tanh)
                xa, xak = lora_in(800 + 16 * d, n, s0, AF.Copy)
                if outs and d == 1:
                    xa0, xa0k = lora_in(800, n, s0, AF.Copy)
                    xg, xgk = xgp.get()
                    xg2, xg2k = xg2p.get()
                    for g in range(4):
                        C.dma('sp', xg[g * 32:(g + 1) * 32, 0:n], rwT[g * 872 + 832:g * 872 + 864, s0:s0 + n], writes=[xgk])
                        C.dma('sp', xg2[g * 8:(g + 1) * 8, 0:n], rwT[g * 872 + 864:g * 872 + 872, s0:s0 + n], writes=[xg2k])
                    C.op('act', lambda e: e.activation(out=xg[:, 0:n], in_=xg[:, 0:n], func=AF.Sigmoid), reads=[xgk], writes=[xgk])
                    C.op('act', lambda e: e.activation(out=xg2[:, 0:n], in_=xg2[:, 0:n], func=AF.Sigmoid), reads=[xg2k], writes=[xg2k])
                for g0 in range(0, 16, GRP):
                  HS = {}
                  for h in range(g0, g0 + GRP):
                    hc = slice(h * 64, (h + 1) * 64)
                    kkc, kac, rkc, lgc, lbc = [rwcv[:, h, q:q + 1] for q in range(5)]
                    kt, kk_ = inp.get(); load_rows(kt, kk_, 256 + h * 16, n, s0)
                    vt, vk_ = inp.get(); load_rows(vt, vk_, 512 + h * 16, n, s0)
                    if outs:
                        rt, rk_ = inp.get(); load_rows(rt, rk_, h * 16, n, s0)
                    pz, pzk = C.ps()
                    C.op('pe', lambda e: e.matmul(pz[0:64, 0:n], lhsT=lw_sb[0:65, d * 1024 + h * 64:d * 1024 + (h + 1) * 64], rhs=xw[0:65, 0:n], start=True, stop=True),
                         reads=['lora', xwk], writes=[pzk])
                    sg, sgk = ew('act', lambda e, t: e.activation(out=t[:, 0:n], in_=pz[0:64, 0:n], func=AF.Sigmoid), [pzk])
                    lw, lwk = ew('dve', lambda e, t: e.tensor_scalar(out=t[:, 0:n], in0=sg[:, 0:n], scalar1=-0.6065306597126334, scalar2=None, op0=ALU.mult), [sgk])
                    Pp, Ppk = ew('dve', lambda e, t: e.tensor_tensor_scan(out=t[:, 0:n], data0=rst[:, 0:n], data1=lw[:, 0:n], initial=0.0, op0=ALU.mult, op1=ALU.add), [lwk, 'rst'])
                    Ee, Eek = ew('dve', lambda e, t: e.tensor_tensor(out=t[:, 0:n], in0=Pp[:, 0:n], in1=lw[:, 0:n], op=ALU.subtract), [Ppk, lwk])
                    P3 = Pp[:, 0:n].rearrange("p (c t) -> p c t", t=64)
                    Qq, Qqk = ew('dve', lambda e, t: e.tensor_tensor(out=t[:, 0:n].rearrange("p (c t) -> p c t", t=64), in0=P3[:, :, 63:64].to_broadcast([64, nch, 64]), in1=P3, op=ALU.subtract), [Ppk])
                    if d == 0:
                        Lin, Link, Lex, Lexk, Lh, Lhk = Pp, Ppk, Ee, Eek, Qq, Qqk
                    else:
                        Lin, Link = ew('dve', lambda e, t: e.tensor_tensor(out=t[:, 0:n], in0=Qq[:, 0:n], in1=lw[:, 0:n], op=ALU.add), [Qqk, lwk])
                        Lex, Lexk, Lh, Lhk = Qq, Qqk, Ee, Eek
                    wc, wck = wcp.get()
                    C.op('act', lambda e: e.activation(out=wc[:, 0:nch], in_=P3[:, :, 63], func=AF.Exp), reads=[Ppk], writes=[wck])
                    eLex, eLexk = ew('act', lambda e, t: e.activation(out=t[:, 0:n], in_=Lex[:, 0:n], func=AF.Exp), [Lexk])
                    eNeg, eNegk = ew('act', lambda e, t: e.activation(out=t[:, 0:n], in_=Lin[:, 0:n], func=AF.Exp, scale=-1.0), [Link])
                    eH, eHk = ew('act', lambda e, t: e.activation(out=t[:, 0:n], in_=Lh[:, 0:n], func=AF.Exp), [Lhk])
                    kk, kkk = ew('dve', lambda e, t: e.tensor_scalar(out=t[:, 0:n], in0=kt[:, 0:n], scalar1=kkc, scalar2=None, op0=ALU.mult), [kk_, 'rwc'])
                    sq, sqk = ew('act', lambda e, t: e.activation(out=t[:, 0:n], in_=kk[:, 0:n], func=AF.Square), [kkk])
                    pss, pssk = C.ps()
                    C.op('pe', lambda e: e.matmul(pss[0:64, 0:n], lhsT=ones64, rhs=sq[:, 0:n], start=True, stop=True), reads=[sqk, 'ones'], writes=[pssk])
                    rn, rnk = ew('act', lambda e, t: e.activation(out=t[:, 0:n], in_=pss[0:64, 0:n], func=AF.Sqrt), [pssk])
                    C.op('dve', lambda e: e.tensor_scalar(out=rn[:, 0:n], in0=rn[:, 0:n], scalar1=1e-6, scalar2=None, op0=ALU.max), reads=[rnk], writes=[rnk])
                    C.op('dve', lambda e: e.reciprocal(out=rn[:, 0:n], in_=rn[:, 0:n]), reads=[rnk], writes=[rnk])
                    kkn, kknk = ew('dve', lambda e, t: e.tensor_tensor(out=t[:, 0:n], in0=kk[:, 0:n], in1=rn[:, 0:n], op=ALU.mult), [kkk, rnk])
                    pa, pak = C.ps()
                    C.op('pe', lambda e: e.matmul(pa[0:64, 0:n], lhsT=lw_sb[0:65, (2 + d) * 1024 + h * 64:(2 + d) * 1024 + (h + 1) * 64], rhs=xa[0:65, 0:n], start=True, stop=True),
                         reads=['lora', xak], writes=[pak])
                    ic, ick = ew('act', lambda e, t: e.activation(out=t[:, 0:n], in_=pa[0:64, 0:n], func=AF.Sigmoid), [pak])
                    bb, bbk = ew('dve', lambda e, t: e.tensor_tensor(out=t[:, 0:n], in0=kkn[:, 0:n], in1=ic[:, 0:n], op=ALU.mult), [kknk, ick])
                    tf, tfk = tfp.get()
                    C.op('dve', lambda e: e.tensor_scalar(out=tf[:, 0:n], in0=ic[:, 0:n], scalar1=kac, scalar2=oka[:, h:h + 1], op0=ALU.mult, op1=ALU.add), reads=[ick, 'rwc', 'oka'], writes=[tfk])
                    kd, kdk = ew('dve', lambda e, t: e.tensor_tensor(out=t[:, 0:n], in0=kt[:, 0:n], in1=tf[:, 0:n], op=ALU.mult), [kk_, tfk])
                    ar, ark = arp.get(); arv = ar[:, 0:nch * 128].rearrange("p (c x) -> p c x", x=128)
                    bk, bkk = bkp.get(); bkv = bk[:, 0:nch * 128].rearrange("p (c x) -> p c x", x=128)
                    bh, bhk = bhp.get(); bhv = bh[:, 0:nch * 128].rearrange("p (c x) -> p c x", x=128)
                    v3 = lambda t: t[:, 0:n].rearrange("p (c t) -> p c t", t=64)
                    C.op('dve', lambda e: e.scalar_tensor_tensor(out=arv[:, :, 0:64], in0=v3(kkn), scalar=-1.0, in1=v3(eLex), op0=ALU.mult, op1=ALU.mult), reads=[kknk, eLexk], writes=[ark])
                    if outs:
                        eLin, eLink = ew('act', lambda e, t: e.activation(out=t[:, 0:n], in_=Lin[:, 0:n], func=AF.Exp), [Link])
                        C.op('dve', lambda e: e.tensor_tensor(out=arv[:, :, 64:128], in0=v3(rt), in1=v3(eLin), op=ALU.mult), reads=[rk_, eLink], writes=[ark])
                    C.op('dve', lambda e: e.tensor_tensor(out=bkv[:, :, 0:64], in0=v3(bb), in1=v3(eNeg), op=ALU.mult), reads=[bbk, eNegk], writes=[bkk])
                    C.op('dve', lambda e: e.tensor_tensor(out=bkv[:, :, 64:128], in0=v3(kd), in1=v3(eNeg), op=ALU.mult), reads=[kdk, eNegk], writes=[bkk])
                    C.op('dve', lambda e: e.tensor_tensor(out=bhv[:, :, 0:64], in0=v3(bb), in1=v3(eH), op=ALU.mult), reads=[bbk, eHk], writes=[bhk])
                    C.op('dve', lambda e: e.tensor_tensor(out=bhv[:, :, 64:128], in0=v3(kd), in1=v3(eH), op=ALU.mult), reads=[kdk, eHk], writes=[bhk])
                    ysb_, ysbk_ = ysbp.get()
                    HS[h] = dict(kt=kt, kk_=kk_, vt=vt, vk_=vk_, rt=(rt if outs else None), rk_=(rk_ if outs else None), arv=arv, ark=ark, bkv=bkv, bkk=bkk,
                                 bhv=bhv, bhk=bhk, wc=wc, wck=wck, tf=tf, tfk=tfk, ysb=ysb_, ysbk=ysbk_)
                  def chunk_gen(h, c, slot):
                    Hh = HS[h]
                    vt, vk_, arv, ark, bkv, bkk, bhv, bhk, wc, wck = (Hh[k] for k in ('vt', 'vk_', 'arv', 'ark', 'bkv', 'bkk', 'bhv', 'bhk', 'wc', 'wck'))
                    sm = smr[slot]
                    nw = 128 if outs else 64
                    aT = arv[:, c, 0:64]
                    bT = bkv[:, c, 0:64]
                    kT = bkv[:, c, 64:128]
                    pt1, pt1k = C.psh()
                    C.op('pe', lambda e: e.transpose(pt1[0:64, 0:64], vt[:, c * 64:(c + 1) * 64], id64), reads=[vk_, 'ident'], writes=[pt1k])
                    C.op('pe', lambda e: e.transpose(pt1[0:64, 64:128], bhv[:, c, 0:64], id64), reads=[bhk, 'ident'], writes=[pt1k])
                    C.op('pe', lambda e: e.transpose(pt1[0:64, 128:192], bhv[:, c, 64:128], id64), reads=[bhk, 'ident'], writes=[pt1k])
                    vbk, vbkk = fixb[slot]['vbk']
                    C.op('act', lambda e: e.activation(out=vbk[:, 0:192], in_=pt1[0:64, 0:192], func=AF.Copy), reads=[pt1k], writes=[vbkk])
                    yield
                    Vt, Bh, Kh = vbk[:, 0:64], vbk[:, 64:128], vbk[:, 128:192]
                    p1, p1k = C.psh()
                    C.op('pe', lambda e: e.matmul(p1[0:64, 0:nw], lhsT=bT, rhs=arv[:, c, 0:nw], start=True, stop=True), reads=[bkk, ark], writes=[p1k])
                    NB, NBk = fixb[slot]['NB']
                    C.op('dve', lambda e: e.tensor_tensor(out=NB[:, 0:nw], in0=p1[0:64, 0:nw], in1=mS[:, 0:nw], op=ALU.mult), reads=[p1k, 'msk'], writes=[NBk])
                    p2, p2k = C.psh()
                    C.op('pe', lambda e: e.matmul(p2[0:64, 0:nw], lhsT=kT, rhs=arv[:, c, 0:nw], start=True, stop=True), reads=[bkk, ark], writes=[p2k])
                    NK, NKk = fixb[slot]['NK']
                    C.op('dve', lambda e: e.tensor_tensor(out=NK[:, 0:nw], in0=p2[0:64, 0:nw], in1=mS[:, 0:nw], op=ALU.mult), reads=[p2k, 'msk'], writes=[NKk])
                    p3, p3k = C.psh()
                    C.op('pe', lambda e: e.matmul(p3[0:64, 0:64], lhsT=aT, rhs=bT, start=True, stop=True), reads=[bkk, ark], writes=[p3k])
                    A, Ak = fixb[slot]['A']
                    C.op('dve', lambda e: e.tensor_tensor(out=A[:, 0:64], in0=p3[0:64, 0:64], in1=mST, op=ALU.mult), reads=[p3k, 'msk'], writes=[Ak])
                    yield
                    px, pxk = C.psh()
                    C.op('pe', lambda e: e.transpose(px[0:64, 0:64], aT.bitcast(F32), id64), reads=[ark, 'ident'], writes=[pxk])
                    C.op('pe', lambda e: e.matmul(px[0:64, 64:128], lhsT=NK[:, 0:64], rhs=Vt, start=True, stop=True), reads=[NKk, vbkk], writes=[pxk])
                    X, Xk = sm.get()
                    C.op('act', lambda e: e.activation(out=X[:, 0:128], in_=px[0:64, 0:128], func=AF.Copy), reads=[pxk], writes=[Xk])
                    yield
                    Nc, Nck, Ac, Ack = NB, NBk, A, Ak
                    for it in range(6):
                        pq, pqk = C.psh()
                        C.op('pe', lambda e: e.matmul(pq[0:64, 0:128], lhsT=Nc[:, 0:64], rhs=X[:, 0:128], start=True, stop=True), reads=[Nck, Xk], writes=[pqk])
                        X2, X2k = sm.get()
                        C.op('dve', lambda e: e.tensor_tensor(out=X2[:, 0:128], in0=pq[0:64, 0:128], in1=X[:, 0:128].bitcast(F32), op=ALU.add), reads=[pqk, Xk], writes=[X2k])
                        yield
                        X, Xk = X2, X2k
                        if it < 5:
                            pn, pnk = C.psh()
                            C.op('pe', lambda e: e.matmul(pn[0:64, 0:64], lhsT=Ac[:, 0:64], rhs=Nc[:, 0:64], start=True, stop=True), reads=[Ack, Nck], writes=[pnk])
                            C.op('pe', lambda e: e.matmul(pn[0:64, 64:128], lhsT=Nc[:, 0:64], rhs=Ac[:, 0:64], start=True, stop=True), reads=[Ack, Nck], writes=[pnk])
                            NA, NAk = sm.get()
                            C.op('act', lambda e: e.activation(out=NA[:, 0:128], in_=pn[0:64, 0:128], func=AF.Copy), reads=[pnk], writes=[NAk])
                            yield
                            Nc, Nck = NA[:, 0:64], NAk
                            Ac, Ack = NA[:, 64:128], NAk
                    Ap, U0 = X[:, 0:64], X[:, 64:128]
                    pm, pmk = C.psh()
                    C.op('pe', lambda e: e.matmul(pm[0:64, 0:64], lhsT=Ap, rhs=Bh, start=True, stop=True), reads=[Xk, vbkk], writes=[pmk])
                    C.op('pe', lambda e: e.matmul(pm[0:64, 64:128], lhsT=Bh, rhs=U0, start=True, stop=False), reads=[Xk, vbkk], writes=[pmk])
                    C.op('pe', lambda e: e.matmul(pm[0:64, 64:128], lhsT=Kh, rhs=Vt, start=False, stop=True), reads=[vbkk], writes=[pmk])
                    MS, MSk = sm.get()
                    C.op('dve', lambda e: e.scalar_tensor_tensor(out=MS[:, 0:64], in0=id64, scalar=wc[:, c:c + 1], in1=pm[0:64, 0:64], op0=ALU.mult, op1=ALU.add),
                         reads=[pmk, wck, 'ident'], writes=[MSk])
                    C.op('act', lambda e: e.activation(out=MS[:, 64:128], in_=pm[0:64, 64:128], func=AF.Copy), reads=[pmk], writes=[MSk])
                    yield
                    zk = f'Z{h}'
                    if outs:
                        pr, prk = C.psh()
                        C.op('pe', lambda e: e.matmul(pr[0:64, 0:64], lhsT=Ap, rhs=NB[:, 64:128], start=True, stop=True), reads=[Xk, NBk], writes=[prk])
                        Rp, Rpk = sm.get()
                        C.op('dve', lambda e: e.tensor_tensor(out=Rp[:, 0:64], in0=pr[0:64, 0:64], in1=arv[:, c, 64:128].bitcast(F32), op=ALU.add), reads=[prk, ark], writes=[Rpk])
                        yield
                        pyc, pyk = C.psh()
                        yc = pyc[0:64, 0:64]
                        C.op('pe', lambda e: e.matmul(yc, lhsT=Z[h][:], rhs=Rp[:, 0:64], start=True, stop=False), reads=[zk, Rpk], writes=[pyk])
                        C.op('pe', lambda e: e.matmul(yc, lhsT=U0, rhs=NB[:, 64:128], start=False, stop=False), reads=[Xk, NBk], writes=[pyk])
                        C.op('pe', lambda e: e.matmul(yc, lhsT=Vt, rhs=NK[:, 64:128], start=False, stop=True), reads=[vbkk, NKk], writes=[pyk])
                        C.op('act', lambda e: e.activation(out=Hh['ysb'][:, c * 64:(c + 1) * 64], in_=yc, func=AF.Copy), reads=[pyk], writes=[Hh['ysbk']])
                        yield
                    pzz, pzzk = C.psh()
                    C.op('pe', lambda e: e.matmul(pzz[0:64, 0:64], lhsT=MS[:, 0:64], rhs=Z[h][:], start=True, stop=True), reads=[MSk, zk], writes=[pzzk])
                    C.op('dve', lambda e: e.tensor_tensor(out=Z[h][:], in0=pzz[0:64, 0:64], in1=MS[:, 64:128].bitcast(F32), op=ALU.add), reads=[pzzk, MSk], writes=[zk])
                    yield

                  order = range(nch) if d == 0 else range(nch - 1, -1, -1)
                  for c in order:
                      active = [chunk_gen(h, c, h - g0) for h in range(g0, g0 + GRP)]
                      while active:
                          for gnr in list(active):
                              try:
                                  next(gnr)
                              except StopIteration:
                                  active.remove(gnr)
                  for h in range(g0, g0 + GRP):
                    Hh = HS[h]
                    hc = slice(h * 64, (h + 1) * 64)
                    kkc, kac, rkc, lgc, lbc = [rwcv[:, h, q:q + 1] for q in range(5)]
                    kt, kk_, vt, vk_, rt, rk_, tf, tfk = (Hh[k] for k in ('kt', 'kk_', 'vt', 'vk_', 'rt', 'rk_', 'tf', 'tfk'))
                    pyv, pyk = Hh['ysb'], Hh['ysbk']
                    if not outs:
                        continue
                    if d == 0:
                        ysb, ysk = ew('act', lambda e, t: e.activation(out=t[:, 0:n], in_=pyv[:, 0:n], func=AF.Copy), [pyk])
                        C.dma('sp', yfT[hc, x0:x0 + n], ysb[:, 0:n], reads=[ysk], writes=['yfT'])
                        continue
                    yf, yfk = tp.get()
                    C.dma('sp', yf[:, 0:n], yfT[hc, x0:x0 + n], reads=['yfT'], writes=[yfk])
                    ys, ysk = ew('dve', lambda e, t: e.tensor_tensor(out=t[:, 0:n], in0=pyv[:, 0:n], in1=yf[:, 0:n], op=ALU.add), [pyk, yfk])
                    pmn, pmnk = C.ps()
                    C.op('pe', lambda e: e.matmul(pmn[0:64, 0:n], lhsT=ones64, rhs=ys[:, 0:n], start=True, stop=True), reads=[ysk, 'ones'], writes=[pmnk])
                    ycn, ycnk = ew('dve', lambda e, t: e.scalar_tensor_tensor(out=t[:, 0:n], in0=pmn[0:64, 0:n], scalar=-1.0 / 64, in1=ys[:, 0:n], op0=ALU.mult, op1=ALU.add), [pmnk, ysk])
                    sq2, sq2k = ew('act', lambda e, t: e.activation(out=t[:, 0:n], in_=ycn[:, 0:n], func=AF.Square), [ycnk])
                    pvr, pvrk = C.ps()
                    C.op('pe', lambda e: e.matmul(pvr[0:64, 0:n], lhsT=ones64, rhs=sq2[:, 0:n], start=True, stop=True), reads=[sq2k, 'ones'], writes=[pvrk])
                    rsd, rsdk = ew('act', lambda e, t: e.activation(out=t[:, 0:n], in_=pvr[0:64, 0:n], func=AF.Sqrt, bias=64e-5, scale=1.0 / 64), [pvrk])
                    C.op('dve', lambda e: e.reciprocal(out=rsd[:, 0:n], in_=rsd[:, 0:n]), reads=[rsdk], writes=[rsdk])
                    yn, ynk = ew('dve', lambda e, t: e.tensor_tensor(out=t[:, 0:n], in0=ycn[:, 0:n], in1=rsd[:, 0:n], op=ALU.mult), [ycnk, rsdk])
                    o1, o1k = ew('act', lambda e, t: e.activation(out=t[:, 0:n], in_=yn[:, 0:n], func=AF.Identity, bias=lbc, scale=lgc), [ynk, 'rwc'])
                    pa0, pa0k = C.ps()
                    C.op('pe', lambda e: e.matmul(pa0[0:64, 0:n], lhsT=lw_sb[0:65, 2 * 1024 + h * 64:2 * 1024 + (h + 1) * 64], rhs=xa0[0:65, 0:n], start=True, stop=True),
                         reads=['lora', xa0k], writes=[pa0k])
                    ic0, ic0k = ew('act', lambda e, t: e.activation(out=t[:, 0:n], in_=pa0[0:64, 0:n], func=AF.Sigmoid), [pa0k])
                    C.op('dve', lambda e: e.tensor_scalar(out=ic0[:, 0:n], in0=ic0[:, 0:n], scalar1=kac, scalar2=oka[:, h:h + 1], op0=ALU.mult, op1=ALU.add), reads=[ic0k, 'rwc', 'oka'], writes=[ic0k])
                    C.op('dve', lambda e: e.tensor_tensor(out=ic0[:, 0:n], in0=ic0[:, 0:n], in1=tf[:, 0:n], op=ALU.add), reads=[ic0k, tfk], writes=[ic0k])
                    C.op('dve', lambda e: e.tensor_tensor(out=ic0[:, 0:n], in0=ic0[:, 0:n], in1=kt[:, 0:n], op=ALU.mult), reads=[ic0k, kk_], writes=[ic0k])
                    rk2, rk2k = ew('dve', lambda e, t: e.scalar_tensor_tensor(out=t[:, 0:n], in0=rt[:, 0:n], scalar=rkc, in1=ic0[:, 0:n], op0=ALU.mult, op1=ALU.mult), [rk_, ic0k, 'rwc'])
                    pb, pbk = C.ps()
                    C.op('pe', lambda e: e.matmul(pb[0:64, 0:n], lhsT=ones64, rhs=rk2[:, 0:n], start=True, stop=True), reads=[rk2k, 'ones'], writes=[pbk])
                    bon, bonk = ew('dve', lambda e, t: e.tensor_tensor(out=t[:, 0:n], in0=pb[0:64, 0:n], in1=vt[:, 0:n], op=ALU.mult), [pbk, vk_])
                    C.op('dve', lambda e: e.tensor_tensor(out=o1[:, 0:n], in0=o1[:, 0:n], in1=bon[:, 0:n], op=ALU.add), reads=[o1k, bonk], writes=[o1k])
                    pg, pgk = C.ps()
                    C.op('pe', lambda e: e.matmul(pg[0:64, 0:n], lhsT=wg_a[:, hc], rhs=xg[:, 0:n], start=True, stop=False), reads=['wg', xgk], writes=[pgk])
                    C.op('pe', lambda e: e.matmul(pg[0:64, 0:n], lhsT=wg_b[:, hc], rhs=xg2[:, 0:n], start=False, stop=True), reads=['wg', xg2k], writes=[pgk])
                    ob, obk = outp.get()
                    C.op('dve', lambda e: e.tensor_tensor(out=ob[:, 0:n], in0=pg[0:64, 0:n], in1=o1[:, 0:n], op=ALU.mult), reads=[pgk, o1k], writes=[obk])
                    C.dma('sp', attnT[1024 + h * 64:1024 + (h + 1) * 64, x0:x0 + n], ob[:, 0:n], reads=[obk], writes=['attnT'])
        C.nrot = 8

    def phase_mlp(self, L, tiles, attn, xsrc, out_fn, final=False):
        C = self.C
        Wo = self.din(f'w_out{L}', [D, D]).rearrange("(k p) c -> p k c", p=128)
        W1 = self.din(f'mlp_w1_{L}', [D, 4 * D]).rearrange("(k p) c -> p k c", p=128)
        W2 = self.din(f'mlp_w2_{L}', [4 * D, D]).rearrange("(k p) c -> p k c", p=128)
        self.wload_init(256, 2, 3)
        xp = Pool(self, 'm_x', [128, KC * 512], F32, 1)
        ap_ = Pool(self, 'm_a', [128, KC * 512], BF16, 1)
        hp = Pool(self, 'm_h', [128, KC * 512], BF16, 1)
        hid = self.sb('m_hid', [128, 64 * 512], BF16)
        sqp = Pool(self, 'm_sq', [128, 512], F32, 3)
        tmpp = Pool(self, 'm_tm', [128, 512], F32, 3)
        ofp = Pool(self, 'm_of', [128, 512], F32, 2) if final else None
        av_src = attn.rearrange("(k p) t -> p k t", p=128)
        xv_src = xsrc.rearrange("(k p) t -> p k t", p=128)
        wl = []
        for _ in tiles:
            wl += [(Wo, c0, 256, KC, 0) for c0 in range(0, D, 256)]
            wl += [(W1, c0, 256, KC, 0) for c0 in range(0, 4 * D, 256)]
            wl += [(W2, c0, 256, 16, k0) for c0 in range(0, D, 256) for k0 in range(0, 64, 16)]
        self.wstream(wl, 1)
        for (a0, n, v, xs0) in tiles:
            xt, xk = xp.get()
            xv = xt[:].rearrange("p (k t) -> p k t", k=KC)[:, :, 0:n]
            at, ak = ap_.get()
            av = at[:].rearrange("p (k t) -> p k t", k=KC)[:, :, 0:n]
            for kq in range(4):
                C.dma('sp', xv[:, kq * 4:(kq + 1) * 4, :], xv_src[:, kq * 4:(kq + 1) * 4, xs0:xs0 + n], reads=['resT'], writes=[xk])
                C.dma('sp', av[:, kq * 4:(kq + 1) * 4, :], av_src[:, kq * 4:(kq + 1) * 4, a0:a0 + n], reads=['attnT'], writes=[ak])
            for c0 in range(0, D, 256):
                wv, wk = self.wnext()
                for m0 in (0, 128):
                    dt = (c0 + m0) // 128
                    pt, pk = C.ps()
                    for kc in range(KC):
                        C.op('pe', lambda e, kc=kc: e.matmul(pt[:, 0:n], lhsT=wv[:, kc, m0:m0 + 128], rhs=av[:, kc, :], start=(kc == 0), stop=(kc == KC - 1)),
                             reads=[wk, ak], writes=[pk], signal=(kc == KC - 1))
                    C.op('dve', lambda e: e.scalar_tensor_tensor(out=xv[:, dt, :], in0=pt[:, 0:n], scalar=self.mcol(L, 2, dt, v), in1=xv[:, dt, :], op0=ALU.mult, op1=ALU.add),
                         reads=[pk, xk, 'mod'], writes=[xk])
            if f'xmid{L}' in self.debug:
                o = self.outs.get(f'dbg_xmid{L}') or self.dout(f'dbg_xmid{L}', [D, NX])
                ov = o.rearrange("(k p) t -> p k t", p=128)
                for kq in range(4):
                    C.dma('sp', ov[:, kq * 4:(kq + 1) * 4, a0:a0 + n], xv[:, kq * 4:(kq + 1) * 4, :], reads=[xk])
            ht, hk = hp.get()
            hv = ht[:].rearrange("p (k t) -> p k t", k=KC)[:, :, 0:n]
            self.norm_mod(xv, xk, n, hv, hk, lambda kc: self.Av[:, L, 1, kc, v:v + 1], lambda kc: self.mcol(L, 3, kc, v), tmpp, sqp)
            hidv = hid[:].rearrange("p (k t) -> p k t", k=64)[:, :, 0:n]
            for c0 in range(0, 4 * D, 256):
                wv, wk = self.wnext()
                for m0 in (0, 128):
                    ht_i = (c0 + m0) // 128
                    pt, pk = C.ps()
                    for kc in range(KC):
                        C.op('pe', lambda e, kc=kc: e.matmul(pt[:, 0:n], lhsT=wv[:, kc, m0:m0 + 128], rhs=hv[:, kc, :], start=(kc == 0), stop=(kc == KC - 1)),
                             reads=[wk, hk], writes=[pk], signal=(kc == KC - 1))
                    tm, tmk = tmpp.get()
                    C.op('act', lambda e: e.activation(out=tm[:, 0:n], in_=pt[:, 0:n], func=AF.Relu), reads=[pk], writes=[tmk])
                    eng = C.pick('m_sq', ['pool', 'dve'])
                    C.op(eng, lambda e: e.tensor_tensor(out=hidv[:, ht_i, :], in0=tm[:, 0:n], in1=tm[:, 0:n], op=ALU.mult), reads=[tmk], writes=['hid'])
            for c0 in range(0, D, 256):
                pts = [C.ps(), C.ps()]
                for k0 in range(0, 64, 16):
                    wv, wk = self.wnext()
                    for mi, m0 in enumerate((0, 128)):
                        pt, pk = pts[mi]
                        for kc in range(16):
                            C.op('pe', lambda e, kc=kc: e.matmul(pt[:, 0:n], lhsT=wv[:, kc, m0:m0 + 128], rhs=hidv[:, k0 + kc, :],
                                                                start=(k0 + kc == 0), stop=(k0 + kc == 63)),
                                 reads=[wk, 'hid'], writes=[pk], signal=(kc == 15))
                for mi, m0 in enumerate((0, 128)):
                    dt = (c0 + m0) // 128
                    pt, pk = pts[mi]
                    C.op('dve', lambda e: e.scalar_tensor_tensor(out=xv[:, dt, :], in0=pt[:, 0:n], scalar=self.mcol(L, 5, dt, v), in1=xv[:, dt, :], op0=ALU.mult, op1=ALU.add),
                         reads=[pk, xk, 'mod'], writes=[xk])
            if not final:
                out_fn(xv, xk, a0, n)
            else:
                pt, pk = C.ps()
                for kc in range(KC):
                    sq, sqk = sqp.get()
                    C.op('act', lambda e, kc=kc: e.activation(out=sq[:, 0:n], in_=xv[:, kc, :], func=AF.Square), reads=[xk], writes=[sqk])
                    C.op('pe', lambda e, kc=kc: e.matmul(pt[:, 0:n], lhsT=self.ones[:], rhs=sq[:, 0:n], start=(kc == 0), stop=(kc == KC - 1)), reads=[sqk, 'ones'], writes=[pk])
                rs, rsk = sqp.get()
                C.op('act', lambda e: e.activation(out=rs[:, 0:n], in_=pt[:, 0:n], func=AF.Sqrt, bias=EPS, scale=1.0 / D), reads=[pk], writes=[rsk])
                C.op('dve', lambda e: e.reciprocal(out=rs[:, 0:n], in_=rs[:, 0:n]), reads=[rsk], writes=[rsk])
                for kc in range(KC):
                    of, ofk = ofp.get()
                    C.op('dve', lambda e, kc=kc: e.scalar_tensor_tensor(out=of[:, 0:n], in0=xv[:, kc, :], scalar=self.nrm[:, 64 + kc:64 + kc + 1], in1=rs[:, 0:n], op0=ALU.mult, op1=ALU.mult),
                         reads=[xk, rsk, 'nrm'], writes=[ofk])
                    out_fn(of, ofk, kc, a0, n)

    def phase_proj1(self):
        C = self.C
        resT = self.scr['resT']
        xsrc = resT.rearrange("(k p) t -> p k t", p=128)
        W = self.din('w_qkv', [D, 3 * D]).rearrange("(k p) c -> p k c", p=128)
        q1T = self.dscr('q1T', [D, NOWN], BF16)
        k1T = self.dscr('k1T', [D, NX], BF16)
        v1 = self.dscr('v1', [NX, D], BF16)
        self.wload_init(256, 2, 3)
        xp = Pool(self, 'q_x', [128, KC * 512], F32, 2)
        hp = Pool(self, 'q_h', [128, KC * 512], BF16, 2)
        sqp = Pool(self, 'q_sq', [128, 512], F32, 3)
        tmpp = Pool(self, 'q_tm', [128, 512], F32, 3)
        evp = Pool(self, 'q_ev', [128, 512], BF16, 4)
        tiles = [(i * 512, 512, 0, True) for i in range(4)] + [(2048, 256, 0, False), (NE, CT, 1, False)]
        wl = []
        for (t0, n, v, own) in tiles:
            wl += [(W, c0, 256, KC, 0) for c0 in range(0 if own else D, 3 * D, 256)]
        self.wstream(wl, 1)
        for (t0, n, v, own) in tiles:
            xt, xk = xp.get()
            xv = xt[:].rearrange("p (k t) -> p k t", k=KC)[:, :, 0:n]
            for kq in range(4):
                C.dma('sp', xv[:, kq * 4:(kq + 1) * 4, :], xsrc[:, kq * 4:(kq + 1) * 4, t0:t0 + n], reads=['resT'], writes=[xk])
            ht, hk = hp.get()
            hv = ht[:].rearrange("p (k t) -> p k t", k=KC)[:, :, 0:n]
            self.norm_mod(xv, xk, n, hv, hk, lambda kc: self.Av[:, 1, 0, kc, v:v + 1], lambda kc: self.mcol(1, 0, kc, v), tmpp, sqp)
            for c0 in range(0 if own else D, 2 * D, 256):
                wv, wk = self.wnext()
                for m0 in (0, 128):
                    pt, pk = C.ps()
                    for kc in range(KC):
                        C.op('pe', lambda e, kc=kc: e.matmul(pt[:, 0:n], lhsT=wv[:, kc, m0:m0 + 128], rhs=hv[:, kc, :], start=(kc == 0), stop=(kc == KC - 1)),
                             reads=[wk, hk], writes=[pk], signal=(kc == KC - 1))
                    ev, evk = evp.get()
                    isq = c0 < D
                    C.op('act', lambda e: e.activation(out=ev[:, 0:n], in_=pt[:, 0:n], func=AF.Copy, scale=(0.125 if isq else 1.0)), reads=[pk], writes=[evk])
                    cc = c0 + m0
                    if isq:
                        C.dma('sp', q1T[cc:cc + 128, t0:t0 + n], ev[:, 0:n], reads=[evk], writes=['q1T'])
                    else:
                        C.dma('sp', k1T[cc - D:cc - D + 128, t0:t0 + n], ev[:, 0:n], reads=[evk], writes=['k1T'])
            for c0 in range(2 * D, 3 * D, 256):
                wv, wk = self.wnext()
                for s0 in range(0, n, 128):
                    pt, pk = C.ps()
                    for kc in range(KC):
                        C.op('pe', lambda e, kc=kc: e.matmul(pt[:, 0:256], lhsT=hv[:, kc, s0:s0 + 128], rhs=wv[:, kc, :], start=(kc == 0), stop=(kc == KC - 1)),
                             reads=[wk, hk], writes=[pk], signal=(kc == KC - 1))
                    ev, evk = evp.get()
                    C.op('act', lambda e: e.activation(out=ev[:, 0:256], in_=pt[:, 0:256], func=AF.Copy), reads=[pk], writes=[evk])
                    C.dma('sp', v1[t0 + s0:t0 + s0 + 128, c0 - 2 * D:c0 - 2 * D + 256], ev[:, 0:256], reads=[evk], writes=['v1'])

    def phase_na(self):
        C = self.C
        q1T, k1T, v1 = self.scr['q1T'], self.scr['k1T'], self.scr['v1']
        a1T = self.dscr('attn1T', [D, NOWN], BF16)
        nab = self.din('na_bias', [32, 128, 15 * 128])
        qp = Pool(self, 'n_q', [128, NOWN], BF16, 2)
        kp = Pool(self, 'n_k', [128, NX], BF16, 2)
        vsp = Pool(self, 'n_vs', [128, 20 * 128], BF16, 2)
        vap = Pool(self, 'n_va', [128, 20 * 2 * 128], BF16, 2)
        bp = Pool(self, 'n_b', [128, 15 * 128], F32, 2)
        sbp = Pool(self, 'n_sb', [128, 128], F32, 6)
        ptp = Pool(self, 'n_pt', [128, 128], BF16, 8)
        osp = Pool(self, 'n_os', [128, 512], F32, 2)
        rsp = Pool(self, 'n_rs', [64, 512], F32, 2)
        onp = Pool(self, 'n_on', [128, 512], BF16, 2)
        C.nrot = 6
        acc_i = 0
        vsrc = v1.rearrange("(c p) x -> p c x", p=128)
        for tp_ in range(16):
            qt, qk = qp.get()
            kt, kk = kp.get()
            C.dma('sp', qt[:], q1T[tp_ * 128:(tp_ + 1) * 128, :], reads=['q1T'], writes=[qk])
            C.dma('sp', kt[:], k1T[tp_ * 128:(tp_ + 1) * 128, :], reads=['k1T'], writes=[kk])
            vs, vsk = vsp.get()
            vsv = vs[:].rearrange("p (c x) -> p c x", c=20)
            for c4 in range(0, 20, 5):
                C.dma('sp', vsv[:, c4:c4 + 5, :], vsrc[:, c4:c4 + 5, tp_ * 128:(tp_ + 1) * 128], reads=['v1'], writes=[vsk])
            va, vak = vap.get()
            vav = va[:].rearrange("p (c h d) -> p c h d", c=20, h=2)
            C.op('dve', lambda e: e.memset(va[:], 1.0), writes=[vak])
            for hh in range(2):
                C.op('dve', lambda e, hh=hh: e.tensor_copy(out=vav[:, :, hh, 0:64], in_=vsv[:, :, hh * 64:(hh + 1) * 64]), reads=[vsk], writes=[vak])
            for hh in range(2):
                h = tp_ * 2 + hh
                base = hh * 64
                bt, bk = bp.get()
                for q5 in range(0, 15, 5):
                    C.dma('sp', bt[:, q5 * 128:(q5 + 5) * 128], nab[h, :, q5 * 128:(q5 + 5) * 128], writes=[bk])
                for qg in range(4):
                    po = C.ps_tiles[6 + acc_i % 2]
                    pok = f'ps{6 + acc_i % 2}'
                    acc_i += 1
                    pend = []

                    def pv(item):
                        pb, pbk, kc, ci, qi, nchk = item
                        C.op('pe', lambda e: e.matmul(po[:, qi * 128:(qi + 1) * 128], lhsT=vav[:, kc, hh, :], rhs=pb[:], start=(ci == 0), stop=(ci == nchk - 1)),
                             reads=[vak, pbk], writes=[pok])
                    for qi in range(4):
                        qb = qg * 4 + qi
                        cls = min(qb, 2)
                        cs = max(qb - 2, 0)
                        chunks = [(cs + j, cls * 5 + j) for j in range(5)] + [(18, None), (19, None)]
                        for ci, (kc, var) in enumerate(chunks):
                            pt, pk = C.ps()
                            C.op('pe', lambda e: e.matmul(pt[:, 0:128], lhsT=kt[base:base + 64, kc * 128:(kc + 1) * 128], rhs=qt[base:base + 64, qb * 128:(qb + 1) * 128], start=True, stop=True),
                                 reads=[kk, qk], writes=[pk])
                            pb, pbk = ptp.get()
                            if var is not None:
                                sb_, sbk = sbp.get()
                                C.op('dve', lambda e: e.tensor_tensor(out=sb_[:], in0=pt[:, 0:128], in1=bt[:, var * 128:(var + 1) * 128], op=ALU.add), reads=[pk, bk], writes=[sbk])
                                C.op('act', lambda e: e.activation(out=pb[:], in_=sb_[:], func=AF.Exp), reads=[sbk], writes=[pbk])
                            else:
                                C.op('act', lambda e: e.activation(out=pb[:], in_=pt[:, 0:128], func=AF.Exp), reads=[pk], writes=[pbk])
                            pend.append((pb, pbk, kc, ci, qi, len(chunks)))
                            if len(pend) > 3:
                                pv(pend.pop(0))
                    while pend:
                        pv(pend.pop(0))
                    osb, osk = osp.get()
                    C.op('dve', lambda e: e.tensor_copy(out=osb[:], in_=po[:, 0:512]), reads=[pok], writes=[osk])
                    rs, rsk = rsp.get()
                    C.dma('sp', rs[:], osb[64:128, :], reads=[osk], writes=[rsk])
                    C.op('dve', lambda e: e.reciprocal(out=rs[:], in_=rs[:]), reads=[rsk], writes=[rsk])
                    on, onk = onp.get()
                    C.op('dve', lambda e: e.tensor_tensor(out=on[0:64, :], in0=osb[0:64, :], in1=rs[:], op=ALU.mult), reads=[osk, rsk], writes=[onk])
                    C.dma('sp', a1T[h * 64:(h + 1) * 64, qg * 512:(qg + 1) * 512], on[0:64, :], reads=[onk], writes=['attn1T'])
        C.nrot = 8

    def _dbg_h(self, hv, hk, t0, n):
        C = self.C
        if 'h0' not in self.outs:
            self.dout('h0', [D, S0], BF16)
        o = self.outs['h0'].rearrange("(k p) t -> p k t", p=128)
        for kq in range(4):
            C.dma('sp', o[:, kq * 4:(kq + 1) * 4, t0:t0 + n], hv[:, kq * 4:(kq + 1) * 4, :], reads=[hk])

    def dbg_copy_scr(self, name, src, rows, cols, dt=F32):
        C = self.C
        o = self.dout('dbg_' + name, [rows, cols], dt)
        for r0 in range(0, rows, 128):
            r1 = min(rows, r0 + 128)
            C.dma('sp', o[r0:r1, :], src[r0:r1, :], reads=[name])


def build(debug=(), upto=99):
    P = Prog(debug)
    C = P.C
    P.load_consts()
    P.phase_mod()
    if 'mod' in P.debug:
        o = P.dout('dbg_mod', [128, 384])
        C.dma('sp', o[:, :], P.mod[:], reads=['mod'])
    if upto >= 1:
        P.begin_phase()
        P.phase_proj0()
        if 'pT' in P.debug:
            C.barrier()
            P.dbg_copy_scr('pT', P.scr['pT'], NCOL0, S0)
            P.dbg_copy_scr('vtok', P.scr['vtok'], S0, 256, BF16)
    if upto >= 2:
        P.begin_phase()
        P.phase_gqa()
    if upto >= 3:
        P.begin_phase()
        P.phase_rwkv_shift()
        if 'rwT' in P.debug:
            C.barrier()
            P.dbg_copy_scr('rwT', P.scr['rwT'], RW, S0)
    if upto >= 4:
        P.begin_phase()
        P.phase_rwkv_scan()
    if upto >= 2 and 'attnT' in P.debug:
        C.barrier()
        P.dbg_copy_scr('attnT', P.scr['attnT'], D if upto >= 4 else 1024, NX, BF16)
    if upto >= 5:
        P.begin_phase()
        resT = P.dscr('resT', [D, NX])
        rv = resT.rearrange("(k p) t -> p k t", p=128)
        tiles0 = [(i * 512, 512, 0, i * 512) for i in range(4)] + [(2048, 256, 0, 2048), (NE, CT, 1, T)]

        def out0(xv, xk, a0, n):
            for kq in range(4):
                C.dma('sp', rv[:, kq * 4:(kq + 1) * 4, a0:a0 + n], xv[:, kq * 4:(kq + 1) * 4, :], reads=[xk], writes=['resT'])
        P.phase_mlp(0, tiles0, P.scr['attnT'], P.inp['xT'], out0)
        if 'resT' in P.debug:
            C.barrier()
            P.dbg_copy_scr('resT', resT, D, NX)
    if upto >= 6:
        P.begin_phase()
        P.phase_proj1()
    if upto >= 7:
        P.begin_phase()
        P.phase_na()
        if 'attn1T' in P.debug:
            C.barrier()
            P.dbg_copy_scr('attn1T', P.scr['attn1T'], D, NOWN, BF16)
    if upto >= 8:
        P.begin_phase()
        outT = P.dout('outT', [D, NOWN])
        tiles1 = [(i * 512, 512, 0, i * 512) for i in range(4)]

        def out1(of, ofk, kc, a0, n):
            C.dma('sp', outT[kc * 128:(kc + 1) * 128, a0:a0 + n], of[:, 0:n], reads=[ofk], writes=['outT'])
        P.phase_mlp(1, tiles1, P.scr['attn1T'], P.scr['resT'], out1, final=True)
    C.finish('sp')
    return P


def pk(v):
    v = np.asarray(v, np.float32)
    return np.ascontiguousarray(v.reshape(-1, 128).T)


def rw_perm(flip):
    grp = [1, 0, 3, 2] if flip else [0, 1, 2, 3]
    ii = np.arange(872)
    if flip:
        ii = np.concatenate([ii[:768], ii[784:800], ii[768:784], ii[816:832], ii[800:816], ii[832:]])
    return np.concatenate([4 * ii + g for g in grp])


def head_perm(flip, n=16):
    grp = [1, 0, 3, 2] if flip else [0, 1, 2, 3]
    return np.concatenate([4 * np.arange(n) + g for g in grp])


def na_bias_tables(rpb, flip):
    out = np.full((32, 128, 15, 128), -30000.0, np.float32)
    kc = np.arange(64)
    qc = np.arange(64)
    for cls, qb in enumerate((0, 1, 4)):
        cs = max(2 * qb - 4, 0)
        for j in range(5):
            var = cls * 5 + j
            for a in range(2):
                for m_ in range(2):
                    kr = cs + 2 * j + a
                    qr = 2 * qb + m_
                    if flip:
                        kro, qro = 63 - kr, 63 - qr
                        kco, qco = 63 - kc, 63 - qc
                    else:
                        kro, qro, kco, qco = kr, qr, kc, qc
                    rs = min(max(qro - 4, 0), 56)
                    if not (rs <= kro < rs + 8):
                        continue
                    cstart = np.clip(qco - 8, 0, 48)
                    valid = (kco[:, None] >= cstart[None, :]) & (kco[:, None] < cstart[None, :] + 16)
                    dcol = kco[:, None] - qco[None, :] + 15
                    drow = kro - qro + 7
                    vals = rpb[:, drow, :][:, np.clip(dcol, 0, 30)]
                    blk = np.where(valid[None], vals, np.float32(-30000.0))
                    out[:, a * 64:(a + 1) * 64, var, m_ * 64:(m_ + 1) * 64] = blk
    return np.ascontiguousarray(out.reshape(32, 128, 15 * 128))


_SHARED = {}


def host_inputs(I, b, half):
    flip = (half == 1)
    m = {}
    x = I['x'][b]
    cx = I['ctx'][b]
    if flip:
        x = x[::-1]
        cx = cx[::-1]
    m['xT'] = np.ascontiguousarray(np.concatenate([x, cx], 0).T)
    cc = np.stack([pk(I['c'][b]), pk(I['c_ctx'])], -1).reshape(128, 32)
    m['ccol'] = np.ascontiguousarray(cc)
    key = ('shared', flip)
    if key in _SHARED:
        m.update(_SHARED[key])
        return m
    sh = {}
    sh['ident'] = np.eye(128, dtype=np.float32)
    ob = np.zeros((128, 128), np.float32)
    ob[:64, :64] = 1
    ob[64:, 64:] = 1
    sh['ones_blk'] = ob
    sh['ada_b'] = np.ascontiguousarray(np.concatenate([pk(I['l0_ada_b']), pk(I['l1_ada_b'])], 1))
    sh['nrm'] = np.ascontiguousarray(np.concatenate([pk(I[k]) for k in ('l0_norm1', 'l0_norm2', 'l1_norm1', 'l1_norm2', 'final_norm')], 1))
    sh['ada_w0'] = I['l0_ada_w']
    sh['ada_w1'] = I['l1_ada_w']
    w_in = I['l0_w_in']
    rwp = rw_perm(flip)
    perm = np.concatenate([np.arange(GQ), GQ + rwp])
    sh['w_in'] = np.ascontiguousarray(w_in[:, perm])
    qn = np.tile(I['l0_q_norm'], 2) * np.float32(64 ** -0.5)
    kn = np.tile(I['l0_k_norm'], 2)
    sh['qkn'] = np.ascontiguousarray(np.stack([qn, kn], 1).astype(np.float32))
    rot = np.zeros((128, 128), np.float32)
    for mm in range(128):
        if mm % 64 < 32:
            rot[mm + 32, mm] = -1.0
        else:
            rot[mm - 32, mm] = 1.0
    sh['rotm'] = rot
    tt = np.arange(T)
    if flip:
        tt = tt[::-1]
    row = (tt // 64).astype(np.float32)
    col = (tt % 64).astype(np.float32)
    inv = (np.float32(10000.0) ** (-np.arange(16, dtype=np.float32) / np.float32(16))).astype(np.float32)
    ang = np.concatenate([row[:, None] * inv, col[:, None] * inv], -1).astype(np.float32)
    cs = np.cos(ang).astype(np.float32)
    sn = np.sin(ang).astype(np.float32)
    idx = np.arange(128) % 32
    sh['rope_cos'] = np.ascontiguousarray(cs[:, idx].T)
    sh['rope_sin'] = np.ascontiguousarray(sn[:, idx].T)
    mu = I['l0_shift_mu'][rwp]
    mc = np.zeros((128, 28), np.float32)
    for g in range(4):
        for j in range(7):
            rows = min(128, 872 - j * 128)
            mc[:rows, g * 7 + j] = mu[g * 872 + j * 128:g * 872 + j * 128 + rows]
    sh['mu_col'] = mc
    hp = head_perm(flip)
    chan = (np.arange(16)[:, None] * 64 + hp[None, :]).reshape(-1)
    rwc = np.zeros((64, 16, 5), np.float32)
    for q, nm in enumerate(('l0_k_k', 'l0_k_a', None, 'l0_lnx_g', 'l0_lnx_b')):
        vec = I['l0_r_k'].reshape(-1) if nm is None else I[nm]
        rwc[:, :, q] = vec[chan].reshape(16, 64).T
    sh['rw_cols'] = np.ascontiguousarray(rwc.reshape(64, 80))
    dn = ('b', 'f') if flip else ('f', 'b')
    lw = np.zeros((65, 4, 1024), np.float32)
    for d in range(2):
        lw[:64, d] = I[f'l0_ww2_{dn[d]}'][hp][:, chan]
        lw[64, d] = I[f'l0_w0_{dn[d]}'][chan]
        lw[:64, 2 + d] = I[f'l0_wa2_{dn[d]}'][hp][:, chan]
        lw[64, 2 + d] = I[f'l0_a0_{dn[d]}'][chan]
    sh['lora_w'] = np.ascontiguousarray(lw.reshape(65, 4096))
    grp = [1, 0, 3, 2] if flip else [0, 1, 2, 3]
    gp = np.concatenate([4 * np.arange(32) + g for g in grp] + [4 * np.arange(32, 40) + g for g in grp])
    sh['wg2'] = np.ascontiguousarray(I['l0_wg2'][gp][:, chan])
    s_ = np.arange(64)
    Ms = (s_[:, None] < s_[None, :]).astype(np.float32)
    Mi = (s_[:, None] <= s_[None, :]).astype(np.float32)
    sh['scan_masks'] = np.ascontiguousarray(np.concatenate([Ms, Mi, Ms.T, Ms.T, Mi.T, Ms], 1))
    rst = np.ones((64, 512), np.float32)
    rst[:, ::64] = 0
    sh['chunk_rst'] = rst
    wo = I['l0_w_out']
    sh['w_out0'] = np.ascontiguousarray(np.concatenate([wo[:1024], wo[1024 + chan]], 0))
    sh['mlp_w1_0'] = I['l0_mlp_w1']
    sh['mlp_w2_0'] = I['l0_mlp_w2']
    sh['w_qkv'] = I['l1_w_qkv']
    sh['na_bias'] = na_bias_tables(I['l1_rpb'], flip)
    sh['w_out1'] = I['l1_w_out']
    sh['mlp_w1_1'] = I['l1_mlp_w1']
    sh['mlp_w2_1'] = I['l1_mlp_w2']
    _SHARED[key] = sh
    m.update(sh)
    return m


_PROG = {}


def kernel(**inputs):
    I = {k: np.asarray(v) for k, v in inputs.items()}
    if 'p' not in _PROG:
        _PROG['p'] = build()
    P = _PROG['p']
    _SHARED.clear()
    in_maps = []
    for b in range(4):
        for half in range(2):
            m = host_inputs(I, b, half)
            in_maps.append({k: v for k, v in m.items() if k in P.inp})
    res = run_bass_kernel_spmd(P.nc, in_maps, core_ids=list(range(8)))
    out = np.empty((4, T, D), np.float32)
    for b in range(4):
        o0 = np.asarray(res.results[2 * b]['outT'])
        o1 = np.asarray(res.results[2 * b + 1]['outT'])
        out[b, :NOWN] = o0.T
        out[b, NOWN:] = o1.T[::-1]
    _SHARED.clear()
    return out
```

```python
import os
import numpy as np
import concourse.bass as bass
import concourse.mybir as mybir
from concourse.bass_utils import run_bass_kernel_spmd
from contextlib import ExitStack

F32 = mybir.dt.float32
BF16 = mybir.dt.bfloat16
F32R = mybir.dt.float32r
AF = mybir.ActivationFunctionType
ALU = mybir.AluOpType

D = 2048
KC = 16
T = 4096
CT = 256
S0 = T + CT
NE = 2304
NX = NE + CT
NOWN = 2048
RW = 3488
GQ = 1536
NCOL0 = GQ + RW
EPS = 1e-6


class Ctx:
    COMPUTE = ('pe', 'act', 'dve', 'pool')

    def __init__(self, nc, n_dma_sems=20):
        self.nc = nc
        self.eng = {'pe': nc.tensor, 'act': nc.scalar, 'dve': nc.vector, 'pool': nc.gpsimd, 'sp': nc.sync}
        self.sem = {e: nc.alloc_semaphore('c_' + e) for e in self.COMPUTE}
        self.cnt = {e: 0 for e in self.COMPUTE}
        self.dq = {}
        for q in ('sp', 'pool', 'act'):
            self.dq[q] = dict(sems=[nc.alloc_semaphore(f'd_{q}{i}') for i in range(n_dma_sems)],
                              val=[0] * n_dma_sems, nxt=0)
        self.seen = {}
        self.lastw = {}
        self.lastr = {}
        self.psn = 0
        self.nrot = 8
        self.ps_tiles = [nc.alloc_psum_tensor(f"ps{i}", [128, 512], F32) for i in range(8)]
        self.rr = {}

    def _semobj(self, key):
        if isinstance(key, str):
            return self.sem[key]
        q, i = key
        return self.dq[q]['sems'][i]

    def _wait(self, eng, tok):
        key, val = tok
        if eng == 'pe' and key == 'pe':
            return
        if self.seen.get((eng, key), 0) >= val:
            return
        self.eng[eng].wait_ge(self._semobj(key), val)
        self.seen[(eng, key)] = val

    def _deps(self, eng, reads, writes):
        for k in reads:
            t = self.lastw.get(k)
            if t is not None:
                self._wait(eng, t)
        for k in writes:
            t = self.lastw.get(k)
            if t is not None:
                self._wait(eng, t)
            for t in self.lastr.get(k, {}).values():
                self._wait(eng, t)

    def _record(self, tok, reads, writes):
        for k in reads:
            self.lastr.setdefault(k, {})[tok[0]] = tok
        for k in writes:
            self.lastw[k] = tok
            self.lastr[k] = {}

    def op(self, eng, fn, reads=(), writes=(), signal=True):
        self._deps(eng, reads, writes)
        ins = fn(self.eng[eng])
        if signal:
            self.cnt[eng] += 1
            ins.then_inc(self.sem[eng], 1)
            tok = (eng, self.cnt[eng])
        else:
            tok = (eng, self.cnt[eng] + 1)
        self._record(tok, reads, writes)
        return ins

    def dma(self, q, out, in_, reads=(), writes=(), **kw):
        d = self.dq[q]
        i = d['nxt']
        d['nxt'] = (i + 1) % len(d['sems'])
        if d['val'][i] > 0:
            self._wait(q, ((q, i), d['val'][i]))
        self._deps(q, reads, writes)
        ins = self.eng[q].dma_start(out=out, in_=in_, **kw)
        d['val'][i] += 16
        ins.then_inc(d['sems'][i], 16)
        tok = ((q, i), d['val'][i])
        self._record(tok, reads, writes)
        return ins

    def barrier(self):
        toks = [(e, self.cnt[e]) for e in self.COMPUTE if self.cnt[e] > 0]
        for q, d in self.dq.items():
            for i, v in enumerate(d['val']):
                if v > 0:
                    toks.append(((q, i), v))
        for e in ('pe', 'act', 'dve', 'pool', 'sp'):
            for t in toks:
                if t[0] == e:
                    continue
                self._wait(e, t)
        self.lastw = {}
        self.lastr = {}

    def finish(self, q='sp'):
        for e in self.COMPUTE:
            if self.cnt[e] > 0:
                self._wait(q, (e, self.cnt[e]))
        for qq, d in self.dq.items():
            for i, v in enumerate(d['val']):
                if v > 0:
                    self._wait(q, ((qq, i), v))

    def ps(self):
        i = self.psn % self.nrot
        self.psn = (i + 1) % self.nrot
        return self.ps_tiles[i], f'ps{i}'

    def psh(self):
        i = getattr(self, '_hn', 0) % len(self.hbanks)
        self._hn = (i + 1) % len(self.hbanks)
        b = self.hbanks[i]
        return self.ps_tiles[b], f'ps{b}'

    def pick(self, name, choices):
        i = self.rr.get(name, 0)
        self.rr[name] = i + 1
        return choices[i % len(choices)]


class Pool:
    def __init__(self, P, name, shape, dtype, n):
        self.t = [P.sb(f"{name}{i}", shape, dtype) for i in range(n)]
        self.k = [f"{name}{i}" for i in range(n)]
        self.i = 0

    def get(self):
        i = self.i
        self.i = (i + 1) % len(self.t)
        return self.t[i], self.k[i]


class Prog:
    def __init__(self, debug=()):
        self.debug = set(debug)
        nc = self.nc = bass.Bass("TRN2", target_bir_lowering=False)
        self.C = Ctx(nc)
        self.inp = {}
        self.outs = {}
        self.scr = {}
        self.gstack = ExitStack()
        self.stack = self.gstack

    def begin_phase(self):
        self.C.barrier()
        if self.stack is not self.gstack:
            self.stack.close()
        self.stack = ExitStack()

    def sub_begin(self):
        self._saved = self.stack
        self.stack = ExitStack()

    def sub_end(self):
        self.C.barrier()
        self.stack.close()
        self.stack = self._saved

    def din(self, name, shape, dt=F32):
        self.inp[name] = self.nc.dram_tensor(name, list(shape), dt, kind="ExternalInput").ap()
        return self.inp[name]

    def dout(self, name, shape, dt=F32):
        self.outs[name] = self.nc.dram_tensor(name, list(shape), dt, kind="ExternalOutput").ap()
        return self.outs[name]

    def dscr(self, name, shape, dt=F32):
        self.scr[name] = self.nc.dram_tensor(name, list(shape), dt, kind="Internal").ap()
        return self.scr[name]

    def sb(self, name, shape, dt=F32):
        self._uid = getattr(self, '_uid', 0) + 1
        return self.stack.enter_context(self.nc.sbuf_tensor(f"{name}_u{self._uid}", list(shape), dt))

    def load_consts(self):
        C = self.C
        self.din('ident', [128, 128])
        self.din('ones_blk', [128, 128])
        self.ident = self.sb('ident_sb', [128, 128])
        self.ones = self.sb('ones_sb', [128, 128])
        self.ones_blk = self.sb('ones_blk_sb', [128, 128])
        C.dma('sp', self.ident[:], self.inp['ident'][:, :], writes=['ident'])
        C.dma('sp', self.ones_blk[:], self.inp['ones_blk'][:, :], writes=['ones_blk'])
        C.op('dve', lambda e: e.memset(self.ones[:], 1.0), writes=['ones'])
        self.din('ccol', [128, 32])
        self.din('ada_b', [128, 192])
        self.din('nrm', [128, 80])
        self.ccol = self.sb('ccol_sb', [128, 32])
        self.ada_b = self.sb('ada_b_sb', [128, 192])
        self.nrm = self.sb('nrm_sb', [128, 80])
        C.dma('sp', self.ccol[:], self.inp['ccol'][:, :], writes=['ccol'])
        C.dma('sp', self.ada_b[:], self.inp['ada_b'][:, :], writes=['ada_b'])
        C.dma('sp', self.nrm[:], self.inp['nrm'][:, :], writes=['nrm'])

    def phase_mod(self):
        C, nc = self.C, self.nc
        self.din('ada_w0', [D, 6 * D])
        self.din('ada_w1', [D, 6 * D])
        self.mod = self.sb('mod', [128, 2 * 96 * 2])
        modv = self.mod[:].rearrange("p (l j v) -> p l j v", l=2, j=96)
        self.Acol = self.sb('Acol', [128, 2 * 2 * 16 * 2])
        Av = self.Acol[:].rearrange("p (l w k v) -> p l w k v", l=2, w=2, k=16)
        self.begin_phase()
        s = self.sb('silu_c', [128, 32])
        C.op('act', lambda e: e.activation(out=s[:], in_=self.ccol[:], func=AF.Silu), reads=['ccol'], writes=['silu_c'])
        NCB = 768
        wp = Pool(self, 'adaw', [128, KC * NCB], F32, 2)
        for L in range(2):
            W = self.inp[f'ada_w{L}'].rearrange("(k p) c -> p k c", p=128)
            for cb in range(6 * D // NCB):
                wt, wk = wp.get()
                wv = wt[:].rearrange("p (k c) -> p k c", k=KC)
                for kq in range(4):
                    C.dma('sp', wv[:, kq * 4:(kq + 1) * 4, :], W[:, kq * 4:(kq + 1) * 4, cb * NCB:(cb + 1) * NCB], writes=[wk])
                pt, pk = C.ps()
                for j in range(NCB // 128):
                    for kc in range(KC):
                        C.op('pe', lambda e, j=j, kc=kc: e.matmul(pt[:, j * 2:j * 2 + 2], lhsT=wv[:, kc, j * 128:(j + 1) * 128],
                                                                 rhs=s[:, kc * 2:kc * 2 + 2], start=(kc == 0), stop=(kc == KC - 1)),
                             reads=[wk, 'silu_c'], writes=[pk], signal=(kc == KC - 1 and j == NCB // 128 - 1))
                for j in range(NCB // 128):
                    jg = cb * (NCB // 128) + j
                    C.op('dve', lambda e, j=j, jg=jg: e.tensor_scalar(out=modv[:, L, jg, :], in0=pt[:, j * 2:j * 2 + 2],
                                                                     scalar1=self.ada_b[:, L * 96 + jg:L * 96 + jg + 1], scalar2=None, op0=ALU.add),
                         reads=[pk, 'ada_b'], writes=['mod'])
            for w, (sci, nidx) in enumerate(((1, 2 * L), (4, 2 * L + 1))):
                for v in range(2):
                    C.op('dve', lambda e, w=w, sci=sci, nidx=nidx, v=v: e.scalar_tensor_tensor(
                        out=Av[:, L, w, :, v], in0=modv[:, L, sci * 16:(sci + 1) * 16, v], scalar=1.0,
                        in1=self.nrm[:, nidx * 16:(nidx + 1) * 16], op0=ALU.add, op1=ALU.mult),
                         reads=['mod', 'nrm'], writes=['Acol'])
        self.modv, self.Av = modv, Av

    def mcol(self, L, s, kc, v):
        return self.modv[:, L, s * 16 + kc, v:v + 1]

    def norm_mod(self, xt, xk, n, h, hk, Afn, shfn, tmpp, sqp):
        C = self.C
        pt, pk = C.ps()
        for kc in range(KC):
            sq, sqk = sqp.get()
            C.op('act', lambda e, kc=kc, sq=sq: e.activation(out=sq[:, 0:n], in_=xt[:, kc, :], func=AF.Square), reads=[xk], writes=[sqk])
            C.op('pe', lambda e, kc=kc, sq=sq: e.matmul(pt[:, 0:n], lhsT=self.ones[:], rhs=sq[:, 0:n], start=(kc == 0), stop=(kc == KC - 1)),
                 reads=[sqk, 'ones'], writes=[pk])
        rs, rsk = sqp.get()
        C.op('act', lambda e: e.activation(out=rs[:, 0:n], in_=pt[:, 0:n], func=AF.Sqrt, bias=EPS, scale=1.0 / D), reads=[pk], writes=[rsk])
        C.op('dve', lambda e: e.reciprocal(out=rs[:, 0:n], in_=rs[:, 0:n]), reads=[rsk], writes=[rsk])
        for kc in range(KC):
            tm, tmk = tmpp.get()
            eng = 'dve'
            C.op(eng, lambda e, kc=kc, tm=tm: e.tensor_tensor(out=tm[:, 0:n], in0=xt[:, kc, :], in1=rs[:, 0:n], op=ALU.mult),
                 reads=[xk, rsk], writes=[tmk])
            if shfn is not None:
                C.op('act', lambda e, kc=kc, tm=tm: e.activation(out=h[:, kc, :], in_=tm[:, 0:n], func=AF.Identity, bias=shfn(kc), scale=Afn(kc)),
                     reads=[tmk, 'mod', 'Acol'], writes=[hk])
            else:
                C.op('act', lambda e, kc=kc, tm=tm: e.activation(out=h[:, kc, :], in_=tm[:, 0:n], func=AF.Copy, scale=Afn(kc)),
                     reads=[tmk, 'nrm'], writes=[hk])

    def wload_init(self, WB=256, nst=2, nwb=3):
        self.WB = WB
        self.wst = Pool(self, 'wst', [128, KC * WB], F32, nst)
        self.wbp = Pool(self, 'wbf', [128, KC * WB], BF16, nwb)

    def wstream(self, blocks, pf=1):
        self._wq = list(blocks)
        self._wi = 0
        self._wissued = []
        self._wpf = pf

    def wnext(self):
        while len(self._wissued) <= self._wi + self._wpf and len(self._wissued) < len(self._wq):
            b = self._wq[len(self._wissued)]
            self._wissued.append(self.wload(*b))
        r = self._wissued[self._wi]
        self._wi += 1
        return r

    def wload(self, Wv, c0, ncols, kchunks=KC, k0=0):
        C = self.C
        st, sk = self.wst.get()
        sv = st[:].rearrange("p (k c) -> p k c", k=KC)[:, 0:kchunks, 0:ncols]
        wb, wk = self.wbp.get()
        wv = wb[:].rearrange("p (k c) -> p k c", k=KC)[:, 0:kchunks, 0:ncols]
        step = max(1, kchunks // 4)
        for kq in range(0, kchunks, step):
            ke = min(kchunks, kq + step)
            C.dma('sp', sv[:, kq:ke, :], Wv[:, k0 + kq:k0 + ke, c0:c0 + ncols], writes=[sk])
        for kq in range(0, kchunks, step):
            ke = min(kchunks, kq + step)
            eng = C.pick('wcast', ['dve', 'act', 'pool', 'act', 'dve', 'act', 'dve', 'act'])
            if eng == 'act':
                C.op('act', lambda e: e.activation(out=wv[:, kq:ke, :], in_=sv[:, kq:ke, :], func=AF.Copy), reads=[sk], writes=[wk])
            else:
                C.op(eng, lambda e: e.tensor_copy(out=wv[:, kq:ke, :], in_=sv[:, kq:ke, :]), reads=[sk], writes=[wk])
        return wv, wk

    def phase_proj0(self):
        C, nc = self.C, self.nc
        xT = self.din('xT', [D, S0]).rearrange("(k p) t -> p k t", p=128)
        W = self.din('w_in', [D, NCOL0]).rearrange("(k p) c -> p k c", p=128)
        pT = self.dscr('pT', [NCOL0, S0])
        vtok = self.dscr('vtok', [S0, 256], BF16)
        xp = Pool(self, 'p1x', [128, KC * 512], F32, 2)
        hp = Pool(self, 'p1h', [128, KC * 512], BF16, 2)
        sqp = Pool(self, 'p1sq', [128, 512], F32, 3)
        tmpp = Pool(self, 'p1tm', [128, 512], F32, 4)
        WB = 256
        self.wload_init(WB)
        evp = Pool(self, 'p1ev', [128, 512], F32, 4)
        vp = Pool(self, 'p1v', [128, 256], BF16, 3)
        tiles = [(i * 512, 512, 0) for i in range(8)] + [(T, CT, 1)]
        VC0, VC1 = 1280, 1536
        fm_blocks = [(c, min(WB, NCOL0 - c)) for c in list(range(0, VC0, WB)) + list(range(VC1, NCOL0, WB))]
        wl = []
        for _ in tiles:
            wl += [(W, c0, nc_, KC, 0) for (c0, nc_) in fm_blocks] + [(W, VC0, 256, KC, 0)]
        self.wstream(wl, 1)
        for (t0, n, v) in tiles:
            xt, xk = xp.get()
            xv = xt[:].rearrange("p (k t) -> p k t", k=KC)[:, :, 0:n]
            for kq in range(4):
                C.dma('sp', xv[:, kq * 4:(kq + 1) * 4, :], xT[:, kq * 4:(kq + 1) * 4, t0:t0 + n], writes=[xk])
            ht, hk = hp.get()
            hv = ht[:].rearrange("p (k t) -> p k t", k=KC)[:, :, 0:n]
            self.norm_mod(xv, xk, n, hv, hk, lambda kc: self.Av[:, 0, 0, kc, v:v + 1], lambda kc: self.mcol(0, 0, kc, v), tmpp, sqp)
            if 'h0' in self.debug:
                self._dbg_h(hv, hk, t0, n)
            for (c0, nc_) in fm_blocks:
                wv, wk = self.wnext()
                for m0 in range(0, nc_, 128):
                    m = min(128, nc_ - m0)
                    pt, pk = C.ps()
                    for kc in range(KC):
                        C.op('pe', lambda e, kc=kc, m0=m0, m=m: e.matmul(pt[0:m, 0:n], lhsT=wv[:, kc, m0:m0 + m], rhs=hv[:, kc, :],
                                                                        start=(kc == 0), stop=(kc == KC - 1)),
                             reads=[wk, hk], writes=[pk], signal=(kc == KC - 1))
                    ev, evk = evp.get()
                    eng = C.pick('p1ev', ['act', 'dve'])
                    if eng == 'act':
                        C.op('act', lambda e, m=m: e.activation(out=ev[0:m, 0:n], in_=pt[0:m, 0:n], func=AF.Copy), reads=[pk], writes=[evk])
                    else:
                        C.op('dve', lambda e, m=m: e.tensor_copy(out=ev[0:m, 0:n], in_=pt[0:m, 0:n]), reads=[pk], writes=[evk])
                    C.dma('sp', pT[c0 + m0:c0 + m0 + m, t0:t0 + n], ev[0:m, 0:n], reads=[evk], writes=['pT'])
            wv, wk = self.wnext()
            for s0 in range(0, n, 128):
                pt, pk = C.ps()
                for kc in range(KC):
                    C.op('pe', lambda e, kc=kc, s0=s0: e.matmul(pt[:, 0:256], lhsT=hv[:, kc, s0:s0 + 128], rhs=wv[:, kc, :],
                                                                start=(kc == 0), stop=(kc == KC - 1)),
                         reads=[wk, hk], writes=[pk], signal=(kc == KC - 1))
                vt, vk = vp.get()
                C.op('act', lambda e: e.activation(out=vt[:], in_=pt[:, 0:256], func=AF.Copy), reads=[pk], writes=[vk])
                C.dma('sp', vtok[t0 + s0:t0 + s0 + 128, :], vt[:], reads=[vk], writes=['vtok'])


    def headnorm(self, raw, rawk, n, colscalar, tp, sqp):
        C = self.C
        sq, sqk = sqp.get()
        C.op('act', lambda e: e.activation(out=sq[:, 0:n], in_=raw, func=AF.Square), reads=[rawk], writes=[sqk])
        pt, pk = C.ps()
        C.op('pe', lambda e: e.matmul(pt[:, 0:n], lhsT=self.ones_blk[:], rhs=sq[:, 0:n], start=True, stop=True), reads=[sqk, 'ones_blk'], writes=[pk])
        rs, rsk = sqp.get()
        C.op('act', lambda e: e.activation(out=rs[:, 0:n], in_=pt[:, 0:n], func=AF.Sqrt, bias=EPS, scale=1.0 / 64), reads=[pk], writes=[rsk])
        C.op('dve', lambda e: e.reciprocal(out=rs[:, 0:n], in_=rs[:, 0:n]), reads=[rsk], writes=[rsk])
        kn, knk = tp.get()
        C.op('dve', lambda e: e.scalar_tensor_tensor(out=kn[:, 0:n], in0=raw, scalar=colscalar, in1=rs[:, 0:n], op0=ALU.mult, op1=ALU.mult),
             reads=[rawk, rsk, 'qkn'], writes=[knk])
        return kn, knk

    def rope(self, kn, knk, n, cs0, out, outk, tp):
        C = self.C
        pt, pk = C.ps()
        C.op('pe', lambda e: e.matmul(pt[:, 0:n], lhsT=self.rotm[:], rhs=kn[:, 0:n], start=True, stop=True), reads=[knk, 'rotm'], writes=[pk])
        t1, t1k = tp.get()
        C.op('dve', lambda e: e.tensor_tensor(out=t1[:, 0:n], in0=kn[:, 0:n], in1=self.cos[:, cs0:cs0 + n], op=ALU.mult), reads=[knk, 'cos'], writes=[t1k])
        t2, t2k = tp.get()
        C.op('dve', lambda e: e.tensor_tensor(out=t2[:, 0:n], in0=pt[:, 0:n], in1=self.sin[:, cs0:cs0 + n], op=ALU.mult), reads=[pk, 'sin'], writes=[t2k])
        C.op('dve', lambda e: e.tensor_tensor(out=out, in0=t1[:, 0:n], in1=t2[:, 0:n], op=ALU.add), reads=[t1k, t2k], writes=[outk])

    def phase_gqa(self):
        C, nc = self.C, self.nc
        pT = self.scr['pT']
        vtok = self.scr['vtok']
        attnT = self.dscr('attnT', [D, NX], BF16)
        self.din('qkn', [128, 2])
        self.din('rotm', [128, 128])
        self.din('rope_cos', [128, T])
        self.din('rope_sin', [128, T])
        KT = [self.sb(f'KT{i}', [128, S0], BF16) for i in range(2)]
        QT = [self.sb(f'QT{i}', [128, NX], BF16) for i in range(8)]
        Vaug = self.sb('Vaug', [128, 34 * 4 * 128], BF16)
        Vv = Vaug[:].rearrange("p (c h d) -> p c h d", c=34, h=4)
        qkn = self.sb('qkn_sb', [128, 2])
        self.rotm = self.sb('rotm_sb', [128, 128])
        C.dma('sp', qkn[:], self.inp['qkn'][:, :], writes=['qkn'])
        C.dma('sp', self.rotm[:], self.inp['rotm'][:, :], writes=['rotm'])
        self.sub_begin()
        self.cos = self.sb('cos_sb', [128, T])
        self.sin = self.sb('sin_sb', [128, T])
        for q4 in range(4):
            C.dma('sp', self.cos[:, q4 * 1024:(q4 + 1) * 1024], self.inp['rope_cos'][:, q4 * 1024:(q4 + 1) * 1024], writes=['cos'])
            C.dma('sp', self.sin[:, q4 * 1024:(q4 + 1) * 1024], self.inp['rope_sin'][:, q4 * 1024:(q4 + 1) * 1024], writes=['sin'])
        rawp = Pool(self, 'g_raw', [128, S0], F32, 2)
        sqp = Pool(self, 'g_sq', [128, 512], F32, 4)
        tp = Pool(self, 'g_tp', [128, 512], F32, 6)
        vst = self.sb('g_vst', [128, 34 * 256], BF16)
        vsv = vst[:].rearrange("p (c x) -> p c x", c=34)
        vsrc = vtok.rearrange("(c p) x -> p c x", p=128)
        for c4 in range(0, 34, 6):
            c5 = min(34, c4 + 6)
            C.dma('sp', vsv[:, c4:c5, :], vsrc[:, c4:c5, :], writes=['g_vst'])
        C.op('dve', lambda e: e.memset(Vaug[:], 1.0), writes=['Vaug'])
        for h in range(4):
            C.op('dve', lambda e, h=h: e.tensor_copy(out=Vv[:, :, h, 0:64], in_=vsv[:, :, h * 64:(h + 1) * 64]), reads=['g_vst'], writes=['Vaug'])
        ktiles = [(i * 512, 512, True) for i in range(8)] + [(T, CT, False)]
        for kt in range(2):
            raw, rawk = rawp.get()
            for q4 in range(0, S0, 1088):
                C.dma('sp', raw[:, q4:q4 + 1088], pT[1024 + kt * 128:1024 + (kt + 1) * 128, q4:q4 + 1088], writes=[rawk])
            for (t0, n, lat) in ktiles:
                kn, knk = self.headnorm(raw[:, t0:t0 + n], rawk, n, qkn[:, 1:2], tp, sqp)
                if lat:
                    self.rope(kn, knk, n, t0, KT[kt][:, t0:t0 + n], f'KT{kt}', tp)
                else:
                    C.op('act', lambda e: e.activation(out=KT[kt][:, t0:t0 + n], in_=kn[:, 0:n], func=AF.Copy), reads=[knk], writes=[f'KT{kt}'])
        self.qpairs = [(0, 4), (1, 5), (2, 6), (3, 7), (8, 12), (9, 13), (10, 14), (11, 15)]
        qtiles = [(i * 512, i * 512, 512, True) for i in range(4)] + [(2048, 2048, 256, True), (T, NE, CT, False)]
        for j, (a, b) in enumerate(self.qpairs):
            raw, rawk = rawp.get()
            for hh, hq in enumerate((a, b)):
                C.dma('sp', raw[hh * 64:(hh + 1) * 64, 0:NE], pT[hq * 64:(hq + 1) * 64, 0:NE], writes=[rawk])
                C.dma('sp', raw[hh * 64:(hh + 1) * 64, NE:NX], pT[hq * 64:(hq + 1) * 64, T:S0], writes=[rawk])
            for (src0, x0, n, lat) in qtiles:
                kn, knk = self.headnorm(raw[:, x0:x0 + n], rawk, n, qkn[:, 0:1], tp, sqp)
                if lat:
                    self.rope(kn, knk, n, src0, QT[j][:, x0:x0 + n], f'QT{j}', tp)
                else:
                    C.op('act', lambda e: e.activation(out=QT[j][:, x0:x0 + n], in_=kn[:, 0:n], func=AF.Copy), reads=[knk], writes=[f'QT{j}'])
        if 'qk' in self.debug:
            o = self.dout('dbg_KT', [256, S0], BF16)
            for i in range(2):
                C.dma('sp', o[i * 128:(i + 1) * 128, :], KT[i][:], reads=[f'KT{i}'])
            o = self.dout('dbg_QT', [1024, NX], BF16)
            for i in range(8):
                C.dma('sp', o[i * 128:(i + 1) * 128, :], QT[i][:], reads=[f'QT{i}'])
        self.sub_end()
        C.nrot = 6
        ptp = Pool(self, 'g_pt', [128, 512], BF16, 6)
        osp = Pool(self, 'g_os', [128, 512], F32, 2)
        rsp = Pool(self, 'g_rs', [64, 512], F32, 2)
        onp = Pool(self, 'g_on', [128, 512], BF16, 2)
        acc_i = 0
        for hq in range(16):
            kvh = hq // 4
            base = (kvh % 2) * 64
            j = [i for i, pr in enumerate(self.qpairs) if hq in pr][0]
            jobs = [(i * 512, 512, list(range(34))) for i in range(4)] + [(2048, 256, list(range(34))), (NE, CT, [32, 33])]
            for (q0, n, chunks) in jobs:
                po = C.ps_tiles[6 + acc_i % 2]
                pok = f'ps{6 + acc_i % 2}'
                acc_i += 1
                pend = []

                def pv(item):
                    pb, pbk, c, ci = item
                    C.op('pe', lambda e: e.matmul(po[:, 0:n], lhsT=Vv[:, c, kvh, :], rhs=pb[:, 0:n],
                                                  start=(ci == 0), stop=(ci == len(chunks) - 1)),
                         reads=['Vaug', pbk], writes=[pok])
                for ci, c in enumerate(chunks):
                    pt, pk = C.ps()
                    C.op('pe', lambda e, c=c: e.matmul(pt[:, 0:n], lhsT=KT[kvh // 2][base:base + 64, c * 128:(c + 1) * 128],
                                                      rhs=QT[j][base:base + 64, q0:q0 + n], start=True, stop=True),
                         reads=[f'KT{kvh // 2}', f'QT{j}'], writes=[pk])
                    pb, pbk = ptp.get()
                    C.op('act', lambda e: e.activation(out=pb[:, 0:n], in_=pt[:, 0:n], func=AF.Exp), reads=[pk], writes=[pbk])
                    pend.append((pb, pbk, c, ci))
                    if len(pend) > 2:
                        pv(pend.pop(0))
                while pend:
                    pv(pend.pop(0))
                osb, osk = osp.get()
                C.op('dve', lambda e: e.tensor_copy(out=osb[:, 0:n], in_=po[:, 0:n]), reads=[pok], writes=[osk])
                rs, rsk = rsp.get()
                C.dma('sp', rs[:, 0:n], osb[64:128, 0:n], reads=[osk], writes=[rsk])
                C.op('dve', lambda e: e.reciprocal(out=rs[:, 0:n], in_=rs[:, 0:n]), reads=[rsk], writes=[rsk])
                on, onk = onp.get()
                C.op('dve', lambda e: e.tensor_tensor(out=on[0:64, 0:n], in0=osb[0:64, 0:n], in1=rs[:, 0:n], op=ALU.mult), reads=[osk, rsk], writes=[onk])
                C.dma('sp', attnT[hq * 64:(hq + 1) * 64, q0:q0 + n], on[0:64, 0:n], reads=[onk], writes=['attnT'])
        C.nrot = 8

    def phase_rwkv_shift(self):
        C = self.C
        pT = self.scr['pT']
        rwT = self.dscr('rwT', [RW, S0])
        self.din('mu_col', [128, 28])
        mu = self.sb('mu_sb', [128, 28])
        omu = self.sb('omu_sb', [128, 28])
        C.dma('sp', mu[:], self.inp['mu_col'][:, :], writes=['mu'])
        C.op('dve', lambda e: e.tensor_scalar(out=omu[:], in0=mu[:], scalar1=-1.0, scalar2=1.0, op0=ALU.mult, op1=ALU.add), reads=['mu'], writes=['omu'])
        rawp = Pool(self, 'rs_raw', [128, S0], F32, 2)
        outp = Pool(self, 'rs_out', [128, S0], F32, 2)
        for g in range(4):
            for j in range(7):
                rows = min(128, 872 - j * 128)
                r0 = g * 872 + j * 128
                ci = g * 7 + j
                raw, rk = rawp.get()
                for q4 in range(0, S0, 1088):
                    C.dma('sp', raw[0:rows, q4:q4 + 1088], pT[GQ + r0:GQ + r0 + rows, q4:q4 + 1088], writes=[rk])
                o, ok = outp.get()
                C.op('act', lambda e: e.activation(out=o[0:rows, :], in_=raw[0:rows, :], func=AF.Copy, scale=omu[0:rows, ci:ci + 1]), reads=[rk, 'omu'], writes=[ok])
                m = mu[0:rows, ci:ci + 1]
                rv = raw[0:rows, 0:T].rearrange("p (r c) -> p r c", c=64)
                ov = o[0:rows, 0:T].rearrange("p (r c) -> p r c", c=64)
                if g == 0:
                    src, dst = rv[:, :, 0:63], ov[:, :, 1:64]
                elif g == 1:
                    src, dst = rv[:, :, 1:64], ov[:, :, 0:63]
                elif g == 2:
                    src, dst = raw[0:rows, 0:T - 64], o[0:rows, 64:T]
                else:
                    src, dst = raw[0:rows, 64:T], o[0:rows, 0:T - 64]
                C.op('dve', lambda e: e.scalar_tensor_tensor(out=dst, in0=src, scalar=m, in1=dst, op0=ALU.mult, op1=ALU.add), reads=[rk, ok, 'mu'], writes=[ok])
                if g in (0, 2):
                    src, dst = raw[0:rows, T:S0 - 1], o[0:rows, T + 1:S0]
                else:
                    src, dst = raw[0:rows, T + 1:S0], o[0:rows, T:S0 - 1]
                C.op('dve', lambda e: e.scalar_tensor_tensor(out=dst, in0=src, scalar=m, in1=dst, op0=ALU.mult, op1=ALU.add), reads=[rk, ok, 'mu'], writes=[ok])
                for q4 in range(0, S0, 1088):
                    C.dma('sp', rwT[r0:r0 + rows, q4:q4 + 1088], o[0:rows, q4:q4 + 1088], reads=[ok], writes=['rwT'])

    def rw_rows(self, i0, hd=None, n16=16):
        return [(g * 872 + i0, n16) for g in range(4)]

    def phase_rwkv_scan(self):
        C, nc = self.C, self.nc
        rwT = self.scr['rwT']
        attnT = self.scr['attnT']
        yfT = self.dscr('yfT', [1024, NX])
        self.din('rw_cols', [64, 16 * 5])
        self.din('lora_w', [65, 4 * 1024])
        self.din('wg2', [160, 1024])
        self.din('scan_masks', [64, 6 * 64])
        self.din('chunk_rst', [64, 512])
        rwc = self.sb('rwc', [64, 80])
        oka = self.sb('oka', [64, 16])
        lw_sb = self.sb('lora_sb', [65, 4096])
        wg_a = self.sb('wg_a', [128, 1024])
        wg_b = self.sb('wg_b', [32, 1024])
        msk = self.sb('scan_msk', [64, 384])
        rst = self.sb('chunk_rst_sb', [64, 512])
        C.dma('sp', rwc[:], self.inp['rw_cols'][:, :], writes=['rwc'])
        for q in range(4):
            C.dma('sp', lw_sb[:, q * 1024:(q + 1) * 1024], self.inp['lora_w'][:, q * 1024:(q + 1) * 1024], writes=['lora'])
        C.dma('sp', wg_a[:], self.inp['wg2'][0:128, :], writes=['wg'])
        C.dma('sp', wg_b[:], self.inp['wg2'][128:160, :], writes=['wg'])
        C.dma('sp', msk[:], self.inp['scan_masks'][:, :], writes=['msk'])
        C.dma('sp', rst[:], self.inp['chunk_rst'][:, :], writes=['rst'])
        rwcv = rwc[:].rearrange("p (h q) -> p h q", q=5)
        C.op('dve', lambda e: e.tensor_scalar(out=oka[:], in0=rwcv[:, :, 1], scalar1=-1.0, scalar2=1.0, op0=ALU.mult, op1=ALU.add), reads=['rwc'], writes=['oka'])
        ones64 = self.ones[0:64, 0:64]
        id64 = self.ident[0:64, 0:64]
        Z = [self.sb(f'Z{h}', [64, 64], F32R) for h in range(16)]
        Zn = [0] * 16
        inp = Pool(self, 'rk_in', [64, 512], F32, 12)
        lop = Pool(self, 'rk_lo', [65, 512], F32, 4)
        tp = Pool(self, 'rk_t', [64, 512], F32, 16)
        arp = Pool(self, 'rk_ar', [64, 8 * 128], F32R, 4)
        bkp = Pool(self, 'rk_bk', [64, 8 * 128], F32R, 4)
        bhp = Pool(self, 'rk_bh', [64, 8 * 128], F32, 4)
        wcp = Pool(self, 'rk_wc', [64, 8], F32, 8)
        GRP = 4
        smr = [Pool(self, f'rk_sm{i}_', [64, 128], F32R, 7) for i in range(GRP)]
        fixb = []
        for i in range(GRP):
            fixb.append({'vbk': (self.sb(f'rk_vbk{i}', [64, 192], F32R), f'rk_vbk{i}'), 'NB': (self.sb(f'rk_NB{i}', [64, 128], F32R), f'rk_NB{i}'),
                         'NK': (self.sb(f'rk_NK{i}', [64, 128], F32R), f'rk_NK{i}'), 'A': (self.sb(f'rk_A{i}', [64, 64], F32R), f'rk_A{i}')})
        tfp = Pool(self, 'rk_tf', [64, 512], F32, 6)
        ysbp = Pool(self, 'rk_ys', [64, 512], F32, 6)
        xgp = Pool(self, 'rk_xg', [128, 512], F32, 2)
        xg2p = Pool(self, 'rk_xg2', [32, 512], F32, 2)
        outp = Pool(self, 'rk_o', [64, 512], BF16, 2)
        C.nrot = 1
        C.psn = 0
        C.hbanks = [1, 2, 3, 4, 5, 6, 7]

        def xcols(kind, c0):
            return (NE + c0 * 64) if kind == 'ctx' else c0 * 64

        def scols(kind, c0):
            return (T + c0 * 64) if kind == 'ctx' else c0 * 64

        def load_rows(dst, dk, i0, n, s0, rows16=16, base=0):
            for g in range(4):
                C.dma('sp', dst[base + g * rows16:base + (g + 1) * rows16, 0:n], rwT[g * 872 + i0:g * 872 + i0 + rows16, s0:s0 + n], writes=[dk])

        def lora_in(i0, n, s0, func):
            t, k = lop.get()
            load_rows(t, k, i0, n, s0)
            C.op('act', lambda e: e.activation(out=t[0:64, 0:n], in_=t[0:64, 0:n], func=func), reads=[k], writes=[k])
            C.op('dve', lambda e: e.memset(t[64:65, 0:n], 1.0), writes=[k])
            return t, k

        def ew(eng, fn, reads, n=None):
            t, k = tp.get()
            C.op(eng, lambda e: fn(e, t), reads=reads, writes=[k])
            return t, k

        for d in range(2):
            if d == 0:
                blocks = [('ctx', 0, 4, True)] + [('lat', c, 8, True) for c in (0, 8, 16, 24)] + [('lat', 32, 4, True)]
            else:
                blocks = [('ctx', 0, 4, True)] + [('lat', c, 8, False) for c in (56, 48, 40)] + [('lat', 36, 4, False), ('lat', 32, 4, True)] + \
                         [('lat', c, 8, True) for c in (24, 16, 8, 0)]
            for h in range(16):
                C.op('dve', lambda e, h=h: e.tensor_scalar(out=Z[h][:], in0=id64, scalar1=0.0, scalar2=None, op0=ALU.mult), reads=['ident'], writes=[f'Z{h}'])
            mS = msk[:, d * 192:d * 192 + 128]
            mST = msk[:, d * 192 + 128:d * 192 + 192]
            for (kind, c0, nch, outs) in blocks:
                n = nch * 64
                s0 = scols(kind, c0)
                x0 = xcols(kind, c0)
                xw, xwk = lora_in(768 + 16 * d, n, s0, AF.Tanh)
                xa, xak = lora_in(800 + 16 * d, n, s0, AF.Copy)
                if outs and d == 1:
                    xa0, xa0k = lora_in(800, n, s0, AF.Copy)
                    xg, xgk = xgp.get()
                    xg2, xg2k = xg2p.get()
                    for g in range(4):
                        C.dma('sp', xg[g * 32:(g + 1) * 32, 0:n], rwT[g * 872 + 832:g * 872 + 864, s0:s0 + n], writes=[xgk])
                        C.dma('sp', xg2[g * 8:(g + 1) * 8, 0:n], rwT[g * 872 + 864:g * 872 + 872, s0:s0 + n], writes=[xg2k])
                    C.op('act', lambda e: e.activation(out=xg[:, 0:n], in_=xg[:, 0:n], func=AF.Sigmoid), reads=[xgk], writes=[xgk])
                    C.op('act', lambda e: e.activation(out=xg2[:, 0:n], in_=xg2[:, 0:n], func=AF.Sigmoid), reads=[xg2k], writes=[xg2k])
                for g0 in range(0, 16, GRP):
                  HS = {}
                  for h in range(g0, g0 + GRP):
                    hc = slice(h * 64, (h + 1) * 64)
                    kkc, kac, rkc, lgc, lbc = [rwcv[:, h, q:q + 1] for q in range(5)]
                    kt, kk_ = inp.get(); load_rows(kt, kk_, 256 + h * 16, n, s0)
                    vt, vk_ = inp.get(); load_rows(vt, vk_, 512 + h * 16, n, s0)
                    if outs:
                        rt, rk_ = inp.get(); load_rows(rt, rk_, h * 16, n, s0)
                    pz, pzk = C.ps()
                    C.op('pe', lambda e: e.matmul(pz[0:64, 0:n], lhsT=lw_sb[0:65, d * 1024 + h * 64:d * 1024 + (h + 1) * 64], rhs=xw[0:65, 0:n], start=True, stop=True),
                         reads=['lora', xwk], writes=[pzk])
                    sg, sgk = ew('act', lambda e, t: e.activation(out=t[:, 0:n], in_=pz[0:64, 0:n], func=AF.Sigmoid), [pzk])
                    lw, lwk = ew('dve', lambda e, t: e.tensor_scalar(out=t[:, 0:n], in0=sg[:, 0:n], scalar1=-0.6065306597126334, scalar2=None, op0=ALU.mult), [sgk])
                    Pp, Ppk = ew('dve', lambda e, t: e.tensor_tensor_scan(out=t[:, 0:n], data0=rst[:, 0:n], data1=lw[:, 0:n], initial=0.0, op0=ALU.mult, op1=ALU.add), [lwk, 'rst'])
                    Ee, Eek = ew('dve', lambda e, t: e.tensor_tensor(out=t[:, 0:n], in0=Pp[:, 0:n], in1=lw[:, 0:n], op=ALU.subtract), [Ppk, lwk])
                    P3 = Pp[:, 0:n].rearrange("p (c t) -> p c t", t=64)
                    Qq, Qqk = ew('dve', lambda e, t: e.tensor_tensor(out=t[:, 0:n].rearrange("p (c t) -> p c t", t=64), in0=P3[:, :, 63:64].to_broadcast([64, nch, 64]), in1=P3, op=ALU.subtract), [Ppk])
                    if d == 0:
                        Lin, Link, Lex, Lexk, Lh, Lhk = Pp, Ppk, Ee, Eek, Qq, Qqk
                    else:
                        Lin, Link = ew('dve', lambda e, t: e.tensor_tensor(out=t[:, 0:n], in0=Qq[:, 0:n], in1=lw[:, 0:n], op=ALU.add), [Qqk, lwk])
                        Lex, Lexk, Lh, Lhk = Qq, Qqk, Ee, Eek
                    wc, wck = wcp.get()
                    C.op('act', lambda e: e.activation(out=wc[:, 0:nch], in_=P3[:, :, 63], func=AF.Exp), reads=[Ppk], writes=[wck])
                    eLex, eLexk = ew('act', lambda e, t: e.activation(out=t[:, 0:n], in_=Lex[:, 0:n], func=AF.Exp), [Lexk])
                    eNeg, eNegk = ew('act', lambda e, t: e.activation(out=t[:, 0:n], in_=Lin[:, 0:n], func=AF.Exp, scale=-1.0), [Link])
                    eH, eHk = ew('act', lambda e, t: e.activation(out=t[:, 0:n], in_=Lh[:, 0:n], func=AF.Exp), [Lhk])
                    kk, kkk = ew('dve', lambda e, t: e.tensor_scalar(out=t[:, 0:n], in0=kt[:, 0:n], scalar1=kkc, scalar2=None, op0=ALU.mult), [kk_, 'rwc'])
                    sq, sqk = ew('act', lambda e, t: e.activation(out=t[:, 0:n], in_=kk[:, 0:n], func=AF.Square), [kkk])
                    pss, pssk = C.ps()
                    C.op('pe', lambda e: e.matmul(pss[0:64, 0:n], lhsT=ones64, rhs=sq[:, 0:n], start=True, stop=True), reads=[sqk, 'ones'], writes=[pssk])
                    rn, rnk = ew('act', lambda e, t: e.activation(out=t[:, 0:n], in_=pss[0:64, 0:n], func=AF.Sqrt), [pssk])
                    C.op('dve', lambda e: e.tensor_scalar(out=rn[:, 0:n], in0=rn[:, 0:n], scalar1=1e-6, scalar2=None, op0=ALU.max), reads=[rnk], writes=[rnk])
                    C.op('dve', lambda e: e.reciprocal(out=rn[:, 0:n], in_=rn[:, 0:n]), reads=[rnk], writes=[rnk])
                    kkn, kknk = ew('dve', lambda e, t: e.tensor_tensor(out=t[:, 0:n], in0=kk[:, 0:n], in1=rn[:, 0:n], op=ALU.mult), [kkk, rnk])
                    pa, pak = C.ps()
                    C.op('pe', lambda e: e.matmul(pa[0:64, 0:n], lhsT=lw_sb[0:65, (2 + d) * 1024 + h * 64:(2 + d) * 1024 + (h + 1) * 64], rhs=xa[0:65, 0:n], start=True, stop=True),
                         reads=['lora', xak], writes=[pak])
                    ic, ick = ew('act', lambda e, t: e.activation(out=t[:, 0:n], in_=pa[0:64, 0:n], func=AF.Sigmoid), [pak])
                    bb, bbk = ew('dve', lambda e, t: e.tensor_tensor(out=t[:, 0:n], in0=kkn[:, 0:n], in1=ic[:, 0:n], op=ALU.mult), [kknk, ick])
                    tf, tfk = tfp.get()
                    C.op('dve', lambda e: e.tensor_scalar(out=tf[:, 0:n], in0=ic[:, 0:n], scalar1=kac, scalar2=oka[:, h:h + 1], op0=ALU.mult, op1=ALU.add), reads=[ick, 'rwc', 'oka'], writes=[tfk])
                    kd, kdk = ew('dve', lambda e, t: e.tensor_tensor(out=t[:, 0:n], in0=kt[:, 0:n], in1=tf[:, 0:n], op=ALU.mult), [kk_, tfk])
                    ar, ark = arp.get(); arv = ar[:, 0:nch * 128].rearrange("p (c x) -> p c x", x=128)
                    bk, bkk = bkp.get(); bkv = bk[:, 0:nch * 128].rearrange("p (c x) -> p c x", x=128)
                    bh, bhk = bhp.get(); bhv = bh[:, 0:nch * 128].rearrange("p (c x) -> p c x", x=128)
                    v3 = lambda t: t[:, 0:n].rearrange("p (c t) -> p c t", t=64)
                    C.op('dve', lambda e: e.scalar_tensor_tensor(out=arv[:, :, 0:64], in0=v3(kkn), scalar=-1.0, in1=v3(eLex), op0=ALU.mult, op1=ALU.mult), reads=[kknk, eLexk], writes=[ark])
                    if outs:
                        eLin, eLink = ew('act', lambda e, t: e.activation(out=t[:, 0:n], in_=Lin[:, 0:n], func=AF.Exp), [Link])
                        C.op('dve', lambda e: e.tensor_tensor(out=arv[:, :, 64:128], in0=v3(rt), in1=v3(eLin), op=ALU.mult), reads=[rk_, eLink], writes=[ark])
                    C.op('dve', lambda e: e.tensor_tensor(out=bkv[:, :, 0:64], in0=v3(bb), in1=v3(eNeg), op=ALU.mult), reads=[bbk, eNegk], writes=[bkk])
                    C.op('dve', lambda e: e.tensor_tensor(out=bkv[:, :, 64:128], in0=v3(kd), in1=v3(eNeg), op=ALU.mult), reads=[kdk, eNegk], writes=[bkk])
                    C.op('dve', lambda e: e.tensor_tensor(out=bhv[:, :, 0:64], in0=v3(bb), in1=v3(eH), op=ALU.mult), reads=[bbk, eHk], writes=[bhk])
                    C.op('dve', lambda e: e.tensor_tensor(out=bhv[:, :, 64:128], in0=v3(kd), in1=v3(eH), op=ALU.mult), reads=[kdk, eHk], writes=[bhk])
                    ysb_, ysbk_ = ysbp.get()
                    HS[h] = dict(kt=kt, kk_=kk_, vt=vt, vk_=vk_, rt=(rt if outs else None), rk_=(rk_ if outs else None), arv=arv, ark=ark, bkv=bkv, bkk=bkk,
                                 bhv=bhv, bhk=bhk, wc=wc, wck=wck, tf=tf, tfk=tfk, ysb=ysb_, ysbk=ysbk_)
                  def chunk_gen(h, c, slot):
                    Hh = HS[h]
                    vt, vk_, arv, ark, bkv, bkk, bhv, bhk, wc, wck = (Hh[k] for k in ('vt', 'vk_', 'arv', 'ark', 'bkv', 'bkk', 'bhv', 'bhk', 'wc', 'wck'))
                    sm = smr[slot]
                    nw = 128 if outs else 64
                    aT = arv[:, c, 0:64]
                    bT = bkv[:, c, 0:64]
                    kT = bkv[:, c, 64:128]
                    pt1, pt1k = C.psh()
                    C.op('pe', lambda e: e.transpose(pt1[0:64, 0:64], vt[:, c * 64:(c + 1) * 64], id64), reads=[vk_, 'ident'], writes=[pt1k])
                    C.op('pe', lambda e: e.transpose(pt1[0:64, 64:128], bhv[:, c, 0:64], id64), reads=[bhk, 'ident'], writes=[pt1k])
                    C.op('pe', lambda e: e.transpose(pt1[0:64, 128:192], bhv[:, c, 64:128], id64), reads=[bhk, 'ident'], writes=[pt1k])
                    vbk, vbkk = fixb[slot]['vbk']
                    C.op('act', lambda e: e.activation(out=vbk[:, 0:192], in_=pt1[0:64, 0:192], func=AF.Copy), reads=[pt1k], writes=[vbkk])
                    yield
                    Vt, Bh, Kh = vbk[:, 0:64], vbk[:, 64:128], vbk[:, 128:192]
                    p1, p1k = C.psh()
                    C.op('pe', lambda e: e.matmul(p1[0:64, 0:nw], lhsT=bT, rhs=arv[:, c, 0:nw], start=True, stop=True), reads=[bkk, ark], writes=[p1k])
                    NB, NBk = fixb[slot]['NB']
                    C.op('dve', lambda e: e.tensor_tensor(out=NB[:, 0:nw], in0=p1[0:64, 0:nw], in1=mS[:, 0:nw], op=ALU.mult), reads=[p1k, 'msk'], writes=[NBk])
                    p2, p2k = C.psh()
                    C.op('pe', lambda e: e.matmul(p2[0:64, 0:nw], lhsT=kT, rhs=arv[:, c, 0:nw], start=True, stop=True), reads=[bkk, ark], writes=[p2k])
                    NK, NKk = fixb[slot]['NK']
                    C.op('dve', lambda e: e.tensor_tensor(out=NK[:, 0:nw], in0=p2[0:64, 0:nw], in1=mS[:, 0:nw], op=ALU.mult), reads=[p2k, 'msk'], writes=[NKk])
                    p3, p3k = C.psh()
                    C.op('pe', lambda e: e.matmul(p3[0:64, 0:64], lhsT=aT, rhs=bT, start=True, stop=True), reads=[bkk, ark], writes=[p3k])
                    A, Ak = fixb[slot]['A']
                    C.op('dve', lambda e: e.tensor_tensor(out=A[:, 0:64], in0=p3[0:64, 0:64], in1=mST, op=ALU.mult), reads=[p3k, 'msk'], writes=[Ak])
                    yield
                    px, pxk = C.psh()
                    C.op('pe', lambda e: e.transpose(px[0:64, 0:64], aT.bitcast(F32), id64), reads=[ark, 'ident'], writes=[pxk])
                    C.op('pe', lambda e: e.matmul(px[0:64, 64:128], lhsT=NK[:, 0:64], rhs=Vt, start=True, stop=True), reads=[NKk, vbkk], writes=[pxk])
                    X, Xk = sm.get()
                    C.op('act', lambda e: e.activation(out=X[:, 0:128], in_=px[0:64, 0:128], func=AF.Copy), reads=[pxk], writes=[Xk])
                    yield
                    Nc, Nck, Ac, Ack = NB, NBk, A, Ak
                    for it in range(6):
                        pq, pqk = C.psh()
                        C.op('pe', lambda e: e.matmul(pq[0:64, 0:128], lhsT=Nc[:, 0:64], rhs=X[:, 0:128], start=True, stop=True), reads=[Nck, Xk], writes=[pqk])
                        X2, X2k = sm.get()
                        C.op('dve', lambda e: e.tensor_tensor(out=X2[:, 0:128], in0=pq[0:64, 0:128], in1=X[:, 0:128].bitcast(F32), op=ALU.add), reads=[pqk, Xk], writes=[X2k])
                        yield
                        X, Xk = X2, X2k
                        if it < 5:
                            pn, pnk = C.psh()
                            C.op('pe', lambda e: e.matmul(pn[0:64, 0:64], lhsT=Ac[:, 0:64], rhs=Nc[:, 0:64], start=True, stop=True), reads=[Ack, Nck], writes=[pnk])
                            C.op('pe', lambda e: e.matmul(pn[0:64, 64:128], lhsT=Nc[:, 0:64], rhs=Ac[:, 0:64], start=True, stop=True), reads=[Ack, Nck], writes=[pnk])
                            NA, NAk = sm.get()
                            C.op('act', lambda e: e.activation(out=NA[:, 0:128], in_=pn[0:64, 0:128], func=AF.Copy), reads=[pnk], writes=[NAk])
                            yield
                            Nc, Nck = NA[:, 0:64], NAk
                            Ac, Ack = NA[:, 64:128], NAk
                    Ap, U0 = X[:, 0:64], X[:, 64:128]
                    pm, pmk = C.psh()
                    C.op('pe', lambda e: e.matmul(pm[0:64, 0:64], lhsT=Ap, rhs=Bh, start=True, stop=True), reads=[Xk, vbkk], writes=[pmk])
                    C.op('pe', lambda e: e.matmul(pm[0:64, 64:128], lhsT=Bh, rhs=U0, start=True, stop=False), reads=[Xk, vbkk], writes=[pmk])
                    C.op('pe', lambda e: e.matmul(pm[0:64, 64:128], lhsT=Kh, rhs=Vt, start=False, stop=True), reads=[vbkk], writes=[pmk])
                    MS, MSk = sm.get()
                    C.op('dve', lambda e: e.scalar_tensor_tensor(out=MS[:, 0:64], in0=id64, scalar=wc[:, c:c + 1], in1=pm[0:64, 0:64], op0=ALU.mult, op1=ALU.add),
                         reads=[pmk, wck, 'ident'], writes=[MSk])
                    C.op('act', lambda e: e.activation(out=MS[:, 64:128], in_=pm[0:64, 64:128], func=AF.Copy), reads=[pmk], writes=[MSk])
                    yield
                    zk = f'Z{h}'
                    if outs:
                        pr, prk = C.psh()
                        C.op('pe', lambda e: e.matmul(pr[0:64, 0:64], lhsT=Ap, rhs=NB[:, 64:128], start=True, stop=True), reads=[Xk, NBk], writes=[prk])
                        Rp, Rpk = sm.get()
                        C.op('dve', lambda e: e.tensor_tensor(out=Rp[:, 0:64], in0=pr[0:64, 0:64], in1=arv[:, c, 64:128].bitcast(F32), op=ALU.add), reads=[prk, ark], writes=[Rpk])
                        yield
                        pyc, pyk = C.psh()
                        yc = pyc[0:64, 0:64]
                        C.op('pe', lambda e: e.matmul(yc, lhsT=Z[h][:], rhs=Rp[:, 0:64], start=True, stop=False), reads=[zk, Rpk], writes=[pyk])
                        C.op('pe', lambda e: e.matmul(yc, lhsT=U0, rhs=NB[:, 64:128], start=False, stop=False), reads=[Xk, NBk], writes=[pyk])
                        C.op('pe', lambda e: e.matmul(yc, lhsT=Vt, rhs=NK[:, 64:128], start=False, stop=True), reads=[vbkk, NKk], writes=[pyk])
                        C.op('act', lambda e: e.activation(out=Hh['ysb'][:, c * 64:(c + 1) * 64], in_=yc, func=AF.Copy), reads=[pyk], writes=[Hh['ysbk']])
                        yield
                    pzz, pzzk = C.psh()
                    C.op('pe', lambda e: e.matmul(pzz[0:64, 0:64], lhsT=MS[:, 0:64], rhs=Z[h][:], start=True, stop=True), reads=[MSk, zk], writes=[pzzk])
                    C.op('dve', lambda e: e.tensor_tensor(out=Z[h][:], in0=pzz[0:64, 0:64], in1=MS[:, 64:128].bitcast(F32), op=ALU.add), reads=[pzzk, MSk], writes=[zk])
                    yield

                  order = range(nch) if d == 0 else range(nch - 1, -1, -1)
                  for c in order:
                      active = [chunk_gen(h, c, h - g0) for h in range(g0, g0 + GRP)]
                      while active:
                          for gnr in list(active):
                              try:
                                  next(gnr)
                              except StopIteration:
                                  active.remove(gnr)
                  for h in range(g0, g0 + GRP):
                    Hh = HS[h]
                    hc = slice(h * 64, (h + 1) * 64)
                    kkc, kac, rkc, lgc, lbc = [rwcv[:, h, q:q + 1] for q in range(5)]
                    kt, kk_, vt, vk_, rt, rk_, tf, tfk = (Hh[k] for k in ('kt', 'kk_', 'vt', 'vk_', 'rt', 'rk_', 'tf', 'tfk'))
                    pyv, pyk = Hh['ysb'], Hh['ysbk']
                    if not outs:
                        continue
                    if d == 0:
                        ysb, ysk = ew('act', lambda e, t: e.activation(out=t[:, 0:n], in_=pyv[:, 0:n], func=AF.Copy), [pyk])
                        C.dma('sp', yfT[hc, x0:x0 + n], ysb[:, 0:n], reads=[ysk], writes=['yfT'])
                        continue
                    yf, yfk = tp.get()
                    C.dma('sp', yf[:, 0:n], yfT[hc, x0:x0 + n], reads=['yfT'], writes=[yfk])
                    ys, ysk = ew('dve', lambda e, t: e.tensor_tensor(out=t[:, 0:n], in0=pyv[:, 0:n], in1=yf[:, 0:n], op=ALU.add), [pyk, yfk])
                    pmn, pmnk = C.ps()
                    C.op('pe', lambda e: e.matmul(pmn[0:64, 0:n], lhsT=ones64, rhs=ys[:, 0:n], start=True, stop=True), reads=[ysk, 'ones'], writes=[pmnk])
                    ycn, ycnk = ew('dve', lambda e, t: e.scalar_tensor_tensor(out=t[:, 0:n], in0=pmn[0:64, 0:n], scalar=-1.0 / 64, in1=ys[:, 0:n], op0=ALU.mult, op1=ALU.add), [pmnk, ysk])
                    sq2, sq2k = ew('act', lambda e, t: e.activation(out=t[:, 0:n], in_=ycn[:, 0:n], func=AF.Square), [ycnk])
                    pvr, pvrk = C.ps()
                    C.op('pe', lambda e: e.matmul(pvr[0:64, 0:n], lhsT=ones64, rhs=sq2[:, 0:n], start=True, stop=True), reads=[sq2k, 'ones'], writes=[pvrk])
                    rsd, rsdk = ew('act', lambda e, t: e.activation(out=t[:, 0:n], in_=pvr[0:64, 0:n], func=AF.Sqrt, bias=64e-5, scale=1.0 / 64), [pvrk])
                    C.op('dve', lambda e: e.reciprocal(out=rsd[:, 0:n], in_=rsd[:, 0:n]), reads=[rsdk], writes=[rsdk])
                    yn, ynk = ew('dve', lambda e, t: e.tensor_tensor(out=t[:, 0:n], in0=ycn[:, 0:n], in1=rsd[:, 0:n], op=ALU.mult), [ycnk, rsdk])
                    o1, o1k = ew('act', lambda e, t: e.activation(out=t[:, 0:n], in_=yn[:, 0:n], func=AF.Identity, bias=lbc, scale=lgc), [ynk, 'rwc'])
                    pa0, pa0k = C.ps()
                    C.op('pe', lambda e: e.matmul(pa0[0:64, 0:n], lhsT=lw_sb[0:65, 2 * 1024 + h * 64:2 * 1024 + (h + 1) * 64], rhs=xa0[0:65, 0:n], start=True, stop=True),
                         reads=['lora', xa0k], writes=[pa0k])
                    ic0, ic0k = ew('act', lambda e, t: e.activation(out=t[:, 0:n], in_=pa0[0:64, 0:n], func=AF.Sigmoid), [pa0k])
                    C.op('dve', lambda e: e.tensor_scalar(out=ic0[:, 0:n], in0=ic0[:, 0:n], scalar1=kac, scalar2=oka[:, h:h + 1], op0=ALU.mult, op1=ALU.add), reads=[ic0k, 'rwc', 'oka'], writes=[ic0k])
                    C.op('dve', lambda e: e.tensor_tensor(out=ic0[:, 0:n], in0=ic0[:, 0:n], in1=tf[:, 0:n], op=ALU.add), reads=[ic0k, tfk], writes=[ic0k])
                    C.op('dve', lambda e: e.tensor_tensor(out=ic0[:, 0:n], in0=ic0[:, 0:n], in1=kt[:, 0:n], op=ALU.mult), reads=[ic0k, kk_], writes=[ic0k])
                    rk2, rk2k = ew('dve', lambda e, t: e.scalar_tensor_tensor(out=t[:, 0:n], in0=rt[:, 0:n], scalar=rkc, in1=ic0[:, 0:n], op0=ALU.mult, op1=ALU.mult), [rk_, ic0k, 'rwc'])
                    pb, pbk = C.ps()
                    C.op('pe', lambda e: e.matmul(pb[0:64, 0:n], lhsT=ones64, rhs=rk2[:, 0:n], start=True, stop=True), reads=[rk2k, 'ones'], writes=[pbk])
                    bon, bonk = ew('dve', lambda e, t: e.tensor_tensor(out=t[:, 0:n], in0=pb[0:64, 0:n], in1=vt[:, 0:n], op=ALU.mult), [pbk, vk_])
                    C.op('dve', lambda e: e.tensor_tensor(out=o1[:, 0:n], in0=o1[:, 0:n], in1=bon[:, 0:n], op=ALU.add), reads=[o1k, bonk], writes=[o1k])
                    pg, pgk = C.ps()
                    C.op('pe', lambda e: e.matmul(pg[0:64, 0:n], lhsT=wg_a[:, hc], rhs=xg[:, 0:n], start=True, stop=False), reads=['wg', xgk], writes=[pgk])
                    C.op('pe', lambda e: e.matmul(pg[0:64, 0:n], lhsT=wg_b[:, hc], rhs=xg2[:, 0:n], start=False, stop=True), reads=['wg', xg2k], writes=[pgk])
                    ob, obk = outp.get()
                    C.op('dve', lambda e: e.tensor_tensor(out=ob[:, 0:n], in0=pg[0:64, 0:n], in1=o1[:, 0:n], op=ALU.mult), reads=[pgk, o1k], writes=[obk])
                    C.dma('sp', attnT[1024 + h * 64:1024 + (h + 1) * 64, x0:x0 + n], ob[:, 0:n], reads=[obk], writes=['attnT'])
        C.nrot = 8

    def phase_mlp(self, L, tiles, attn, xsrc, out_fn, final=False):
        C = self.C
        Wo = self.din(f'w_out{L}', [D, D]).rearrange("(k p) c -> p k c", p=128)
        W1 = self.din(f'mlp_w1_{L}', [D, 4 * D]).rearrange("(k p) c -> p k c", p=128)
        W2 = self.din(f'mlp_w2_{L}', [4 * D, D]).rearrange("(k p) c -> p k c", p=128)
        self.wload_init(256, 2, 3)
        xp = Pool(self, 'm_x', [128, KC * 512], F32, 1)
        ap_ = Pool(self, 'm_a', [128, KC * 512], BF16, 1)
        hp = Pool(self, 'm_h', [128, KC * 512], BF16, 1)
        hid = self.sb('m_hid', [128, 64 * 512], BF16)
        sqp = Pool(self, 'm_sq', [128, 512], F32, 3)
        tmpp = Pool(self, 'm_tm', [128, 512], F32, 3)
        ofp = Pool(self, 'm_of', [128, 512], F32, 2) if final else None
        av_src = attn.rearrange("(k p) t -> p k t", p=128)
        xv_src = xsrc.rearrange("(k p) t -> p k t", p=128)
        wl = []
        for _ in tiles:
            wl += [(Wo, c0, 256, KC, 0) for c0 in range(0, D, 256)]
            wl += [(W1, c0, 256, KC, 0) for c0 in range(0, 4 * D, 256)]
            wl += [(W2, c0, 256, 16, k0) for c0 in range(0, D, 256) for k0 in range(0, 64, 16)]
        self.wstream(wl, 1)
        for (a0, n, v, xs0) in tiles:
            xt, xk = xp.get()
            xv = xt[:].rearrange("p (k t) -> p k t", k=KC)[:, :, 0:n]
            at, ak = ap_.get()
            av = at[:].rearrange("p (k t) -> p k t", k=KC)[:, :, 0:n]
            for kq in range(4):
                C.dma('sp', xv[:, kq * 4:(kq + 1) * 4, :], xv_src[:, kq * 4:(kq + 1) * 4, xs0:xs0 + n], reads=['resT'], writes=[xk])
                C.dma('sp', av[:, kq * 4:(kq + 1) * 4, :], av_src[:, kq * 4:(kq + 1) * 4, a0:a0 + n], reads=['attnT'], writes=[ak])
            for c0 in range(0, D, 256):
                wv, wk = self.wnext()
                for m0 in (0, 128):
                    dt = (c0 + m0) // 128
                    pt, pk = C.ps()
                    for kc in range(KC):
                        C.op('pe', lambda e, kc=kc: e.matmul(pt[:, 0:n], lhsT=wv[:, kc, m0:m0 + 128], rhs=av[:, kc, :], start=(kc == 0), stop=(kc == KC - 1)),
                             reads=[wk, ak], writes=[pk], signal=(kc == KC - 1))
                    C.op('dve', lambda e: e.scalar_tensor_tensor(out=xv[:, dt, :], in0=pt[:, 0:n], scalar=self.mcol(L, 2, dt, v), in1=xv[:, dt, :], op0=ALU.mult, op1=ALU.add),
                         reads=[pk, xk, 'mod'], writes=[xk])
            if f'xmid{L}' in self.debug:
                o = self.outs.get(f'dbg_xmid{L}') or self.dout(f'dbg_xmid{L}', [D, NX])
                ov = o.rearrange("(k p) t -> p k t", p=128)
                for kq in range(4):
                    C.dma('sp', ov[:, kq * 4:(kq + 1) * 4, a0:a0 + n], xv[:, kq * 4:(kq + 1) * 4, :], reads=[xk])
            ht, hk = hp.get()
            hv = ht[:].rearrange("p (k t) -> p k t", k=KC)[:, :, 0:n]
            self.norm_mod(xv, xk, n, hv, hk, lambda kc: self.Av[:, L, 1, kc, v:v + 1], lambda kc: self.mcol(L, 3, kc, v), tmpp, sqp)
            hidv = hid[:].rearrange("p (k t) -> p k t", k=64)[:, :, 0:n]
            for c0 in range(0, 4 * D, 256):
                wv, wk = self.wnext()
                for m0 in (0, 128):
                    ht_i = (c0 + m0) // 128
                    pt, pk = C.ps()
                    for kc in range(KC):
                        C.op('pe', lambda e, kc=kc: e.matmul(pt[:, 0:n], lhsT=wv[:, kc, m0:m0 + 128], rhs=hv[:, kc, :], start=(kc == 0), stop=(kc == KC - 1)),
                             reads=[wk, hk], writes=[pk], signal=(kc == KC - 1))
                    tm, tmk = tmpp.get()
                    C.op('act', lambda e: e.activation(out=tm[:, 0:n], in_=pt[:, 0:n], func=AF.Relu), reads=[pk], writes=[tmk])
                    eng = C.pick('m_sq', ['pool', 'dve'])
                    C.op(eng, lambda e: e.tensor_tensor(out=hidv[:, ht_i, :], in0=tm[:, 0:n], in1=tm[:, 0:n], op=ALU.mult), reads=[tmk], writes=['hid'])
            for c0 in range(0, D, 256):
                pts = [C.ps(), C.ps()]
                for k0 in range(0, 64, 16):
                    wv, wk = self.wnext()
                    for mi, m0 in enumerate((0, 128)):
                        pt, pk = pts[mi]
                        for kc in range(16):
                            C.op('pe', lambda e, kc=kc: e.matmul(pt[:, 0:n], lhsT=wv[:, kc, m0:m0 + 128], rhs=hidv[:, k0 + kc, :],
                                                                start=(k0 + kc == 0), stop=(k0 + kc == 63)),
                                 reads=[wk, 'hid'], writes=[pk], signal=(kc == 15))
                for mi, m0 in enumerate((0, 128)):
                    dt = (c0 + m0) // 128
                    pt, pk = pts[mi]
                    C.op('dve', lambda e: e.scalar_tensor_tensor(out=xv[:, dt, :], in0=pt[:, 0:n], scalar=self.mcol(L, 5, dt, v), in1=xv[:, dt, :], op0=ALU.mult, op1=ALU.add),
                         reads=[pk, xk, 'mod'], writes=[xk])
            if not final:
                out_fn(xv, xk, a0, n)
            else:
                pt, pk = C.ps()
                for kc in range(KC):
                    sq, sqk = sqp.get()
                    C.op('act', lambda e, kc=kc: e.activation(out=sq[:, 0:n], in_=xv[:, kc, :], func=AF.Square), reads=[xk], writes=[sqk])
                    C.op('pe', lambda e, kc=kc: e.matmul(pt[:, 0:n], lhsT=self.ones[:], rhs=sq[:, 0:n], start=(kc == 0), stop=(kc == KC - 1)), reads=[sqk, 'ones'], writes=[pk])
                rs, rsk = sqp.get()
                C.op('act', lambda e: e.activation(out=rs[:, 0:n], in_=pt[:, 0:n], func=AF.Sqrt, bias=EPS, scale=1.0 / D), reads=[pk], writes=[rsk])
                C.op('dve', lambda e: e.reciprocal(out=rs[:, 0:n], in_=rs[:, 0:n]), reads=[rsk], writes=[rsk])
                for kc in range(KC):
                    of, ofk = ofp.get()
                    C.op('dve', lambda e, kc=kc: e.scalar_tensor_tensor(out=of[:, 0:n], in0=xv[:, kc, :], scalar=self.nrm[:, 64 + kc:64 + kc + 1], in1=rs[:, 0:n], op0=ALU.mult, op1=ALU.mult),
                         reads=[xk, rsk, 'nrm'], writes=[ofk])
                    out_fn(of, ofk, kc, a0, n)

    def phase_proj1(self):
        C = self.C
        resT = self.scr['resT']
        xsrc = resT.rearrange("(k p) t -> p k t", p=128)
        W = self.din('w_qkv', [D, 3 * D]).rearrange("(k p) c -> p k c", p=128)
        q1T = self.dscr('q1T', [D, NOWN], BF16)
        k1T = self.dscr('k1T', [D, NX], BF16)
        v1 = self.dscr('v1', [NX, D], BF16)
        self.wload_init(256, 2, 3)
        xp = Pool(self, 'q_x', [128, KC * 512], F32, 2)
        hp = Pool(self, 'q_h', [128, KC * 512], BF16, 2)
        sqp = Pool(self, 'q_sq', [128, 512], F32, 3)
        tmpp = Pool(self, 'q_tm', [128, 512], F32, 3)
        evp = Pool(self, 'q_ev', [128, 512], BF16, 4)
        tiles = [(i * 512, 512, 0, True) for i in range(4)] + [(2048, 256, 0, False), (NE, CT, 1, False)]
        wl = []
        for (t0, n, v, own) in tiles:
            wl += [(W, c0, 256, KC, 0) for c0 in range(0 if own else D, 3 * D, 256)]
        self.wstream(wl, 1)
        for (t0, n, v, own) in tiles:
            xt, xk = xp.get()
            xv = xt[:].rearrange("p (k t) -> p k t", k=KC)[:, :, 0:n]
            for kq in range(4):
                C.dma('sp', xv[:, kq * 4:(kq + 1) * 4, :], xsrc[:, kq * 4:(kq + 1) * 4, t0:t0 + n], reads=['resT'], writes=[xk])
            ht, hk = hp.get()
            hv = ht[:].rearrange("p (k t) -> p k t", k=KC)[:, :, 0:n]
            self.norm_mod(xv, xk, n, hv, hk, lambda kc: self.Av[:, 1, 0, kc, v:v + 1], lambda kc: self.mcol(1, 0, kc, v), tmpp, sqp)
            for c0 in range(0 if own else D, 2 * D, 256):
                wv, wk = self.wnext()
                for m0 in (0, 128):
                    pt, pk = C.ps()
                    for kc in range(KC):
                        C.op('pe', lambda e, kc=kc: e.matmul(pt[:, 0:n], lhsT=wv[:, kc, m0:m0 + 128], rhs=hv[:, kc, :], start=(kc == 0), stop=(kc == KC - 1)),
                             reads=[wk, hk], writes=[pk], signal=(kc == KC - 1))
                    ev, evk = evp.get()
                    isq = c0 < D
                    C.op('act', lambda e: e.activation(out=ev[:, 0:n], in_=pt[:, 0:n], func=AF.Copy, scale=(0.125 if isq else 1.0)), reads=[pk], writes=[evk])
                    cc = c0 + m0
                    if isq:
                        C.dma('sp', q1T[cc:cc + 128, t0:t0 + n], ev[:, 0:n], reads=[evk], writes=['q1T'])
                    else:
                        C.dma('sp', k1T[cc - D:cc - D + 128, t0:t0 + n], ev[:, 0:n], reads=[evk], writes=['k1T'])
            for c0 in range(2 * D, 3 * D, 256):
                wv, wk = self.wnext()
                for s0 in range(0, n, 128):
                    pt, pk = C.ps()
                    for kc in range(KC):
                        C.op('pe', lambda e, kc=kc: e.matmul(pt[:, 0:256], lhsT=hv[:, kc, s0:s0 + 128], rhs=wv[:, kc, :], start=(kc == 0), stop=(kc == KC - 1)),
                             reads=[wk, hk], writes=[pk], signal=(kc == KC - 1))
                    ev, evk = evp.get()
                    C.op('act', lambda e: e.activation(out=ev[:, 0:256], in_=pt[:, 0:256], func=AF.Copy), reads=[pk], writes=[evk])
                    C.dma('sp', v1[t0 + s0:t0 + s0 + 128, c0 - 2 * D:c0 - 2 * D + 256], ev[:, 0:256], reads=[evk], writes=['v1'])

    def phase_na(self):
        C = self.C
        q1T, k1T, v1 = self.scr['q1T'], self.scr['k1T'], self.scr['v1']
        a1T = self.dscr('attn1T', [D, NOWN], BF16)
        nab = self.din('na_bias', [32, 128, 15 * 128])
        qp = Pool(self, 'n_q', [128, NOWN], BF16, 2)
        kp = Pool(self, 'n_k', [128, NX], BF16, 2)
        vsp = Pool(self, 'n_vs', [128, 20 * 128], BF16, 2)
        vap = Pool(self, 'n_va', [128, 20 * 2 * 128], BF16, 2)
        bp = Pool(self, 'n_b', [128, 15 * 128], F32, 2)
        sbp = Pool(self, 'n_sb', [128, 128], F32, 6)
        ptp = Pool(self, 'n_pt', [128, 128], BF16, 8)
        osp = Pool(self, 'n_os', [128, 512], F32, 2)
        rsp = Pool(self, 'n_rs', [64, 512], F32, 2)
        onp = Pool(self, 'n_on', [128, 512], BF16, 2)
        C.nrot = 6
        acc_i = 0
        vsrc = v1.rearrange("(c p) x -> p c x", p=128)
        for tp_ in range(16):
            qt, qk = qp.get()
            kt, kk = kp.get()
            C.dma('sp', qt[:], q1T[tp_ * 128:(tp_ + 1) * 128, :], reads=['q1T'], writes=[qk])
            C.dma('sp', kt[:], k1T[tp_ * 128:(tp_ + 1) * 128, :], reads=['k1T'], writes=[kk])
            vs, vsk = vsp.get()
            vsv = vs[:].rearrange("p (c x) -> p c x", c=20)
            for c4 in range(0, 20, 5):
                C.dma('sp', vsv[:, c4:c4 + 5, :], vsrc[:, c4:c4 + 5, tp_ * 128:(tp_ + 1) * 128], reads=['v1'], writes=[vsk])
            va, vak = vap.get()
            vav = va[:].rearrange("p (c h d) -> p c h d", c=20, h=2)
            C.op('dve', lambda e: e.memset(va[:], 1.0), writes=[vak])
            for hh in range(2):
                C.op('dve', lambda e, hh=hh: e.tensor_copy(out=vav[:, :, hh, 0:64], in_=vsv[:, :, hh * 64:(hh + 1) * 64]), reads=[vsk], writes=[vak])
            for hh in range(2):
                h = tp_ * 2 + hh
                base = hh * 64
                bt, bk = bp.get()
                for q5 in range(0, 15, 5):
                    C.dma('sp', bt[:, q5 * 128:(q5 + 5) * 128], nab[h, :, q5 * 128:(q5 + 5) * 128], writes=[bk])
                for qg in range(4):
                    po = C.ps_tiles[6 + acc_i % 2]
                    pok = f'ps{6 + acc_i % 2}'
                    acc_i += 1
                    pend = []

                    def pv(item):
                        pb, pbk, kc, ci, qi, nchk = item
                        C.op('pe', lambda e: e.matmul(po[:, qi * 128:(qi + 1) * 128], lhsT=vav[:, kc, hh, :], rhs=pb[:], start=(ci == 0), stop=(ci == nchk - 1)),
                             reads=[vak, pbk], writes=[pok])
                    for qi in range(4):
                        qb = qg * 4 + qi
                        cls = min(qb, 2)
                        cs = max(qb - 2, 0)
                        chunks = [(cs + j, cls * 5 + j) for j in range(5)] + [(18, None), (19, None)]
                        for ci, (kc, var) in enumerate(chunks):
                            pt, pk = C.ps()
                            C.op('pe', lambda e: e.matmul(pt[:, 0:128], lhsT=kt[base:base + 64, kc * 128:(kc + 1) * 128], rhs=qt[base:base + 64, qb * 128:(qb + 1) * 128], start=True, stop=True),
                                 reads=[kk, qk], writes=[pk])
                            pb, pbk = ptp.get()
                            if var is not None:
                                sb_, sbk = sbp.get()
                                C.op('dve', lambda e: e.tensor_tensor(out=sb_[:], in0=pt[:, 0:128], in1=bt[:, var * 128:(var + 1) * 128], op=ALU.add), reads=[pk, bk], writes=[sbk])
                                C.op('act', lambda e: e.activation(out=pb[:], in_=sb_[:], func=AF.Exp), reads=[sbk], writes=[pbk])
                            else:
                                C.op('act', lambda e: e.activation(out=pb[:], in_=pt[:, 0:128], func=AF.Exp), reads=[pk], writes=[pbk])
                            pend.append((pb, pbk, kc, ci, qi, len(chunks)))
                            if len(pend) > 3:
                                pv(pend.pop(0))
                    while pend:
                        pv(pend.pop(0))
                    osb, osk = osp.get()
                    C.op('dve', lambda e: e.tensor_copy(out=osb[:], in_=po[:, 0:512]), reads=[pok], writes=[osk])
                    rs, rsk = rsp.get()
                    C.dma('sp', rs[:], osb[64:128, :], reads=[osk], writes=[rsk])
                    C.op('dve', lambda e: e.reciprocal(out=rs[:], in_=rs[:]), reads=[rsk], writes=[rsk])
                    on, onk = onp.get()
                    C.op('dve', lambda e: e.tensor_tensor(out=on[0:64, :], in0=osb[0:64, :], in1=rs[:], op=ALU.mult), reads=[osk, rsk], writes=[onk])
                    C.dma('sp', a1T[h * 64:(h + 1) * 64, qg * 512:(qg + 1) * 512], on[0:64, :], reads=[onk], writes=['attn1T'])
        C.nrot = 8

    def _dbg_h(self, hv, hk, t0, n):
        C = self.C
        if 'h0' not in self.outs:
            self.dout('h0', [D, S0], BF16)
        o = self.outs['h0'].rearrange("(k p) t -> p k t", p=128)
        for kq in range(4):
            C.dma('sp', o[:, kq * 4:(kq + 1) * 4, t0:t0 + n], hv[:, kq * 4:(kq + 1) * 4, :], reads=[hk])

    def dbg_copy_scr(self, name, src, rows, cols, dt=F32):
        C = self.C
        o = self.dout('dbg_' + name, [rows, cols], dt)
        for r0 in range(0, rows, 128):
            r1 = min(rows, r0 + 128)
            C.dma('sp', o[r0:r1, :], src[r0:r1, :], reads=[name])


def build(debug=(), upto=99):
    P = Prog(debug)
    C = P.C
    P.load_consts()
    P.phase_mod()
    if 'mod' in P.debug:
        o = P.dout('dbg_mod', [128, 384])
        C.dma('sp', o[:, :], P.mod[:], reads=['mod'])
    if upto >= 1:
        P.begin_phase()
        P.phase_proj0()
        if 'pT' in P.debug:
            C.barrier()
            P.dbg_copy_scr('pT', P.scr['pT'], NCOL0, S0)
            P.dbg_copy_scr('vtok', P.scr['vtok'], S0, 256, BF16)
    if upto >= 2:
        P.begin_phase()
        P.phase_gqa()
    if upto >= 3:
        P.begin_phase()
        P.phase_rwkv_shift()
        if 'rwT' in P.debug:
            C.barrier()
            P.dbg_copy_scr('rwT', P.scr['rwT'], RW, S0)
    if upto >= 4:
        P.begin_phase()
        P.phase_rwkv_scan()
    if upto >= 2 and 'attnT' in P.debug:
        C.barrier()
        P.dbg_copy_scr('attnT', P.scr['attnT'], D if upto >= 4 else 1024, NX, BF16)
    if upto >= 5:
        P.begin_phase()
        resT = P.dscr('resT', [D, NX])
        rv = resT.rearrange("(k p) t -> p k t", p=128)
        tiles0 = [(i * 512, 512, 0, i * 512) for i in range(4)] + [(2048, 256, 0, 2048), (NE, CT, 1, T)]

        def out0(xv, xk, a0, n):
            for kq in range(4):
                C.dma('sp', rv[:, kq * 4:(kq + 1) * 4, a0:a0 + n], xv[:, kq * 4:(kq + 1) * 4, :], reads=[xk], writes=['resT'])
        P.phase_mlp(0, tiles0, P.scr['attnT'], P.inp['xT'], out0)
        if 'resT' in P.debug:
            C.barrier()
            P.dbg_copy_scr('resT', resT, D, NX)
    if upto >= 6:
        P.begin_phase()
        P.phase_proj1()
    if upto >= 7:
        P.begin_phase()
        P.phase_na()
        if 'attn1T' in P.debug:
            C.barrier()
            P.dbg_copy_scr('attn1T', P.scr['attn1T'], D, NOWN, BF16)
    if upto >= 8:
        P.begin_phase()
        outT = P.dout('outT', [D, NOWN])
        tiles1 = [(i * 512, 512, 0, i * 512) for i in range(4)]

        def out1(of, ofk, kc, a0, n):
            C.dma('sp', outT[kc * 128:(kc + 1) * 128, a0:a0 + n], of[:, 0:n], reads=[ofk], writes=['outT'])
        P.phase_mlp(1, tiles1, P.scr['attn1T'], P.scr['resT'], out1, final=True)
    C.finish('sp')
    return P


def pk(v):
    v = np.asarray(v, np.float32)
    return np.ascontiguousarray(v.reshape(-1, 128).T)


def rw_perm(flip):
    grp = [1, 0, 3, 2] if flip else [0, 1, 2, 3]
    ii = np.arange(872)
    if flip:
        ii = np.concatenate([ii[:768], ii[784:800], ii[768:784], ii[816:832], ii[800:816], ii[832:]])
    return np.concatenate([4 * ii + g for g in grp])


def head_perm(flip, n=16):
    grp = [1, 0, 3, 2] if flip else [0, 1, 2, 3]
    return np.concatenate([4 * np.arange(n) + g for g in grp])


def na_bias_tables(rpb, flip):
    out = np.full((32, 128, 15, 128), -30000.0, np.float32)
    kc = np.arange(64)
    qc = np.arange(64)
    for cls, qb in enumerate((0, 1, 4)):
        cs = max(2 * qb - 4, 0)
        for j in range(5):
            var = cls * 5 + j
            for a in range(2):
                for m_ in range(2):
                    kr = cs + 2 * j + a
                    qr = 2 * qb + m_
                    if flip:
                        kro, qro = 63 - kr, 63 - qr
                        kco, qco = 63 - kc, 63 - qc
                    else:
                        kro, qro, kco, qco = kr, qr, kc, qc
                    rs = min(max(qro - 4, 0), 56)
                    if not (rs <= kro < rs + 8):
                        continue
                    cstart = np.clip(qco - 8, 0, 48)
                    valid = (kco[:, None] >= cstart[None, :]) & (kco[:, None] < cstart[None, :] + 16)
                    dcol = kco[:, None] - qco[None, :] + 15
                    drow = kro - qro + 7
                    vals = rpb[:, drow, :][:, np.clip(dcol, 0, 30)]
                    blk = np.where(valid[None], vals, np.float32(-30000.0))
                    out[:, a * 64:(a + 1) * 64, var, m_ * 64:(m_ + 1) * 64] = blk
    return np.ascontiguousarray(out.reshape(32, 128, 15 * 128))


_SHARED = {}


def host_inputs(I, b, half):
    flip = (half == 1)
    m = {}
    x = I['x'][b]
    cx = I['ctx'][b]
    if flip:
        x = x[::-1]
        cx = cx[::-1]
    m['xT'] = np.ascontiguousarray(np.concatenate([x, cx], 0).T)
    cc = np.stack([pk(I['c'][b]), pk(I['c_ctx'])], -1).reshape(128, 32)
    m['ccol'] = np.ascontiguousarray(cc)
    key = ('shared', flip)
    if key in _SHARED:
        m.update(_SHARED[key])
        return m
    sh = {}
    sh['ident'] = np.eye(128, dtype=np.float32)
    ob = np.zeros((128, 128), np.float32)
    ob[:64, :64] = 1
    ob[64:, 64:] = 1
    sh['ones_blk'] = ob
    sh['ada_b'] = np.ascontiguousarray(np.concatenate([pk(I['l0_ada_b']), pk(I['l1_ada_b'])], 1))
    sh['nrm'] = np.ascontiguousarray(np.concatenate([pk(I[k]) for k in ('l0_norm1', 'l0_norm2', 'l1_norm1', 'l1_norm2', 'final_norm')], 1))
    sh['ada_w0'] = I['l0_ada_w']
    sh['ada_w1'] = I['l1_ada_w']
    w_in = I['l0_w_in']
    rwp = rw_perm(flip)
    perm = np.concatenate([np.arange(GQ), GQ + rwp])
    sh['w_in'] = np.ascontiguousarray(w_in[:, perm])
    qn = np.tile(I['l0_q_norm'], 2) * np.float32(64 ** -0.5)
    kn = np.tile(I['l0_k_norm'], 2)
    sh['qkn'] = np.ascontiguousarray(np.stack([qn, kn], 1).astype(np.float32))
    rot = np.zeros((128, 128), np.float32)
    for mm in range(128):
        if mm % 64 < 32:
            rot[mm + 32, mm] = -1.0
        else:
            rot[mm - 32, mm] = 1.0
    sh['rotm'] = rot
    tt = np.arange(T)
    if flip:
        tt = tt[::-1]
    row = (tt // 64).astype(np.float32)
    col = (tt % 64).astype(np.float32)
    inv = (np.float32(10000.0) ** (-np.arange(16, dtype=np.float32) / np.float32(16))).astype(np.float32)
    ang = np.concatenate([row[:, None] * inv, col[:, None] * inv], -1).astype(np.float32)
    cs = np.cos(ang).astype(np.float32)
    sn = np.sin(ang).astype(np.float32)
    idx = np.arange(128) % 32
    sh['rope_cos'] = np.ascontiguousarray(cs[:, idx].T)
    sh['rope_sin'] = np.ascontiguousarray(sn[:, idx].T)
    mu = I['l0_shift_mu'][rwp]
    mc = np.zeros((128, 28), np.float32)
    for g in range(4):
        for j in range(7):
            rows = min(128, 872 - j * 128)
            mc[:rows, g * 7 + j] = mu[g * 872 + j * 128:g * 872 + j * 128 + rows]
    sh['mu_col'] = mc
    hp = head_perm(flip)
    chan = (np.arange(16)[:, None] * 64 + hp[None, :]).reshape(-1)
    rwc = np.zeros((64, 16, 5), np.float32)
    for q, nm in enumerate(('l0_k_k', 'l0_k_a', None, 'l0_lnx_g', 'l0_lnx_b')):
        vec = I['l0_r_k'].reshape(-1) if nm is None else I[nm]
        rwc[:, :, q] = vec[chan].reshape(16, 64).T
    sh['rw_cols'] = np.ascontiguousarray(rwc.reshape(64, 80))
    dn = ('b', 'f') if flip else ('f', 'b')
    lw = np.zeros((65, 4, 1024), np.float32)
    for d in range(2):
        lw[:64, d] = I[f'l0_ww2_{dn[d]}'][hp][:, chan]
        lw[64, d] = I[f'l0_w0_{dn[d]}'][chan]
        lw[:64, 2 + d] = I[f'l0_wa2_{dn[d]}'][hp][:, chan]
        lw[64, 2 + d] = I[f'l0_a0_{dn[d]}'][chan]
    sh['lora_w'] = np.ascontiguousarray(lw.reshape(65, 4096))
    grp = [1, 0, 3, 2] if flip else [0, 1, 2, 3]
    gp = np.concatenate([4 * np.arange(32) + g for g in grp] + [4 * np.arange(32, 40) + g for g in grp])
    sh['wg2'] = np.ascontiguousarray(I['l0_wg2'][gp][:, chan])
    s_ = np.arange(64)
    Ms = (s_[:, None] < s_[None, :]).astype(np.float32)
    Mi = (s_[:, None] <= s_[None, :]).astype(np.float32)
    sh['scan_masks'] = np.ascontiguousarray(np.concatenate([Ms, Mi, Ms.T, Ms.T, Mi.T, Ms], 1))
    rst = np.ones((64, 512), np.float32)
    rst[:, ::64] = 0
    sh['chunk_rst'] = rst
    wo = I['l0_w_out']
    sh['w_out0'] = np.ascontiguousarray(np.concatenate([wo[:1024], wo[1024 + chan]], 0))
    sh['mlp_w1_0'] = I['l0_mlp_w1']
    sh['mlp_w2_0'] = I['l0_mlp_w2']
    sh['w_qkv'] = I['l1_w_qkv']
    sh['na_bias'] = na_bias_tables(I['l1_rpb'], flip)
    sh['w_out1'] = I['l1_w_out']
    sh['mlp_w1_1'] = I['l1_mlp_w1']
    sh['mlp_w2_1'] = I['l1_mlp_w2']
    _SHARED[key] = sh
    m.update(sh)
    return m


_PROG = {}


def kernel(**inputs):
    I = {k: np.asarray(v) for k, v in inputs.items()}
    if 'p' not in _PROG:
        _PROG['p'] = build()
    P = _PROG['p']
    _SHARED.clear()
    in_maps = []
    for b in range(4):
        for half in range(2):
            m = host_inputs(I, b, half)
            in_maps.append({k: v for k, v in m.items() if k in P.inp})
    res = run_bass_kernel_spmd(P.nc, in_maps, core_ids=list(range(8)))
    out = np.empty((4, T, D), np.float32)
    for b in range(4):
        o0 = np.asarray(res.results[2 * b]['outT'])
        o1 = np.asarray(res.results[2 * b + 1]['outT'])
        out[b, :NOWN] = o0.T
        out[b, NOWN:] = o1.T[::-1]
    _SHARED.clear()
    return out
```

```python
import os
import numpy as np
import concourse.bass as bass
import concourse.mybir as mybir
from concourse.bass_utils import run_bass_kernel_spmd
from contextlib import ExitStack

F32 = mybir.dt.float32
BF16 = mybir.dt.bfloat16
F32R = mybir.dt.float32r
AF = mybir.ActivationFunctionType
ALU = mybir.AluOpType

D = 2048
KC = 16
T = 4096
CT = 256
S0 = T + CT
NE = 2304
NX = NE + CT
NOWN = 2048
RW = 3488
GQ = 1536
NCOL0 = GQ + RW
EPS = 1e-6


class Ctx:
    COMPUTE = ('pe', 'act', 'dve', 'pool')

    def __init__(self, nc, n_dma_sems=20):
        self.nc = nc
        self.eng = {'pe': nc.tensor, 'act': nc.scalar, 'dve': nc.vector, 'pool': nc.gpsimd, 'sp': nc.sync}
        self.sem = {e: nc.alloc_semaphore('c_' + e) for e in self.COMPUTE}
        self.cnt = {e: 0 for e in self.COMPUTE}
        self.dq = {}
        for q in ('sp', 'pool', 'act'):
            self.dq[q] = dict(sems=[nc.alloc_semaphore(f'd_{q}{i}') for i in range(n_dma_sems)],
                              val=[0] * n_dma_sems, nxt=0)
        self.seen = {}
        self.lastw = {}
        self.lastr = {}
        self.psn = 0
        self.nrot = 8
        self.ps_tiles = [nc.alloc_psum_tensor(f"ps{i}", [128, 512], F32) for i in range(8)]
        self.rr = {}

    def _semobj(self, key):
        if isinstance(key, str):
            return self.sem[key]
        q, i = key
        return self.dq[q]['sems'][i]

    def _wait(self, eng, tok):
        key, val = tok
        if eng == 'pe' and key == 'pe':
            return
        if self.seen.get((eng, key), 0) >= val:
            return
        self.eng[eng].wait_ge(self._semobj(key), val)
        self.seen[(eng, key)] = val

    def _deps(self, eng, reads, writes):
        for k in reads:
            t = self.lastw.get(k)
            if t is not None:
                self._wait(eng, t)
        for k in writes:
            t = self.lastw.get(k)
            if t is not None:
                self._wait(eng, t)
            for t in self.lastr.get(k, {}).values():
                self._wait(eng, t)

    def _record(self, tok, reads, writes):
        for k in reads:
            self.lastr.setdefault(k, {})[tok[0]] = tok
        for k in writes:
            self.lastw[k] = tok
            self.lastr[k] = {}

    def op(self, eng, fn, reads=(), writes=(), signal=True):
        self._deps(eng, reads, writes)
        ins = fn(self.eng[eng])
        if signal:
            self.cnt[eng] += 1
            ins.then_inc(self.sem[eng], 1)
            tok = (eng, self.cnt[eng])
        else:
            tok = (eng, self.cnt[eng] + 1)
        self._record(tok, reads, writes)
        return ins

    def dma(self, q, out, in_, reads=(), writes=(), **kw):
        d = self.dq[q]
        i = d['nxt']
        d['nxt'] = (i + 1) % len(d['sems'])
        if d['val'][i] > 0:
            self._wait(q, ((q, i), d['val'][i]))
        self._deps(q, reads, writes)
        ins = self.eng[q].dma_start(out=out, in_=in_, **kw)
        d['val'][i] += 16
        ins.then_inc(d['sems'][i], 16)
        tok = ((q, i), d['val'][i])
        self._record(tok, reads, writes)
        return ins

    def barrier(self):
        toks = [(e, self.cnt[e]) for e in self.COMPUTE if self.cnt[e] > 0]
        for q, d in self.dq.items():
            for i, v in enumerate(d['val']):
                if v > 0:
                    toks.append(((q, i), v))
        for e in ('pe', 'act', 'dve', 'pool', 'sp'):
            for t in toks:
                if t[0] == e:
                    continue
                self._wait(e, t)
        self.lastw = {}
        self.lastr = {}

    def finish(self, q='sp'):
        for e in self.COMPUTE:
            if self.cnt[e] > 0:
                self._wait(q, (e, self.cnt[e]))
        for qq, d in self.dq.items():
            for i, v in enumerate(d['val']):
                if v > 0:
                    self._wait(q, ((qq, i), v))

    def ps(self):
        i = self.psn % self.nrot
        self.psn = (i + 1) % self.nrot
        return self.ps_tiles[i], f'ps{i}'

    def psh(self):
        i = getattr(self, '_hn', 0) % len(self.hbanks)
        self._hn = (i + 1) % len(self.hbanks)
        b = self.hbanks[i]
        return self.ps_tiles[b], f'ps{b}'

    def pick(self, name, choices):
        i = self.rr.get(name, 0)
        self.rr[name] = i + 1
        return choices[i % len(choices)]


class Pool:
    def __init__(self, P, name, shape, dtype, n):
        self.t = [P.sb(f"{name}{i}", shape, dtype) for i in range(n)]
        self.k = [f"{name}{i}" for i in range(n)]
        self.i = 0

    def get(self):
        i = self.i
        self.i = (i + 1) % len(self.t)
        return self.t[i], self.k[i]


class Prog:
    def __init__(self, debug=()):
        self.debug = set(debug)
        nc = self.nc = bass.Bass("TRN2", target_bir_lowering=False)
        self.C = Ctx(nc)
        self.inp = {}
        self.outs = {}
        self.scr = {}
        self.gstack = ExitStack()
        self.stack = self.gstack

    def begin_phase(self):
        self.C.barrier()
        if self.stack is not self.gstack:
            self.stack.close()
        self.stack = ExitStack()

    def sub_begin(self):
        self._saved = self.stack
        self.stack = ExitStack()

    def sub_end(self):
        self.C.barrier()
        self.stack.close()
        self.stack = self._saved

    def din(self, name, shape, dt=F32):
        self.inp[name] = self.nc.dram_tensor(name, list(shape), dt, kind="ExternalInput").ap()
        return self.inp[name]

    def dout(self, name, shape, dt=F32):
        self.outs[name] = self.nc.dram_tensor(name, list(shape), dt, kind="ExternalOutput").ap()
        return self.outs[name]

    def dscr(self, name, shape, dt=F32):
        self.scr[name] = self.nc.dram_tensor(name, list(shape), dt, kind="Internal").ap()
        return self.scr[name]

    def sb(self, name, shape, dt=F32):
        self._uid = getattr(self, '_uid', 0) + 1
        return self.stack.enter_context(self.nc.sbuf_tensor(f"{name}_u{self._uid}", list(shape), dt))

    def load_consts(self):
        C = self.C
        self.din('ident', [128, 128])
        self.din('ones_blk', [128, 128])
        self.ident = self.sb('ident_sb', [128, 128])
        self.ones = self.sb('ones_sb', [128, 128])
        self.ones_blk = self.sb('ones_blk_sb', [128, 128])
        C.dma('sp', self.ident[:], self.inp['ident'][:, :], writes=['ident'])
        C.dma('sp', self.ones_blk[:], self.inp['ones_blk'][:, :], writes=['ones_blk'])
        C.op('dve', lambda e: e.memset(self.ones[:], 1.0), writes=['ones'])
        self.din('ccol', [128, 32])
        self.din('ada_b', [128, 192])
        self.din('nrm', [128, 80])
        self.ccol = self.sb('ccol_sb', [128, 32])
        self.ada_b = self.sb('ada_b_sb', [128, 192])
        self.nrm = self.sb('nrm_sb', [128, 80])
        C.dma('sp', self.ccol[:], self.inp['ccol'][:, :], writes=['ccol'])
        C.dma('sp', self.ada_b[:], self.inp['ada_b'][:, :], writes=['ada_b'])
        C.dma('sp', self.nrm[:], self.inp['nrm'][:, :], writes=['nrm'])

    def phase_mod(self):
        C, nc = self.C, self.nc
        self.din('ada_w0', [D, 6 * D])
        self.din('ada_w1', [D, 6 * D])
        self.mod = self.sb('mod', [128, 2 * 96 * 2])
        modv = self.mod[:].rearrange("p (l j v) -> p l j v", l=2, j=96)
        self.Acol = self.sb('Acol', [128, 2 * 2 * 16 * 2])
        Av = self.Acol[:].rearrange("p (l w k v) -> p l w k v", l=2, w=2, k=16)
        self.begin_phase()
        s = self.sb('silu_c', [128, 32])
        C.op('act', lambda e: e.activation(out=s[:], in_=self.ccol[:], func=AF.Silu), reads=['ccol'], writes=['silu_c'])
        NCB = 768
        wp = Pool(self, 'adaw', [128, KC * NCB], F32, 2)
        for L in range(2):
            W = self.inp[f'ada_w{L}'].rearrange("(k p) c -> p k c", p=128)
            for cb in range(6 * D // NCB):
                wt, wk = wp.get()
                wv = wt[:].rearrange("p (k c) -> p k c", k=KC)
                for kq in range(4):
                    C.dma('sp', wv[:, kq * 4:(kq + 1) * 4, :], W[:, kq * 4:(kq + 1) * 4, cb * NCB:(cb + 1) * NCB], writes=[wk])
                pt, pk = C.ps()
                for j in range(NCB // 128):
                    for kc in range(KC):
                        C.op('pe', lambda e, j=j, kc=kc: e.matmul(pt[:, j * 2:j * 2 + 2], lhsT=wv[:, kc, j * 128:(j + 1) * 128],
                                                                 rhs=s[:, kc * 2:kc * 2 + 2], start=(kc == 0), stop=(kc == KC - 1)),
                             reads=[wk, 'silu_c'], writes=[pk], signal=(kc == KC - 1 and j == NCB // 128 - 1))
                for j in range(NCB // 128):
                    jg = cb * (NCB // 128) + j
                    C.op('dve', lambda e, j=j, jg=jg: e.tensor_scalar(out=modv[:, L, jg, :], in0=pt[:, j * 2:j * 2 + 2],
                                                                     scalar1=self.ada_b[:, L * 96 + jg:L * 96 + jg + 1], scalar2=None, op0=ALU.add),
                         reads=[pk, 'ada_b'], writes=['mod'])
            for w, (sci, nidx) in enumerate(((1, 2 * L), (4, 2 * L + 1))):
                for v in range(2):
                    C.op('dve', lambda e, w=w, sci=sci, nidx=nidx, v=v: e.scalar_tensor_tensor(
                        out=Av[:, L, w, :, v], in0=modv[:, L, sci * 16:(sci + 1) * 16, v], scalar=1.0,
                        in1=self.nrm[:, nidx * 16:(nidx + 1) * 16], op0=ALU.add, op1=ALU.mult),
                         reads=['mod', 'nrm'], writes=['Acol'])
        self.modv, self.Av = modv, Av

    def mcol(self, L, s, kc, v):
        return self.modv[:, L, s * 16 + kc, v:v + 1]

    def norm_mod(self, xt, xk, n, h, hk, Afn, shfn, tmpp, sqp):
        C = self.C
        pt, pk = C.ps()
        for kc in range(KC):
            sq, sqk = sqp.get()
            C.op('act', lambda e, kc=kc, sq=sq: e.activation(out=sq[:, 0:n], in_=xt[:, kc, :], func=AF.Square), reads=[xk], writes=[sqk])
            C.op('pe', lambda e, kc=kc, sq=sq: e.matmul(pt[:, 0:n], lhsT=self.ones[:], rhs=sq[:, 0:n], start=(kc == 0), stop=(kc == KC - 1)),
                 reads=[sqk, 'ones'], writes=[pk])
        rs, rsk = sqp.get()
        C.op('act', lambda e: e.activation(out=rs[:, 0:n], in_=pt[:, 0:n], func=AF.Sqrt, bias=EPS, scale=1.0 / D), reads=[pk], writes=[rsk])
        C.op('dve', lambda e: e.reciprocal(out=rs[:, 0:n], in_=rs[:, 0:n]), reads=[rsk], writes=[rsk])
        for kc in range(KC):
            tm, tmk = tmpp.get()
            eng = 'dve'
            C.op(eng, lambda e, kc=kc, tm=tm: e.tensor_tensor(out=tm[:, 0:n], in0=xt[:, kc, :], in1=rs[:, 0:n], op=ALU.mult),
                 reads=[xk, rsk], writes=[tmk])
            if shfn is not None:
                C.op('act', lambda e, kc=kc, tm=tm: e.activation(out=h[:, kc, :], in_=tm[:, 0:n], func=AF.Identity, bias=shfn(kc), scale=Afn(kc)),
                     reads=[tmk, 'mod', 'Acol'], writes=[hk])
            else:
                C.op('act', lambda e, kc=kc, tm=tm: e.activation(out=h[:, kc, :], in_=tm[:, 0:n], func=AF.Copy, scale=Afn(kc)),
                     reads=[tmk, 'nrm'], writes=[hk])

    def wload_init(self, WB=256, nst=2, nwb=3):
        self.WB = WB
        self.wst = Pool(self, 'wst', [128, KC * WB], F32, nst)
        self.wbp = Pool(self, 'wbf', [128, KC * WB], BF16, nwb)

    def wstream(self, blocks, pf=1):
        self._wq = list(blocks)
        self._wi = 0
        self._wissued = []
        self._wpf = pf

    def wnext(self):
        while len(self._wissued) <= self._wi + self._wpf and len(self._wissued) < len(self._wq):
            b = self._wq[len(self._wissued)]
            self._wissued.append(self.wload(*b))
        r = self._wissued[self._wi]
        self._wi += 1
        return r

    def wload(self, Wv, c0, ncols, kchunks=KC, k0=0):
        C = self.C
        st, sk = self.wst.get()
        sv = st[:].rearrange("p (k c) -> p k c", k=KC)[:, 0:kchunks, 0:ncols]
        wb, wk = self.wbp.get()
        wv = wb[:].rearrange("p (k c) -> p k c", k=KC)[:, 0:kchunks, 0:ncols]
        step = max(1, kchunks // 4)
        dstep = max(1, kchunks // 2)
        for kq in range(0, kchunks, dstep):
            ke = min(kchunks, kq + dstep)
            C.dma('sp', sv[:, kq:ke, :], Wv[:, k0 + kq:k0 + ke, c0:c0 + ncols], writes=[sk])
        for kq in range(0, kchunks, step):
            ke = min(kchunks, kq + step)
            eng = C.pick('wcast', ['dve', 'act', 'pool', 'act', 'dve', 'act', 'dve', 'act'])
            if eng == 'act':
                C.op('act', lambda e: e.activation(out=wv[:, kq:ke, :], in_=sv[:, kq:ke, :], func=AF.Copy), reads=[sk], writes=[wk])
            else:
                C.op(eng, lambda e: e.tensor_copy(out=wv[:, kq:ke, :], in_=sv[:, kq:ke, :]), reads=[sk], writes=[wk])
        return wv, wk

    def phase_proj0(self):
        C, nc = self.C, self.nc
        xT = self.din('xT', [D, S0]).rearrange("(k p) t -> p k t", p=128)
        W = self.din('w_in', [D, NCOL0]).rearrange("(k p) c -> p k c", p=128)
        pT = self.dscr('pT', [NCOL0, S0])
        vtok = self.dscr('vtok', [S0, 256], BF16)
        xp = Pool(self, 'p1x', [128, KC * 512], F32, 2)
        hp = Pool(self, 'p1h', [128, KC * 512], BF16, 2)
        sqp = Pool(self, 'p1sq', [128, 512], F32, 3)
        tmpp = Pool(self, 'p1tm', [128, 512], F32, 4)
        WB = 256
        self.wload_init(WB)
        evp = Pool(self, 'p1ev', [128, 512], F32, 4)
        vp = Pool(self, 'p1v', [128, 256], BF16, 3)
        tiles = [(i * 512, 512, 0) for i in range(8)] + [(T, CT, 1)]
        VC0, VC1 = 1280, 1536
        fm_blocks = [(c, min(WB, NCOL0 - c)) for c in list(range(0, VC0, WB)) + list(range(VC1, NCOL0, WB))]
        wl = []
        for _ in tiles:
            wl += [(W, c0, nc_, KC, 0) for (c0, nc_) in fm_blocks] + [(W, VC0, 256, KC, 0)]
        self.wstream(wl, 1)
        for (t0, n, v) in tiles:
            xt, xk = xp.get()
            xv = xt[:].rearrange("p (k t) -> p k t", k=KC)[:, :, 0:n]
            for kq in range(4):
                C.dma('sp', xv[:, kq * 4:(kq + 1) * 4, :], xT[:, kq * 4:(kq + 1) * 4, t0:t0 + n], writes=[xk])
            ht, hk = hp.get()
            hv = ht[:].rearrange("p (k t) -> p k t", k=KC)[:, :, 0:n]
            self.norm_mod(xv, xk, n, hv, hk, lambda kc: self.Av[:, 0, 0, kc, v:v + 1], lambda kc: self.mcol(0, 0, kc, v), tmpp, sqp)
            if 'h0' in self.debug:
                self._dbg_h(hv, hk, t0, n)
            for (c0, nc_) in fm_blocks:
                wv, wk = self.wnext()
                for m0 in range(0, nc_, 128):
                    m = min(128, nc_ - m0)
                    pt, pk = C.ps()
                    for kc in range(KC):
                        C.op('pe', lambda e, kc=kc, m0=m0, m=m: e.matmul(pt[0:m, 0:n], lhsT=wv[:, kc, m0:m0 + m], rhs=hv[:, kc, :],
                                                                        start=(kc == 0), stop=(kc == KC - 1)),
                             reads=[wk, hk], writes=[pk], signal=(kc == KC - 1))
                    ev, evk = evp.get()
                    eng = C.pick('p1ev', ['act', 'dve'])
                    if eng == 'act':
                        C.op('act', lambda e, m=m: e.activation(out=ev[0:m, 0:n], in_=pt[0:m, 0:n], func=AF.Copy), reads=[pk], writes=[evk])
                    else:
                        C.op('dve', lambda e, m=m: e.tensor_copy(out=ev[0:m, 0:n], in_=pt[0:m, 0:n]), reads=[pk], writes=[evk])
                    C.dma('sp', pT[c0 + m0:c0 + m0 + m, t0:t0 + n], ev[0:m, 0:n], reads=[evk], writes=['pT'])
            wv, wk = self.wnext()
            for s0 in range(0, n, 128):
                pt, pk = C.ps()
                for kc in range(KC):
                    C.op('pe', lambda e, kc=kc, s0=s0: e.matmul(pt[:, 0:256], lhsT=hv[:, kc, s0:s0 + 128], rhs=wv[:, kc, :],
                                                                start=(kc == 0), stop=(kc == KC - 1)),
                         reads=[wk, hk], writes=[pk], signal=(kc == KC - 1))
                vt, vk = vp.get()
                C.op('act', lambda e: e.activation(out=vt[:], in_=pt[:, 0:256], func=AF.Copy), reads=[pk], writes=[vk])
                C.dma('sp', vtok[t0 + s0:t0 + s0 + 128, :], vt[:], reads=[vk], writes=['vtok'])


    def headnorm(self, raw, rawk, n, colscalar, tp, sqp):
        C = self.C
        sq, sqk = sqp.get()
        C.op('act', lambda e: e.activation(out=sq[:, 0:n], in_=raw, func=AF.Square), reads=[rawk], writes=[sqk])
        pt, pk = C.ps()
        C.op('pe', lambda e: e.matmul(pt[:, 0:n], lhsT=self.ones_blk[:], rhs=sq[:, 0:n], start=True, stop=True), reads=[sqk, 'ones_blk'], writes=[pk])
        rs, rsk = sqp.get()
        C.op('act', lambda e: e.activation(out=rs[:, 0:n], in_=pt[:, 0:n], func=AF.Sqrt, bias=EPS, scale=1.0 / 64), reads=[pk], writes=[rsk])
        C.op('dve', lambda e: e.reciprocal(out=rs[:, 0:n], in_=rs[:, 0:n]), reads=[rsk], writes=[rsk])
        kn, knk = tp.get()
        C.op('dve', lambda e: e.scalar_tensor_tensor(out=kn[:, 0:n], in0=raw, scalar=colscalar, in1=rs[:, 0:n], op0=ALU.mult, op1=ALU.mult),
             reads=[rawk, rsk, 'qkn'], writes=[knk])
        return kn, knk

    def rope(self, kn, knk, n, cs0, out, outk, tp):
        C = self.C
        pt, pk = C.ps()
        C.op('pe', lambda e: e.matmul(pt[:, 0:n], lhsT=self.rotm[:], rhs=kn[:, 0:n], start=True, stop=True), reads=[knk, 'rotm'], writes=[pk])
        t1, t1k = tp.get()
        C.op('dve', lambda e: e.tensor_tensor(out=t1[:, 0:n], in0=kn[:, 0:n], in1=self.cos[:, cs0:cs0 + n], op=ALU.mult), reads=[knk, 'cos'], writes=[t1k])
        t2, t2k = tp.get()
        C.op('dve', lambda e: e.tensor_tensor(out=t2[:, 0:n], in0=pt[:, 0:n], in1=self.sin[:, cs0:cs0 + n], op=ALU.mult), reads=[pk, 'sin'], writes=[t2k])
        C.op('dve', lambda e: e.tensor_tensor(out=out, in0=t1[:, 0:n], in1=t2[:, 0:n], op=ALU.add), reads=[t1k, t2k], writes=[outk])

    def phase_gqa(self):
        C, nc = self.C, self.nc
        pT = self.scr['pT']
        vtok = self.scr['vtok']
        attnT = self.dscr('attnT', [D, NX], BF16)
        self.din('qkn', [128, 2])
        self.din('rotm', [128, 128])
        self.din('rope_cos', [128, T])
        self.din('rope_sin', [128, T])
        KT = [self.sb(f'KT{i}', [128, S0], BF16) for i in range(2)]
        QT = [self.sb(f'QT{i}', [128, NX], BF16) for i in range(8)]
        Vaug = self.sb('Vaug', [128, 34 * 4 * 128], BF16)
        Vv = Vaug[:].rearrange("p (c h d) -> p c h d", c=34, h=4)
        qkn = self.sb('qkn_sb', [128, 2])
        self.rotm = self.sb('rotm_sb', [128, 128])
        C.dma('sp', qkn[:], self.inp['qkn'][:, :], writes=['qkn'])
        C.dma('sp', self.rotm[:], self.inp['rotm'][:, :], writes=['rotm'])
        self.sub_begin()
        self.cos = self.sb('cos_sb', [128, T])
        self.sin = self.sb('sin_sb', [128, T])
        for q4 in range(4):
            C.dma('sp', self.cos[:, q4 * 1024:(q4 + 1) * 1024], self.inp['rope_cos'][:, q4 * 1024:(q4 + 1) * 1024], writes=['cos'])
            C.dma('sp', self.sin[:, q4 * 1024:(q4 + 1) * 1024], self.inp['rope_sin'][:, q4 * 1024:(q4 + 1) * 1024], writes=['sin'])
        rawp = Pool(self, 'g_raw', [128, S0], F32, 2)
        sqp = Pool(self, 'g_sq', [128, 512], F32, 4)
        tp = Pool(self, 'g_tp', [128, 512], F32, 6)
        vst = self.sb('g_vst', [128, 34 * 256], BF16)
        vsv = vst[:].rearrange("p (c x) -> p c x", c=34)
        vsrc = vtok.rearrange("(c p) x -> p c x", p=128)
        for c4 in range(0, 34, 6):
            c5 = min(34, c4 + 6)
            C.dma('sp', vsv[:, c4:c5, :], vsrc[:, c4:c5, :], writes=['g_vst'])
        C.op('dve', lambda e: e.memset(Vaug[:], 1.0), writes=['Vaug'])
        for h in range(4):
            C.op('dve', lambda e, h=h: e.tensor_copy(out=Vv[:, :, h, 0:64], in_=vsv[:, :, h * 64:(h + 1) * 64]), reads=['g_vst'], writes=['Vaug'])
        ktiles = [(i * 512, 512, True) for i in range(8)] + [(T, CT, False)]
        for kt in range(2):
            raw, rawk = rawp.get()
            for q4 in range(0, S0, 1088):
                C.dma('sp', raw[:, q4:q4 + 1088], pT[1024 + kt * 128:1024 + (kt + 1) * 128, q4:q4 + 1088], writes=[rawk])
            for (t0, n, lat) in ktiles:
                kn, knk = self.headnorm(raw[:, t0:t0 + n], rawk, n, qkn[:, 1:2], tp, sqp)
                if lat:
                    self.rope(kn, knk, n, t0, KT[kt][:, t0:t0 + n], f'KT{kt}', tp)
                else:
                    C.op('act', lambda e: e.activation(out=KT[kt][:, t0:t0 + n], in_=kn[:, 0:n], func=AF.Copy), reads=[knk], writes=[f'KT{kt}'])
        self.qpairs = [(0, 4), (1, 5), (2, 6), (3, 7), (8, 12), (9, 13), (10, 14), (11, 15)]
        qtiles = [(i * 512, i * 512, 512, True) for i in range(4)] + [(2048, 2048, 256, True), (T, NE, CT, False)]
        for j, (a, b) in enumerate(self.qpairs):
            raw, rawk = rawp.get()
            for hh, hq in enumerate((a, b)):
                C.dma('sp', raw[hh * 64:(hh + 1) * 64, 0:NE], pT[hq * 64:(hq + 1) * 64, 0:NE], writes=[rawk])
                C.dma('sp', raw[hh * 64:(hh + 1) * 64, NE:NX], pT[hq * 64:(hq + 1) * 64, T:S0], writes=[rawk])
            for (src0, x0, n, lat) in qtiles:
                kn, knk = self.headnorm(raw[:, x0:x0 + n], rawk, n, qkn[:, 0:1], tp, sqp)
                if lat:
                    self.rope(kn, knk, n, src0, QT[j][:, x0:x0 + n], f'QT{j}', tp)
                else:
                    C.op('act', lambda e: e.activation(out=QT[j][:, x0:x0 + n], in_=kn[:, 0:n], func=AF.Copy), reads=[knk], writes=[f'QT{j}'])
        if 'qk' in self.debug:
            o = self.dout('dbg_KT', [256, S0], BF16)
            for i in range(2):
                C.dma('sp', o[i * 128:(i + 1) * 128, :], KT[i][:], reads=[f'KT{i}'])
            o = self.dout('dbg_QT', [1024, NX], BF16)
            for i in range(8):
                C.dma('sp', o[i * 128:(i + 1) * 128, :], QT[i][:], reads=[f'QT{i}'])
        self.sub_end()
        C.nrot = 6
        ptp = Pool(self, 'g_pt', [128, 512], BF16, 6)
        osp = Pool(self, 'g_os', [128, 512], F32, 2)
        rsp = Pool(self, 'g_rs', [64, 512], F32, 2)
        onp = Pool(self, 'g_on', [128, 512], BF16, 2)
        acc_i = 0
        for hq in range(16):
            kvh = hq // 4
            base = (kvh % 2) * 64
            j = [i for i, pr in enumerate(self.qpairs) if hq in pr][0]
            jobs = [(i * 512, 512, list(range(34))) for i in range(4)] + [(2048, 256, list(range(34))), (NE, CT, [32, 33])]
            for (q0, n, chunks) in jobs:
                po = C.ps_tiles[6 + acc_i % 2]
                pok = f'ps{6 + acc_i % 2}'
                acc_i += 1
                pend = []

                def pv(item):
                    pb, pbk, c, ci = item
                    C.op('pe', lambda e: e.matmul(po[:, 0:n], lhsT=Vv[:, c, kvh, :], rhs=pb[:, 0:n],
                                                  start=(ci == 0), stop=(ci == len(chunks) - 1)),
                         reads=['Vaug', pbk], writes=[pok])
                for ci, c in enumerate(chunks):
                    pt, pk = C.ps()
                    C.op('pe', lambda e, c=c: e.matmul(pt[:, 0:n], lhsT=KT[kvh // 2][base:base + 64, c * 128:(c + 1) * 128],
                                                      rhs=QT[j][base:base + 64, q0:q0 + n], start=True, stop=True),
                         reads=[f'KT{kvh // 2}', f'QT{j}'], writes=[pk])
                    pb, pbk = ptp.get()
                    C.op('act', lambda e: e.activation(out=pb[:, 0:n], in_=pt[:, 0:n], func=AF.Exp), reads=[pk], writes=[pbk])
                    pend.append((pb, pbk, c, ci))
                    if len(pend) > 2:
                        pv(pend.pop(0))
                while pend:
                    pv(pend.pop(0))
                osb, osk = osp.get()
                C.op('dve', lambda e: e.tensor_copy(out=osb[:, 0:n], in_=po[:, 0:n]), reads=[pok], writes=[osk])
                rs, rsk = rsp.get()
                C.dma('sp', rs[:, 0:n], osb[64:128, 0:n], reads=[osk], writes=[rsk])
                C.op('dve', lambda e: e.reciprocal(out=rs[:, 0:n], in_=rs[:, 0:n]), reads=[rsk], writes=[rsk])
                on, onk = onp.get()
                C.op('dve', lambda e: e.tensor_tensor(out=on[0:64, 0:n], in0=osb[0:64, 0:n], in1=rs[:, 0:n], op=ALU.mult), reads=[osk, rsk], writes=[onk])
                C.dma('sp', attnT[hq * 64:(hq + 1) * 64, q0:q0 + n], on[0:64, 0:n], reads=[onk], writes=['attnT'])
        C.nrot = 8

    def phase_rwkv_shift(self):
        C = self.C
        pT = self.scr['pT']
        rwT = self.dscr('rwT', [RW, S0])
        self.din('mu_col', [128, 28])
        mu = self.sb('mu_sb', [128, 28])
        omu = self.sb('omu_sb', [128, 28])
        C.dma('sp', mu[:], self.inp['mu_col'][:, :], writes=['mu'])
        C.op('dve', lambda e: e.tensor_scalar(out=omu[:], in0=mu[:], scalar1=-1.0, scalar2=1.0, op0=ALU.mult, op1=ALU.add), reads=['mu'], writes=['omu'])
        rawp = Pool(self, 'rs_raw', [128, S0], F32, 2)
        outp = Pool(self, 'rs_out', [128, S0], F32, 2)
        for g in range(4):
            for j in range(7):
                rows = min(128, 872 - j * 128)
                r0 = g * 872 + j * 128
                ci = g * 7 + j
                raw, rk = rawp.get()
                for q4 in range(0, S0, 1088):
                    C.dma('sp', raw[0:rows, q4:q4 + 1088], pT[GQ + r0:GQ + r0 + rows, q4:q4 + 1088], writes=[rk])
                o, ok = outp.get()
                C.op('act', lambda e: e.activation(out=o[0:rows, :], in_=raw[0:rows, :], func=AF.Copy, scale=omu[0:rows, ci:ci + 1]), reads=[rk, 'omu'], writes=[ok])
                m = mu[0:rows, ci:ci + 1]
                rv = raw[0:rows, 0:T].rearrange("p (r c) -> p r c", c=64)
                ov = o[0:rows, 0:T].rearrange("p (r c) -> p r c", c=64)
                if g == 0:
                    src, dst = rv[:, :, 0:63], ov[:, :, 1:64]
                elif g == 1:
                    src, dst = rv[:, :, 1:64], ov[:, :, 0:63]
                elif g == 2:
                    src, dst = raw[0:rows, 0:T - 64], o[0:rows, 64:T]
                else:
                    src, dst = raw[0:rows, 64:T], o[0:rows, 0:T - 64]
                C.op('dve', lambda e: e.scalar_tensor_tensor(out=dst, in0=src, scalar=m, in1=dst, op0=ALU.mult, op1=ALU.add), reads=[rk, ok, 'mu'], writes=[ok])
                if g in (0, 2):
                    src, dst = raw[0:rows, T:S0 - 1], o[0:rows, T + 1:S0]
                else:
                    src, dst = raw[0:rows, T + 1:S0], o[0:rows, T:S0 - 1]
                C.op('dve', lambda e: e.scalar_tensor_tensor(out=dst, in0=src, scalar=m, in1=dst, op0=ALU.mult, op1=ALU.add), reads=[rk, ok, 'mu'], writes=[ok])
                for q4 in range(0, S0, 1088):
                    C.dma('sp', rwT[r0:r0 + rows, q4:q4 + 1088], o[0:rows, q4:q4 + 1088], reads=[ok], writes=['rwT'])

    def rw_rows(self, i0, hd=None, n16=16):
        return [(g * 872 + i0, n16) for g in range(4)]

    def phase_rwkv_scan(self):
        C, nc = self.C, self.nc
        rwT = self.scr['rwT']
        attnT = self.scr['attnT']
        yfT = self.dscr('yfT', [1024, NX])
        self.din('rw_cols', [64, 16 * 5])
        self.din('lora_w', [65, 4 * 1024])
        self.din('wg2', [160, 1024])
        self.din('scan_masks', [64, 6 * 64])
        self.din('chunk_rst', [64, 512])
        rwc = self.sb('rwc', [64, 80])
        oka = self.sb('oka', [64, 16])
        lw_sb = self.sb('lora_sb', [65, 4096])
        wg_a = self.sb('wg_a', [128, 1024])
        wg_b = self.sb('wg_b', [32, 1024])
        msk = self.sb('scan_msk', [64, 384])
        rst = self.sb('chunk_rst_sb', [64, 512])
        C.dma('sp', rwc[:], self.inp['rw_cols'][:, :], writes=['rwc'])
        for q in range(4):
            C.dma('sp', lw_sb[:, q * 1024:(q + 1) * 1024], self.inp['lora_w'][:, q * 1024:(q + 1) * 1024], writes=['lora'])
        C.dma('sp', wg_a[:], self.inp['wg2'][0:128, :], writes=['wg'])
        C.dma('sp', wg_b[:], self.inp['wg2'][128:160, :], writes=['wg'])
        C.dma('sp', msk[:], self.inp['scan_masks'][:, :], writes=['msk'])
        C.dma('sp', rst[:], self.inp['chunk_rst'][:, :], writes=['rst'])
        rwcv = rwc[:].rearrange("p (h q) -> p h q", q=5)
        C.op('dve', lambda e: e.tensor_scalar(out=oka[:], in0=rwcv[:, :, 1], scalar1=-1.0, scalar2=1.0, op0=ALU.mult, op1=ALU.add), reads=['rwc'], writes=['oka'])
        ones64 = self.ones[0:64, 0:64]
        id64 = self.ident[0:64, 0:64]
        Z = [self.sb(f'Z{h}', [64, 64], F32R) for h in range(16)]
        Zn = [0] * 16
        inp = Pool(self, 'rk_in', [64, 512], F32, 12)
        lop = Pool(self, 'rk_lo', [65, 512], F32, 4)
        tp = Pool(self, 'rk_t', [64, 512], F32, 16)
        arp = Pool(self, 'rk_ar', [64, 8 * 128], F32R, 4)
        bkp = Pool(self, 'rk_bk', [64, 8 * 128], F32R, 4)
        bhp = Pool(self, 'rk_bh', [64, 8 * 128], F32, 4)
        wcp = Pool(self, 'rk_wc', [64, 8], F32, 8)
        GRP = 4
        smr = [Pool(self, f'rk_sm{i}_', [64, 128], F32R, 7) for i in range(GRP)]
        fixb = []
        for i in range(GRP):
            fixb.append({'vbk': (self.sb(f'rk_vbk{i}', [64, 192], F32R), f'rk_vbk{i}'), 'NB': (self.sb(f'rk_NB{i}', [64, 128], F32R), f'rk_NB{i}'),
                         'NK': (self.sb(f'rk_NK{i}', [64, 128], F32R), f'rk_NK{i}'), 'A': (self.sb(f'rk_A{i}', [64, 64], F32R), f'rk_A{i}')})
        tfp = Pool(self, 'rk_tf', [64, 512], F32, 6)
        ysbp = Pool(self, 'rk_ys', [64, 512], F32, 6)
        xgp = Pool(self, 'rk_xg', [128, 512], F32, 2)
        xg2p = Pool(self, 'rk_xg2', [32, 512], F32, 2)
        outp = Pool(self, 'rk_o', [64, 512], BF16, 2)
        C.nrot = 1
        C.psn = 0
        C.hbanks = [1, 2, 3, 4, 5, 6, 7]

        def xcols(kind, c0):
            return (NE + c0 * 64) if kind == 'ctx' else c0 * 64

        def scols(kind, c0):
            return (T + c0 * 64) if kind == 'ctx' else c0 * 64

        def load_rows(dst, dk, i0, n, s0, rows16=16, base=0):
            for g in range(4):
                C.dma('sp', dst[base + g * rows16:base + (g + 1) * rows16, 0:n], rwT[g * 872 + i0:g * 872 + i0 + rows16, s0:s0 + n], writes=[dk])

        def lora_in(i0, n, s0, func):
            t, k = lop.get()
            load_rows(t, k, i0, n, s0)
            C.op('act', lambda e: e.activation(out=t[0:64, 0:n], in_=t[0:64, 0:n], func=func), reads=[k], writes=[k])
            C.op('dve', lambda e: e.memset(t[64:65, 0:n], 1.0), writes=[k])
            return t, k

        def ew(eng, fn, reads, n=None):
            t, k = tp.get()
            C.op(eng, lambda e: fn(e, t), reads=reads, writes=[k])
            return t, k

        for d in range(2):
            if d == 0:
                blocks = [('ctx', 0, 4, True)] + [('lat', c, 8, True) for c in (0, 8, 16, 24)] + [('lat', 32, 4, True)]
            else:
                blocks = [('ctx', 0, 4, True)] + [('lat', c, 8, False) for c in (56, 48, 40)] + [('lat', 36, 4, False), ('lat', 32, 4, True)] + \
                         [('lat', c, 8, True) for c in (24, 16, 8, 0)]
            for h in range(16):
                C.op('dve', lambda e, h=h: e.tensor_scalar(out=Z[h][:], in0=id64, scalar1=0.0, scalar2=None, op0=ALU.mult), reads=['ident'], writes=[f'Z{h}'])
            mS = msk[:, d * 192:d * 192 + 128]
            mST = msk[:, d * 192 + 128:d * 192 + 192]
            for (kind, c0, nch, outs) in blocks:
                n = nch * 64
                s0 = scols(kind, c0)
                x0 = xcols(kind, c0)
                xw, xwk = lora_in(768 + 16 * d, n, s0, AF.Tanh)
                xa, xak = lora_in(800 + 16 * d, n, s0, AF.Copy)
                if outs and d == 1:
                    xa0, xa0k = lora_in(800, n, s0, AF.Copy)
                    xg, xgk = xgp.get()
                    xg2, xg2k = xg2p.get()
                    for g in range(4):
                        C.dma('sp', xg[g * 32:(g + 1) * 32, 0:n], rwT[g * 872 + 832:g * 872 + 864, s0:s0 + n], writes=[xgk])
                        C.dma('sp', xg2[g * 8:(g + 1) * 8, 0:n], rwT[g * 872 + 864:g * 872 + 872, s0:s0 + n], writes=[xg2k])
                    C.op('act', lambda e: e.activation(out=xg[:, 0:n], in_=xg[:, 0:n], func=AF.Sigmoid), reads=[xgk], writes=[xgk])
                    C.op('act', lambda e: e.activation(out=xg2[:, 0:n], in_=xg2[:, 0:n], func=AF.Sigmoid), reads=[xg2k], writes=[xg2k])
                for g0 in range(0, 16, GRP):
                  HS = {}
                  for h in range(g0, g0 + GRP):
                    hc = slice(h * 64, (h + 1) * 64)
                    kkc, kac, rkc, lgc, lbc = [rwcv[:, h, q:q + 1] for q in range(5)]
                    kt, kk_ = inp.get(); load_rows(kt, kk_, 256 + h * 16, n, s0)
                    vt, vk_ = inp.get(); load_rows(vt, vk_, 512 + h * 16, n, s0)
                    if outs:
                        rt, rk_ = inp.get(); load_rows(rt, rk_, h * 16, n, s0)
                    pz, pzk = C.ps()
                    C.op('pe', lambda e: e.matmul(pz[0:64, 0:n], lhsT=lw_sb[0:65, d * 1024 + h * 64:d * 1024 + (h + 1) * 64], rhs=xw[0:65, 0:n], start=True, stop=True),
                         reads=['lora', xwk], writes=[pzk])
                    sg, sgk = ew('act', lambda e, t: e.activation(out=t[:, 0:n], in_=pz[0:64, 0:n], func=AF.Sigmoid), [pzk])
                    lw, lwk = ew('dve', lambda e, t: e.tensor_scalar(out=t[:, 0:n], in0=sg[:, 0:n], scalar1=-0.6065306597126334, scalar2=None, op0=ALU.mult), [sgk])
                    Pp, Ppk = ew('dve', lambda e, t: e.tensor_tensor_scan(out=t[:, 0:n], data0=rst[:, 0:n], data1=lw[:, 0:n], initial=0.0, op0=ALU.mult, op1=ALU.add), [lwk, 'rst'])
                    Ee, Eek = ew('dve', lambda e, t: e.tensor_tensor(out=t[:, 0:n], in0=Pp[:, 0:n], in1=lw[:, 0:n], op=ALU.subtract), [Ppk, lwk])
                    P3 = Pp[:, 0:n].rearrange("p (c t) -> p c t", t=64)
                    Qq, Qqk = ew('dve', lambda e, t: e.tensor_tensor(out=t[:, 0:n].rearrange("p (c t) -> p c t", t=64), in0=P3[:, :, 63:64].to_broadcast([64, nch, 64]), in1=P3, op=ALU.subtract), [Ppk])
                    if d == 0:
                        Lin, Link, Lex, Lexk, Lh, Lhk = Pp, Ppk, Ee, Eek, Qq, Qqk
                    else:
                        Lin, Link = ew('dve', lambda e, t: e.tensor_tensor(out=t[:, 0:n], in0=Qq[:, 0:n], in1=lw[:, 0:n], op=ALU.add), [Qqk, lwk])
                        Lex, Lexk, Lh, Lhk = Qq, Qqk, Ee, Eek
                    wc, wck = wcp.get()
                    C.op('act', lambda e: e.activation(out=wc[:, 0:nch], in_=P3[:, :, 63], func=AF.Exp), reads=[Ppk], writes=[wck])
                    eLex, eLexk = ew('act', lambda e, t: e.activation(out=t[:, 0:n], in_=Lex[:, 0:n], func=AF.Exp), [Lexk])
                    eNeg, eNegk = ew('act', lambda e, t: e.activation(out=t[:, 0:n], in_=Lin[:, 0:n], func=AF.Exp, scale=-1.0), [Link])
                    eH, eHk = ew('act', lambda e, t: e.activation(out=t[:, 0:n], in_=Lh[:, 0:n], func=AF.Exp), [Lhk])
                    kk, kkk = ew('dve', lambda e, t: e.tensor_scalar(out=t[:, 0:n], in0=kt[:, 0:n], scalar1=kkc, scalar2=None, op0=ALU.mult), [kk_, 'rwc'])
                    sq, sqk = ew('act', lambda e, t: e.activation(out=t[:, 0:n], in_=kk[:, 0:n], func=AF.Square), [kkk])
                    pss, pssk = C.ps()
                    C.op('pe', lambda e: e.matmul(pss[0:64, 0:n], lhsT=ones64, rhs=sq[:, 0:n], start=True, stop=True), reads=[sqk, 'ones'], writes=[pssk])
                    rn, rnk = ew('act', lambda e, t: e.activation(out=t[:, 0:n], in_=pss[0:64, 0:n], func=AF.Sqrt), [pssk])
                    C.op('dve', lambda e: e.tensor_scalar(out=rn[:, 0:n], in0=rn[:, 0:n], scalar1=1e-6, scalar2=None, op0=ALU.max), reads=[rnk], writes=[rnk])
                    C.op('dve', lambda e: e.reciprocal(out=rn[:, 0:n], in_=rn[:, 0:n]), reads=[rnk], writes=[rnk])
                    kkn, kknk = ew('dve', lambda e, t: e.tensor_tensor(out=t[:, 0:n], in0=kk[:, 0:n], in1=rn[:, 0:n], op=ALU.mult), [kkk, rnk])
                    pa, pak = C.ps()
                    C.op('pe', lambda e: e.matmul(pa[0:64, 0:n], lhsT=lw_sb[0:65, (2 + d) * 1024 + h * 64:(2 + d) * 1024 + (h + 1) * 64], rhs=xa[0:65, 0:n], start=True, stop=True),
                         reads=['lora', xak], writes=[pak])
                    ic, ick = ew('act', lambda e, t: e.activation(out=t[:, 0:n], in_=pa[0:64, 0:n], func=AF.Sigmoid), [pak])
                    bb, bbk = ew('dve', lambda e, t: e.tensor_tensor(out=t[:, 0:n], in0=kkn[:, 0:n], in1=ic[:, 0:n], op=ALU.mult), [kknk, ick])
                    tf, tfk = tfp.get()
                    C.op('dve', lambda e: e.tensor_scalar(out=tf[:, 0:n], in0=ic[:, 0:n], scalar1=kac, scalar2=oka[:, h:h + 1], op0=ALU.mult, op1=ALU.add), reads=[ick, 'rwc', 'oka'], writes=[tfk])
                    kd, kdk = ew('dve', lambda e, t: e.tensor_tensor(out=t[:, 0:n], in0=kt[:, 0:n], in1=tf[:, 0:n], op=ALU.mult), [kk_, tfk])
                    ar, ark = arp.get(); arv = ar[:, 0:nch * 128].rearrange("p (c x) -> p c x", x=128)
                    bk, bkk = bkp.get(); bkv = bk[:, 0:nch * 128].rearrange("p (c x) -> p c x", x=128)
                    bh, bhk = bhp.get(); bhv = bh[:, 0:nch * 128].rearrange("p (c x) -> p c x", x=128)
                    v3 = lambda t: t[:, 0:n].rearrange("p (c t) -> p c t", t=64)
                    C.op('dve', lambda e: e.scalar_tensor_tensor(out=arv[:, :, 0:64], in0=v3(kkn), scalar=-1.0, in1=v3(eLex), op0=ALU.mult, op1=ALU.mult), reads=[kknk, eLexk], writes=[ark])
                    if outs:
                        eLin, eLink = ew('act', lambda e, t: e.activation(out=t[:, 0:n], in_=Lin[:, 0:n], func=AF.Exp), [Link])
                        C.op('dve', lambda e: e.tensor_tensor(out=arv[:, :, 64:128], in0=v3(rt), in1=v3(eLin), op=ALU.mult), reads=[rk_, eLink], writes=[ark])
                    C.op('dve', lambda e: e.tensor_tensor(out=bkv[:, :, 0:64], in0=v3(bb), in1=v3(eNeg), op=ALU.mult), reads=[bbk, eNegk], writes=[bkk])
                    C.op('dve', lambda e: e.tensor_tensor(out=bkv[:, :, 64:128], in0=v3(kd), in1=v3(eNeg), op=ALU.mult), reads=[kdk, eNegk], writes=[bkk])
                    C.op('dve', lambda e: e.tensor_tensor(out=bhv[:, :, 0:64], in0=v3(bb), in1=v3(eH), op=ALU.mult), reads=[bbk, eHk], writes=[bhk])
                    C.op('dve', lambda e: e.tensor_tensor(out=bhv[:, :, 64:128], in0=v3(kd), in1=v3(eH), op=ALU.mult), reads=[kdk, eHk], writes=[bhk])
                    ysb_, ysbk_ = ysbp.get()
                    HS[h] = dict(kt=kt, kk_=kk_, vt=vt, vk_=vk_, rt=(rt if outs else None), rk_=(rk_ if outs else None), arv=arv, ark=ark, bkv=bkv, bkk=bkk,
                                 bhv=bhv, bhk=bhk, wc=wc, wck=wck, tf=tf, tfk=tfk, ysb=ysb_, ysbk=ysbk_)
                  def chunk_gen(h, c, slot):
                    Hh = HS[h]
                    vt, vk_, arv, ark, bkv, bkk, bhv, bhk, wc, wck = (Hh[k] for k in ('vt', 'vk_', 'arv', 'ark', 'bkv', 'bkk', 'bhv', 'bhk', 'wc', 'wck'))
                    sm = smr[slot]
                    nw = 128 if outs else 64
                    aT = arv[:, c, 0:64]
                    bT = bkv[:, c, 0:64]
                    kT = bkv[:, c, 64:128]
                    pt1, pt1k = C.psh()
                    C.op('pe', lambda e: e.transpose(pt1[0:64, 0:64], vt[:, c * 64:(c + 1) * 64], id64), reads=[vk_, 'ident'], writes=[pt1k])
                    C.op('pe', lambda e: e.transpose(pt1[0:64, 64:128], bhv[:, c, 0:64], id64), reads=[bhk, 'ident'], writes=[pt1k])
                    C.op('pe', lambda e: e.transpose(pt1[0:64, 128:192], bhv[:, c, 64:128], id64), reads=[bhk, 'ident'], writes=[pt1k])
                    vbk, vbkk = fixb[slot]['vbk']
                    C.op('act', lambda e: e.activation(out=vbk[:, 0:192], in_=pt1[0:64, 0:192], func=AF.Copy), reads=[pt1k], writes=[vbkk])
                    yield
                    Vt, Bh, Kh = vbk[:, 0:64], vbk[:, 64:128], vbk[:, 128:192]
                    p1, p1k = C.psh()
                    C.op('pe', lambda e: e.matmul(p1[0:64, 0:nw], lhsT=bT, rhs=arv[:, c, 0:nw], start=True, stop=True), reads=[bkk, ark], writes=[p1k])
                    NB, NBk = fixb[slot]['NB']
                    C.op('dve', lambda e: e.tensor_tensor(out=NB[:, 0:nw], in0=p1[0:64, 0:nw], in1=mS[:, 0:nw], op=ALU.mult), reads=[p1k, 'msk'], writes=[NBk])
                    p2, p2k = C.psh()
                    C.op('pe', lambda e: e.matmul(p2[0:64, 0:nw], lhsT=kT, rhs=arv[:, c, 0:nw], start=True, stop=True), reads=[bkk, ark], writes=[p2k])
                    NK, NKk = fixb[slot]['NK']
                    C.op('dve', lambda e: e.tensor_tensor(out=NK[:, 0:nw], in0=p2[0:64, 0:nw], in1=mS[:, 0:nw], op=ALU.mult), reads=[p2k, 'msk'], writes=[NKk])
                    p3, p3k = C.psh()
                    C.op('pe', lambda e: e.matmul(p3[0:64, 0:64], lhsT=aT, rhs=bT, start=True, stop=True), reads=[bkk, ark], writes=[p3k])
                    A, Ak = fixb[slot]['A']
                    C.op('dve', lambda e: e.tensor_tensor(out=A[:, 0:64], in0=p3[0:64, 0:64], in1=mST, op=ALU.mult), reads=[p3k, 'msk'], writes=[Ak])
                    yield
                    px, pxk = C.psh()
                    C.op('pe', lambda e: e.transpose(px[0:64, 0:64], aT.bitcast(F32), id64), reads=[ark, 'ident'], writes=[pxk])
                    C.op('pe', lambda e: e.matmul(px[0:64, 64:128], lhsT=NK[:, 0:64], rhs=Vt, start=True, stop=True), reads=[NKk, vbkk], writes=[pxk])
                    X, Xk = sm.get()
                    C.op('act', lambda e: e.activation(out=X[:, 0:128], in_=px[0:64, 0:128], func=AF.Copy), reads=[pxk], writes=[Xk])
                    yield
                    Nc, Nck, Ac, Ack = NB, NBk, A, Ak
                    for it in range(6):
                        pq, pqk = C.psh()
                        C.op('pe', lambda e: e.matmul(pq[0:64, 0:128], lhsT=Nc[:, 0:64], rhs=X[:, 0:128], start=True, stop=True), reads=[Nck, Xk], writes=[pqk])
                        X2, X2k = sm.get()
                        C.op('dve', lambda e: e.tensor_tensor(out=X2[:, 0:128], in0=pq[0:64, 0:128], in1=X[:, 0:128].bitcast(F32), op=ALU.add), reads=[pqk, Xk], writes=[X2k])
                        yield
                        X, Xk = X2, X2k
                        if it < 5:
                            pn, pnk = C.psh()
                            C.op('pe', lambda e: e.matmul(pn[0:64, 0:64], lhsT=Ac[:, 0:64], rhs=Nc[:, 0:64], start=True, stop=True), reads=[Ack, Nck], writes=[pnk])
                            C.op('pe', lambda e: e.matmul(pn[0:64, 64:128], lhsT=Nc[:, 0:64], rhs=Ac[:, 0:64], start=True, stop=True), reads=[Ack, Nck], writes=[pnk])
                            NA, NAk = sm.get()
                            C.op('act', lambda e: e.activation(out=NA[:, 0:128], in_=pn[0:64, 0:128], func=AF.Copy), reads=[pnk], writes=[NAk])
                            yield
                            Nc, Nck = NA[:, 0:64], NAk
                            Ac, Ack = NA[:, 64:128], NAk
                    Ap, U0 = X[:, 0:64], X[:, 64:128]
                    pm, pmk = C.psh()
                    C.op('pe', lambda e: e.matmul(pm[0:64, 0:64], lhsT=Ap, rhs=Bh, start=True, stop=True), reads=[Xk, vbkk], writes=[pmk])
                    C.op('pe', lambda e: e.matmul(pm[0:64, 64:128], lhsT=Bh, rhs=U0, start=True, stop=False), reads=[Xk, vbkk], writes=[pmk])
                    C.op('pe', lambda e: e.matmul(pm[0:64, 64:128], lhsT=Kh, rhs=Vt, start=False, stop=True), reads=[vbkk], writes=[pmk])
                    MS, MSk = sm.get()
                    C.op('dve', lambda e: e.scalar_tensor_tensor(out=MS[:, 0:64], in0=id64, scalar=wc[:, c:c + 1], in1=pm[0:64, 0:64], op0=ALU.mult, op1=ALU.add),
                         reads=[pmk, wck, 'ident'], writes=[MSk])
                    C.op('act', lambda e: e.activation(out=MS[:, 64:128], in_=pm[0:64, 64:128], func=AF.Copy), reads=[pmk], writes=[MSk])
                    yield
                    zk = f'Z{h}'
                    if outs:
                        pr, prk = C.psh()
                        C.op('pe', lambda e: e.matmul(pr[0:64, 0:64], lhsT=Ap, rhs=NB[:, 64:128], start=True, stop=True), reads=[Xk, NBk], writes=[prk])
                        Rp, Rpk = sm.get()
                        C.op('dve', lambda e: e.tensor_tensor(out=Rp[:, 0:64], in0=pr[0:64, 0:64], in1=arv[:, c, 64:128].bitcast(F32), op=ALU.add), reads=[prk, ark], writes=[Rpk])
                        yield
                        pyc, pyk = C.psh()
                        yc = pyc[0:64, 0:64]
                        C.op('pe', lambda e: e.matmul(yc, lhsT=Z[h][:], rhs=Rp[:, 0:64], start=True, stop=False), reads=[zk, Rpk], writes=[pyk])
                        C.op('pe', lambda e: e.matmul(yc, lhsT=U0, rhs=NB[:, 64:128], start=False, stop=False), reads=[Xk, NBk], writes=[pyk])
                        C.op('pe', lambda e: e.matmul(yc, lhsT=Vt, rhs=NK[:, 64:128], start=False, stop=True), reads=[vbkk, NKk], writes=[pyk])
                        C.op('act', lambda e: e.activation(out=Hh['ysb'][:, c * 64:(c + 1) * 64], in_=yc, func=AF.Copy), reads=[pyk], writes=[Hh['ysbk']])
                        yield
                    pzz, pzzk = C.psh()
                    C.op('pe', lambda e: e.matmul(pzz[0:64, 0:64], lhsT=MS[:, 0:64], rhs=Z[h][:], start=True, stop=True), reads=[MSk, zk], writes=[pzzk])
                    C.op('dve', lambda e: e.tensor_tensor(out=Z[h][:], in0=pzz[0:64, 0:64], in1=MS[:, 64:128].bitcast(F32), op=ALU.add), reads=[pzzk, MSk], writes=[zk])
                    yield

                  order = range(nch) if d == 0 else range(nch - 1, -1, -1)
                  for c in order:
                      active = [chunk_gen(h, c, h - g0) for h in range(g0, g0 + GRP)]
                      while active:
                          for gnr in list(active):
                              try:
                                  next(gnr)
                              except StopIteration:
                                  active.remove(gnr)
                  for h in range(g0, g0 + GRP):
                    Hh = HS[h]
                    hc = slice(h * 64, (h + 1) * 64)
                    kkc, kac, rkc, lgc, lbc = [rwcv[:, h, q:q + 1] for q in range(5)]
                    kt, kk_, vt, vk_, rt, rk_, tf, tfk = (Hh[k] for k in ('kt', 'kk_', 'vt', 'vk_', 'rt', 'rk_', 'tf', 'tfk'))
                    pyv, pyk = Hh['ysb'], Hh['ysbk']
                    if not outs:
                        continue
                    if d == 0:
                        ysb, ysk = ew('act', lambda e, t: e.activation(out=t[:, 0:n], in_=pyv[:, 0:n], func=AF.Copy), [pyk])
                        C.dma('sp', yfT[hc, x0:x0 + n], ysb[:, 0:n], reads=[ysk], writes=['yfT'])
                        continue
                    yf, yfk = tp.get()
                    C.dma('sp', yf[:, 0:n], yfT[hc, x0:x0 + n], reads=['yfT'], writes=[yfk])
                    ys, ysk = ew('dve', lambda e, t: e.tensor_tensor(out=t[:, 0:n], in0=pyv[:, 0:n], in1=yf[:, 0:n], op=ALU.add), [pyk, yfk])
                    pmn, pmnk = C.ps()
                    C.op('pe', lambda e: e.matmul(pmn[0:64, 0:n], lhsT=ones64, rhs=ys[:, 0:n], start=True, stop=True), reads=[ysk, 'ones'], writes=[pmnk])
                    ycn, ycnk = ew('dve', lambda e, t: e.scalar_tensor_tensor(out=t[:, 0:n], in0=pmn[0:64, 0:n], scalar=-1.0 / 64, in1=ys[:, 0:n], op0=ALU.mult, op1=ALU.add), [pmnk, ysk])
                    sq2, sq2k = ew('act', lambda e, t: e.activation(out=t[:, 0:n], in_=ycn[:, 0:n], func=AF.Square), [ycnk])
                    pvr, pvrk = C.ps()
                    C.op('pe', lambda e: e.matmul(pvr[0:64, 0:n], lhsT=ones64, rhs=sq2[:, 0:n], start=True, stop=True), reads=[sq2k, 'ones'], writes=[pvrk])
                    rsd, rsdk = ew('act', lambda e, t: e.activation(out=t[:, 0:n], in_=pvr[0:64, 0:n], func=AF.Sqrt, bias=64e-5, scale=1.0 / 64), [pvrk])
                    C.op('dve', lambda e: e.reciprocal(out=rsd[:, 0:n], in_=rsd[:, 0:n]), reads=[rsdk], writes=[rsdk])
                    yn, ynk = ew('dve', lambda e, t: e.tensor_tensor(out=t[:, 0:n], in0=ycn[:, 0:n], in1=rsd[:, 0:n], op=ALU.mult), [ycnk, rsdk])
                    o1, o1k = ew('act', lambda e, t: e.activation(out=t[:, 0:n], in_=yn[:, 0:n], func=AF.Identity, bias=lbc, scale=lgc), [ynk, 'rwc'])
                    pa0, pa0k = C.ps()
                    C.op('pe', lambda e: e.matmul(pa0[0:64, 0:n], lhsT=lw_sb[0:65, 2 * 1024 + h * 64:2 * 1024 + (h + 1) * 64], rhs=xa0[0:65, 0:n], start=True, stop=True),
                         reads=['lora', xa0k], writes=[pa0k])
                    ic0, ic0k = ew('act', lambda e, t: e.activation(out=t[:, 0:n], in_=pa0[0:64, 0:n], func=AF.Sigmoid), [pa0k])
                    C.op('dve', lambda e: e.tensor_scalar(out=ic0[:, 0:n], in0=ic0[:, 0:n], scalar1=kac, scalar2=oka[:, h:h + 1], op0=ALU.mult, op1=ALU.add), reads=[ic0k, 'rwc', 'oka'], writes=[ic0k])
                    C.op('dve', lambda e: e.tensor_tensor(out=ic0[:, 0:n], in0=ic0[:, 0:n], in1=tf[:, 0:n], op=ALU.add), reads=[ic0k, tfk], writes=[ic0k])
                    C.op('dve', lambda e: e.tensor_tensor(out=ic0[:, 0:n], in0=ic0[:, 0:n], in1=kt[:, 0:n], op=ALU.mult), reads=[ic0k, kk_], writes=[ic0k])
                    rk2, rk2k = ew('dve', lambda e, t: e.scalar_tensor_tensor(out=t[:, 0:n], in0=rt[:, 0:n], scalar=rkc, in1=ic0[:, 0:n], op0=ALU.mult, op1=ALU.mult), [rk_, ic0k, 'rwc'])
                    pb, pbk = C.ps()
                    C.op('pe', lambda e: e.matmul(pb[0:64, 0:n], lhsT=ones64, rhs=rk2[:, 0:n], start=True, stop=True), reads=[rk2k, 'ones'], writes=[pbk])
                    bon, bonk = ew('dve', lambda e, t: e.tensor_tensor(out=t[:, 0:n], in0=pb[0:64, 0:n], in1=vt[:, 0:n], op=ALU.mult), [pbk, vk_])
                    C.op('dve', lambda e: e.tensor_tensor(out=o1[:, 0:n], in0=o1[:, 0:n], in1=bon[:, 0:n], op=ALU.add), reads=[o1k, bonk], writes=[o1k])
                    pg, pgk = C.ps()
                    C.op('pe', lambda e: e.matmul(pg[0:64, 0:n], lhsT=wg_a[:, hc], rhs=xg[:, 0:n], start=True, stop=False), reads=['wg', xgk], writes=[pgk])
                    C.op('pe', lambda e: e.matmul(pg[0:64, 0:n], lhsT=wg_b[:, hc], rhs=xg2[:, 0:n], start=False, stop=True), reads=['wg', xg2k], writes=[pgk])
                    ob, obk = outp.get()
                    C.op('dve', lambda e: e.tensor_tensor(out=ob[:, 0:n], in0=pg[0:64, 0:n], in1=o1[:, 0:n], op=ALU.mult), reads=[pgk, o1k], writes=[obk])
                    C.dma('sp', attnT[1024 + h * 64:1024 + (h + 1) * 64, x0:x0 + n], ob[:, 0:n], reads=[obk], writes=['attnT'])
        C.nrot = 8

    def phase_mlp(self, L, tiles, attn, xsrc, out_fn, final=False):
        C = self.C
        Wo = self.din(f'w_out{L}', [D, D]).rearrange("(k p) c -> p k c", p=128)
        W1 = self.din(f'mlp_w1_{L}', [D, 4 * D]).rearrange("(k p) c -> p k c", p=128)
        W2 = self.din(f'mlp_w2_{L}', [4 * D, D]).rearrange("(k p) c -> p k c", p=128)
        self.wload_init(256, 2, 3)
        xp = Pool(self, 'm_x', [128, KC * 512], F32, 1)
        ap_ = Pool(self, 'm_a', [128, KC * 512], BF16, 1)
        hp = Pool(self, 'm_h', [128, KC * 512], BF16, 1)
        hid = self.sb('m_hid', [128, 64 * 512], BF16)
        sqp = Pool(self, 'm_sq', [128, 512], F32, 3)
        tmpp = Pool(self, 'm_tm', [128, 512], F32, 3)
        ofp = Pool(self, 'm_of', [128, 512], F32, 2) if final else None
        av_src = attn.rearrange("(k p) t -> p k t", p=128)
        xv_src = xsrc.rearrange("(k p) t -> p k t", p=128)
        wl = []
        for _ in tiles:
            wl += [(Wo, c0, 256, KC, 0) for c0 in range(0, D, 256)]
            wl += [(W1, c0, 256, KC, 0) for c0 in range(0, 4 * D, 256)]
            wl += [(W2, c0, 256, 16, k0) for c0 in range(0, D, 256) for k0 in range(0, 64, 16)]
        self.wstream(wl, 1)
        for (a0, n, v, xs0) in tiles:
            xt, xk = xp.get()
            xv = xt[:].rearrange("p (k t) -> p k t", k=KC)[:, :, 0:n]
            at, ak = ap_.get()
            av = at[:].rearrange("p (k t) -> p k t", k=KC)[:, :, 0:n]
            for kq in range(4):
                C.dma('sp', xv[:, kq * 4:(kq + 1) * 4, :], xv_src[:, kq * 4:(kq + 1) * 4, xs0:xs0 + n], reads=['resT'], writes=[xk])
                C.dma('sp', av[:, kq * 4:(kq + 1) * 4, :], av_src[:, kq * 4:(kq + 1) * 4, a0:a0 + n], reads=['attnT'], writes=[ak])
            for c0 in range(0, D, 256):
                wv, wk = self.wnext()
                for m0 in (0, 128):
                    dt = (c0 + m0) // 128
                    pt, pk = C.ps()
                    for kc in range(KC):
                        C.op('pe', lambda e, kc=kc: e.matmul(pt[:, 0:n], lhsT=wv[:, kc, m0:m0 + 128], rhs=av[:, kc, :], start=(kc == 0), stop=(kc == KC - 1)),
                             reads=[wk, ak], writes=[pk], signal=(kc == KC - 1))
                    C.op('dve', lambda e: e.scalar_tensor_tensor(out=xv[:, dt, :], in0=pt[:, 0:n], scalar=self.mcol(L, 2, dt, v), in1=xv[:, dt, :], op0=ALU.mult, op1=ALU.add),
                         reads=[pk, xk, 'mod'], writes=[xk])
            if f'xmid{L}' in self.debug:
                o = self.outs.get(f'dbg_xmid{L}') or self.dout(f'dbg_xmid{L}', [D, NX])
                ov = o.rearrange("(k p) t -> p k t", p=128)
                for kq in range(4):
                    C.dma('sp', ov[:, kq * 4:(kq + 1) * 4, a0:a0 + n], xv[:, kq * 4:(kq + 1) * 4, :], reads=[xk])
            ht, hk = hp.get()
            hv = ht[:].rearrange("p (k t) -> p k t", k=KC)[:, :, 0:n]
            self.norm_mod(xv, xk, n, hv, hk, lambda kc: self.Av[:, L, 1, kc, v:v + 1], lambda kc: self.mcol(L, 3, kc, v), tmpp, sqp)
            hidv = hid[:].rearrange("p (k t) -> p k t", k=64)[:, :, 0:n]
            for c0 in range(0, 4 * D, 256):
                wv, wk = self.wnext()
                for m0 in (0, 128):
                    ht_i = (c0 + m0) // 128
                    pt, pk = C.ps()
                    for kc in range(KC):
                        C.op('pe', lambda e, kc=kc: e.matmul(pt[:, 0:n], lhsT=wv[:, kc, m0:m0 + 128], rhs=hv[:, kc, :], start=(kc == 0), stop=(kc == KC - 1)),
                             reads=[wk, hk], writes=[pk], signal=(kc == KC - 1))
                    tm, tmk = tmpp.get()
                    C.op('act', lambda e: e.activation(out=tm[:, 0:n], in_=pt[:, 0:n], func=AF.Relu), reads=[pk], writes=[tmk])
                    eng = C.pick('m_sq', ['pool', 'dve'])
                    C.op(eng, lambda e: e.tensor_tensor(out=hidv[:, ht_i, :], in0=tm[:, 0:n], in1=tm[:, 0:n], op=ALU.mult), reads=[tmk], writes=['hid'])
            for c0 in range(0, D, 256):
                pts = [C.ps(), C.ps()]
                for k0 in range(0, 64, 16):
                    wv, wk = self.wnext()
                    for mi, m0 in enumerate((0, 128)):
                        pt, pk = pts[mi]
                        for kc in range(16):
                            C.op('pe', lambda e, kc=kc: e.matmul(pt[:, 0:n], lhsT=wv[:, kc, m0:m0 + 128], rhs=hidv[:, k0 + kc, :],
                                                                start=(k0 + kc == 0), stop=(k0 + kc == 63)),
                                 reads=[wk, 'hid'], writes=[pk], signal=(kc == 15))
                for mi, m0 in enumerate((0, 128)):
                    dt = (c0 + m0) // 128
                    pt, pk = pts[mi]
                    C.op('dve', lambda e: e.scalar_tensor_tensor(out=xv[:, dt, :], in0=pt[:, 0:n], scalar=self.mcol(L, 5, dt, v), in1=xv[:, dt, :], op0=ALU.mult, op1=ALU.add),
                         reads=[pk, xk, 'mod'], writes=[xk])
            if not final:
                out_fn(xv, xk, a0, n)
            else:
                pt, pk = C.ps()
                for kc in range(KC):
                    sq, sqk = sqp.get()
                    C.op('act', lambda e, kc=kc: e.activation(out=sq[:, 0:n], in_=xv[:, kc, :], func=AF.Square), reads=[xk], writes=[sqk])
                    C.op('pe', lambda e, kc=kc: e.matmul(pt[:, 0:n], lhsT=self.ones[:], rhs=sq[:, 0:n], start=(kc == 0), stop=(kc == KC - 1)), reads=[sqk, 'ones'], writes=[pk])
                rs, rsk = sqp.get()
                C.op('act', lambda e: e.activation(out=rs[:, 0:n], in_=pt[:, 0:n], func=AF.Sqrt, bias=EPS, scale=1.0 / D), reads=[pk], writes=[rsk])
                C.op('dve', lambda e: e.reciprocal(out=rs[:, 0:n], in_=rs[:, 0:n]), reads=[rsk], writes=[rsk])
                for kc in range(KC):
                    of, ofk = ofp.get()
                    C.op('dve', lambda e, kc=kc: e.scalar_tensor_tensor(out=of[:, 0:n], in0=xv[:, kc, :], scalar=self.nrm[:, 64 + kc:64 + kc + 1], in1=rs[:, 0:n], op0=ALU.mult, op1=ALU.mult),
                         reads=[xk, rsk, 'nrm'], writes=[ofk])
                    out_fn(of, ofk, kc, a0, n)

    def phase_proj1(self):
        C = self.C
        resT = self.scr['resT']
        xsrc = resT.rearrange("(k p) t -> p k t", p=128)
        W = self.din('w_qkv', [D, 3 * D]).rearrange("(k p) c -> p k c", p=128)
        q1T = self.dscr('q1T', [D, NOWN], BF16)
        k1T = self.dscr('k1T', [D, NX], BF16)
        v1 = self.dscr('v1', [NX, D], BF16)
        self.wload_init(256, 2, 3)
        xp = Pool(self, 'q_x', [128, KC * 512], F32, 2)
        hp = Pool(self, 'q_h', [128, KC * 512], BF16, 2)
        sqp = Pool(self, 'q_sq', [128, 512], F32, 3)
        tmpp = Pool(self, 'q_tm', [128, 512], F32, 3)
        evp = Pool(self, 'q_ev', [128, 512], BF16, 4)
        tiles = [(i * 512, 512, 0, True) for i in range(4)] + [(2048, 256, 0, False), (NE, CT, 1, False)]
        wl = []
        for (t0, n, v, own) in tiles:
            wl += [(W, c0, 256, KC, 0) for c0 in range(0 if own else D, 3 * D, 256)]
        self.wstream(wl, 1)
        for (t0, n, v, own) in tiles:
            xt, xk = xp.get()
            xv = xt[:].rearrange("p (k t) -> p k t", k=KC)[:, :, 0:n]
            for kq in range(4):
                C.dma('sp', xv[:, kq * 4:(kq + 1) * 4, :], xsrc[:, kq * 4:(kq + 1) * 4, t0:t0 + n], reads=['resT'], writes=[xk])
            ht, hk = hp.get()
            hv = ht[:].rearrange("p (k t) -> p k t", k=KC)[:, :, 0:n]
            self.norm_mod(xv, xk, n, hv, hk, lambda kc: self.Av[:, 1, 0, kc, v:v + 1], lambda kc: self.mcol(1, 0, kc, v), tmpp, sqp)
            for c0 in range(0 if own else D, 2 * D, 256):
                wv, wk = self.wnext()
                for m0 in (0, 128):
                    pt, pk = C.ps()
                    for kc in range(KC):
                        C.op('pe', lambda e, kc=kc: e.matmul(pt[:, 0:n], lhsT=wv[:, kc, m0:m0 + 128], rhs=hv[:, kc, :], start=(kc == 0), stop=(kc == KC - 1)),
                             reads=[wk, hk], writes=[pk], signal=(kc == KC - 1))
                    ev, evk = evp.get()
                    isq = c0 < D
                    C.op('act', lambda e: e.activation(out=ev[:, 0:n], in_=pt[:, 0:n], func=AF.Copy, scale=(0.125 if isq else 1.0)), reads=[pk], writes=[evk])
                    cc = c0 + m0
                    if isq:
                        C.dma('sp', q1T[cc:cc + 128, t0:t0 + n], ev[:, 0:n], reads=[evk], writes=['q1T'])
                    else:
                        C.dma('sp', k1T[cc - D:cc - D + 128, t0:t0 + n], ev[:, 0:n], reads=[evk], writes=['k1T'])
            for c0 in range(2 * D, 3 * D, 256):
                wv, wk = self.wnext()
                for s0 in range(0, n, 128):
                    pt, pk = C.ps()
                    for kc in range(KC):
                        C.op('pe', lambda e, kc=kc: e.matmul(pt[:, 0:256], lhsT=hv[:, kc, s0:s0 + 128], rhs=wv[:, kc, :], start=(kc == 0), stop=(kc == KC - 1)),
                             reads=[wk, hk], writes=[pk], signal=(kc == KC - 1))
                    ev, evk = evp.get()
                    C.op('act', lambda e: e.activation(out=ev[:, 0:256], in_=pt[:, 0:256], func=AF.Copy), reads=[pk], writes=[evk])
                    C.dma('sp', v1[t0 + s0:t0 + s0 + 128, c0 - 2 * D:c0 - 2 * D + 256], ev[:, 0:256], reads=[evk], writes=['v1'])

    def phase_na(self):
        C = self.C
        q1T, k1T, v1 = self.scr['q1T'], self.scr['k1T'], self.scr['v1']
        a1T = self.dscr('attn1T', [D, NOWN], BF16)
        nab = self.din('na_bias', [32, 128, 15 * 128])
        qp = Pool(self, 'n_q', [128, NOWN], BF16, 2)
        kp = Pool(self, 'n_k', [128, NX], BF16, 2)
        vsp = Pool(self, 'n_vs', [128, 20 * 128], BF16, 2)
        vap = Pool(self, 'n_va', [128, 20 * 2 * 128], BF16, 2)
        bp = Pool(self, 'n_b', [128, 15 * 128], F32, 2)
        sbp = Pool(self, 'n_sb', [128, 128], F32, 6)
        ptp = Pool(self, 'n_pt', [128, 128], BF16, 8)
        osp = Pool(self, 'n_os', [128, 512], F32, 2)
        rsp = Pool(self, 'n_rs', [64, 512], F32, 2)
        onp = Pool(self, 'n_on', [128, 512], BF16, 2)
        C.nrot = 6
        acc_i = 0
        vsrc = v1.rearrange("(c p) x -> p c x", p=128)
        for tp_ in range(16):
            qt, qk = qp.get()
            kt, kk = kp.get()
            C.dma('sp', qt[:], q1T[tp_ * 128:(tp_ + 1) * 128, :], reads=['q1T'], writes=[qk])
            C.dma('sp', kt[:], k1T[tp_ * 128:(tp_ + 1) * 128, :], reads=['k1T'], writes=[kk])
            vs, vsk = vsp.get()
            vsv = vs[:].rearrange("p (c x) -> p c x", c=20)
            for c4 in range(0, 20, 5):
                C.dma('sp', vsv[:, c4:c4 + 5, :], vsrc[:, c4:c4 + 5, tp_ * 128:(tp_ + 1) * 128], reads=['v1'], writes=[vsk])
            va, vak = vap.get()
            vav = va[:].rearrange("p (c h d) -> p c h d", c=20, h=2)
            C.op('dve', lambda e: e.memset(va[:], 1.0), writes=[vak])
            for hh in range(2):
                C.op('dve', lambda e, hh=hh: e.tensor_copy(out=vav[:, :, hh, 0:64], in_=vsv[:, :, hh * 64:(hh + 1) * 64]), reads=[vsk], writes=[vak])
            for hh in range(2):
                h = tp_ * 2 + hh
                base = hh * 64
                bt, bk = bp.get()
                for q5 in range(0, 15, 5):
                    C.dma('sp', bt[:, q5 * 128:(q5 + 5) * 128], nab[h, :, q5 * 128:(q5 + 5) * 128], writes=[bk])
                for qg in range(4):
                    po = C.ps_tiles[6 + acc_i % 2]
                    pok = f'ps{6 + acc_i % 2}'
                    acc_i += 1
                    pend = []

                    def pv(item):
                        pb, pbk, kc, ci, qi, nchk = item
                        C.op('pe', lambda e: e.matmul(po[:, qi * 128:(qi + 1) * 128], lhsT=vav[:, kc, hh, :], rhs=pb[:], start=(ci == 0), stop=(ci == nchk - 1)),
                             reads=[vak, pbk], writes=[pok])
                    for qi in range(4):
                        qb = qg * 4 + qi
                        cls = min(qb, 2)
                        cs = max(qb - 2, 0)
                        chunks = [(cs + j, cls * 5 + j) for j in range(5)] + [(18, None), (19, None)]
                        for ci, (kc, var) in enumerate(chunks):
                            pt, pk = C.ps()
                            C.op('pe', lambda e: e.matmul(pt[:, 0:128], lhsT=kt[base:base + 64, kc * 128:(kc + 1) * 128], rhs=qt[base:base + 64, qb * 128:(qb + 1) * 128], start=True, stop=True),
                                 reads=[kk, qk], writes=[pk])
                            pb, pbk = ptp.get()
                            if var is not None:
                                sb_, sbk = sbp.get()
                                C.op('dve', lambda e: e.tensor_tensor(out=sb_[:], in0=pt[:, 0:128], in1=bt[:, var * 128:(var + 1) * 128], op=ALU.add), reads=[pk, bk], writes=[sbk])
                                C.op('act', lambda e: e.activation(out=pb[:], in_=sb_[:], func=AF.Exp), reads=[sbk], writes=[pbk])
                            else:
                                C.op('act', lambda e: e.activation(out=pb[:], in_=pt[:, 0:128], func=AF.Exp), reads=[pk], writes=[pbk])
                            pend.append((pb, pbk, kc, ci, qi, len(chunks)))
                            if len(pend) > 3:
                                pv(pend.pop(0))
                    while pend:
                        pv(pend.pop(0))
                    osb, osk = osp.get()
                    C.op('dve', lambda e: e.tensor_copy(out=osb[:], in_=po[:, 0:512]), reads=[pok], writes=[osk])
                    rs, rsk = rsp.get()
                    C.dma('sp', rs[:], osb[64:128, :], reads=[osk], writes=[rsk])
                    C.op('dve', lambda e: e.reciprocal(out=rs[:], in_=rs[:]), reads=[rsk], writes=[rsk])
                    on, onk = onp.get()
                    C.op('dve', lambda e: e.tensor_tensor(out=on[0:64, :], in0=osb[0:64, :], in1=rs[:], op=ALU.mult), reads=[osk, rsk], writes=[onk])
                    C.dma('sp', a1T[h * 64:(h + 1) * 64, qg * 512:(qg + 1) * 512], on[0:64, :], reads=[onk], writes=['attn1T'])
        C.nrot = 8

    def _dbg_h(self, hv, hk, t0, n):
        C = self.C
        if 'h0' not in self.outs:
            self.dout('h0', [D, S0], BF16)
        o = self.outs['h0'].rearrange("(k p) t -> p k t", p=128)
        for kq in range(4):
            C.dma('sp', o[:, kq * 4:(kq + 1) * 4, t0:t0 + n], hv[:, kq * 4:(kq + 1) * 4, :], reads=[hk])

    def dbg_copy_scr(self, name, src, rows, cols, dt=F32):
        C = self.C
        o = self.dout('dbg_' + name, [rows, cols], dt)
        for r0 in range(0, rows, 128):
            r1 = min(rows, r0 + 128)
            C.dma('sp', o[r0:r1, :], src[r0:r1, :], reads=[name])


def build(debug=(), upto=99):
    P = Prog(debug)
    C = P.C
    P.load_consts()
    P.phase_mod()
    if 'mod' in P.debug:
        o = P.dout('dbg_mod', [128, 384])
        C.dma('sp', o[:, :], P.mod[:], reads=['mod'])
    if upto >= 1:
        P.begin_phase()
        P.phase_proj0()
        if 'pT' in P.debug:
            C.barrier()
            P.dbg_copy_scr('pT', P.scr['pT'], NCOL0, S0)
            P.dbg_copy_scr('vtok', P.scr['vtok'], S0, 256, BF16)
    if upto >= 2:
        P.begin_phase()
        P.phase_gqa()
    if upto >= 3:
        P.begin_phase()
        P.phase_rwkv_shift()
        if 'rwT' in P.debug:
            C.barrier()
            P.dbg_copy_scr('rwT', P.scr['rwT'], RW, S0)
    if upto >= 4:
        P.begin_phase()
        P.phase_rwkv_scan()
    if upto >= 2 and 'attnT' in P.debug:
        C.barrier()
        P.dbg_copy_scr('attnT', P.scr['attnT'], D if upto >= 4 else 1024, NX, BF16)
    if upto >= 5:
        P.begin_phase()
        resT = P.dscr('resT', [D, NX])
        rv = resT.rearrange("(k p) t -> p k t", p=128)
        tiles0 = [(i * 512, 512, 0, i * 512) for i in range(4)] + [(2048, 256, 0, 2048), (NE, CT, 1, T)]

        def out0(xv, xk, a0, n):
            for kq in range(4):
                C.dma('sp', rv[:, kq * 4:(kq + 1) * 4, a0:a0 + n], xv[:, kq * 4:(kq + 1) * 4, :], reads=[xk], writes=['resT'])
        P.phase_mlp(0, tiles0, P.scr['attnT'], P.inp['xT'], out0)
        if 'resT' in P.debug:
            C.barrier()
            P.dbg_copy_scr('resT', resT, D, NX)
    if upto >= 6:
        P.begin_phase()
        P.phase_proj1()
    if upto >= 7:
        P.begin_phase()
        P.phase_na()
        if 'attn1T' in P.debug:
            C.barrier()
            P.dbg_copy_scr('attn1T', P.scr['attn1T'], D, NOWN, BF16)
    if upto >= 8:
        P.begin_phase()
        outT = P.dout('outT', [D, NOWN])
        tiles1 = [(i * 512, 512, 0, i * 512) for i in range(4)]

        def out1(of, ofk, kc, a0, n):
            C.dma('sp', outT[kc * 128:(kc + 1) * 128, a0:a0 + n], of[:, 0:n], reads=[ofk], writes=['outT'])
        P.phase_mlp(1, tiles1, P.scr['attn1T'], P.scr['resT'], out1, final=True)
    C.finish('sp')
    return P


def pk(v):
    v = np.asarray(v, np.float32)
    return np.ascontiguousarray(v.reshape(-1, 128).T)


def rw_perm(flip):
    grp = [1, 0, 3, 2] if flip else [0, 1, 2, 3]
    ii = np.arange(872)
    if flip:
        ii = np.concatenate([ii[:768], ii[784:800], ii[768:784], ii[816:832], ii[800:816], ii[832:]])
    return np.concatenate([4 * ii + g for g in grp])


def head_perm(flip, n=16):
    grp = [1, 0, 3, 2] if flip else [0, 1, 2, 3]
    return np.concatenate([4 * np.arange(n) + g for g in grp])


def na_bias_tables(rpb, flip):
    out = np.full((32, 128, 15, 128), -30000.0, np.float32)
    kc = np.arange(64)
    qc = np.arange(64)
    for cls, qb in enumerate((0, 1, 4)):
        cs = max(2 * qb - 4, 0)
        for j in range(5):
            var = cls * 5 + j
            for a in range(2):
                for m_ in range(2):
                    kr = cs + 2 * j + a
                    qr = 2 * qb + m_
                    if flip:
                        kro, qro = 63 - kr, 63 - qr
                        kco, qco = 63 - kc, 63 - qc
                    else:
                        kro, qro, kco, qco = kr, qr, kc, qc
                    rs = min(max(qro - 4, 0), 56)
                    if not (rs <= kro < rs + 8):
                        continue
                    cstart = np.clip(qco - 8, 0, 48)
                    valid = (kco[:, None] >= cstart[None, :]) & (kco[:, None] < cstart[None, :] + 16)
                    dcol = kco[:, None] - qco[None, :] + 15
                    drow = kro - qro + 7
                    vals = rpb[:, drow, :][:, np.clip(dcol, 0, 30)]
                    blk = np.where(valid[None], vals, np.float32(-30000.0))
                    out[:, a * 64:(a + 1) * 64, var, m_ * 64:(m_ + 1) * 64] = blk
    return np.ascontiguousarray(out.reshape(32, 128, 15 * 128))


_SHARED = {}


def host_inputs(I, b, half):
    flip = (half == 1)
    m = {}
    x = I['x'][b]
    cx = I['ctx'][b]
    if flip:
        x = x[::-1]
        cx = cx[::-1]
    m['xT'] = np.ascontiguousarray(np.concatenate([x, cx], 0).T)
    cc = np.stack([pk(I['c'][b]), pk(I['c_ctx'])], -1).reshape(128, 32)
    m['ccol'] = np.ascontiguousarray(cc)
    key = ('shared', flip)
    if key in _SHARED:
        m.update(_SHARED[key])
        return m
    sh = {}
    sh['ident'] = np.eye(128, dtype=np.float32)
    ob = np.zeros((128, 128), np.float32)
    ob[:64, :64] = 1
    ob[64:, 64:] = 1
    sh['ones_blk'] = ob
    sh['ada_b'] = np.ascontiguousarray(np.concatenate([pk(I['l0_ada_b']), pk(I['l1_ada_b'])], 1))
    sh['nrm'] = np.ascontiguousarray(np.concatenate([pk(I[k]) for k in ('l0_norm1', 'l0_norm2', 'l1_norm1', 'l1_norm2', 'final_norm')], 1))
    sh['ada_w0'] = I['l0_ada_w']
    sh['ada_w1'] = I['l1_ada_w']
    w_in = I['l0_w_in']
    rwp = rw_perm(flip)
    perm = np.concatenate([np.arange(GQ), GQ + rwp])
    sh['w_in'] = np.ascontiguousarray(w_in[:, perm])
    qn = np.tile(I['l0_q_norm'], 2) * np.float32(64 ** -0.5)
    kn = np.tile(I['l0_k_norm'], 2)
    sh['qkn'] = np.ascontiguousarray(np.stack([qn, kn], 1).astype(np.float32))
    rot = np.zeros((128, 128), np.float32)
    for mm in range(128):
        if mm % 64 < 32:
            rot[mm + 32, mm] = -1.0
        else:
            rot[mm - 32, mm] = 1.0
    sh['rotm'] = rot
    tt = np.arange(T)
    if flip:
        tt = tt[::-1]
    row = (tt // 64).astype(np.float32)
    col = (tt % 64).astype(np.float32)
    inv = (np.float32(10000.0) ** (-np.arange(16, dtype=np.float32) / np.float32(16))).astype(np.float32)
    ang = np.concatenate([row[:, None] * inv, col[:, None] * inv], -1).astype(np.float32)
    cs = np.cos(ang).astype(np.float32)
    sn = np.sin(ang).astype(np.float32)
    idx = np.arange(128) % 32
    sh['rope_cos'] = np.ascontiguousarray(cs[:, idx].T)
    sh['rope_sin'] = np.ascontiguousarray(sn[:, idx].T)
    mu = I['l0_shift_mu'][rwp]
    mc = np.zeros((128, 28), np.float32)
    for g in range(4):
        for j in range(7):
            rows = min(128, 872 - j * 128)
            mc[:rows, g * 7 + j] = mu[g * 872 + j * 128:g * 872 + j * 128 + rows]
    sh['mu_col'] = mc
    hp = head_perm(flip)
    chan = (np.arange(16)[:, None] * 64 + hp[None, :]).reshape(-1)
    rwc = np.zeros((64, 16, 5), np.float32)
    for q, nm in enumerate(('l0_k_k', 'l0_k_a', None, 'l0_lnx_g', 'l0_lnx_b')):
        vec = I['l0_r_k'].reshape(-1) if nm is None else I[nm]
        rwc[:, :, q] = vec[chan].reshape(16, 64).T
    sh['rw_cols'] = np.ascontiguousarray(rwc.reshape(64, 80))
    dn = ('b', 'f') if flip else ('f', 'b')
    lw = np.zeros((65, 4, 1024), np.float32)
    for d in range(2):
        lw[:64, d] = I[f'l0_ww2_{dn[d]}'][hp][:, chan]
        lw[64, d] = I[f'l0_w0_{dn[d]}'][chan]
        lw[:64, 2 + d] = I[f'l0_wa2_{dn[d]}'][hp][:, chan]
        lw[64, 2 + d] = I[f'l0_a0_{dn[d]}'][chan]
    sh['lora_w'] = np.ascontiguousarray(lw.reshape(65, 4096))
    grp = [1, 0, 3, 2] if flip else [0, 1, 2, 3]
    gp = np.concatenate([4 * np.arange(32) + g for g in grp] + [4 * np.arange(32, 40) + g for g in grp])
    sh['wg2'] = np.ascontiguousarray(I['l0_wg2'][gp][:, chan])
    s_ = np.arange(64)
    Ms = (s_[:, None] < s_[None, :]).astype(np.float32)
    Mi = (s_[:, None] <= s_[None, :]).astype(np.float32)
    sh['scan_masks'] = np.ascontiguousarray(np.concatenate([Ms, Mi, Ms.T, Ms.T, Mi.T, Ms], 1))
    rst = np.ones((64, 512), np.float32)
    rst[:, ::64] = 0
    sh['chunk_rst'] = rst
    wo = I['l0_w_out']
    sh['w_out0'] = np.ascontiguousarray(np.concatenate([wo[:1024], wo[1024 + chan]], 0))
    sh['mlp_w1_0'] = I['l0_mlp_w1']
    sh['mlp_w2_0'] = I['l0_mlp_w2']
    sh['w_qkv'] = I['l1_w_qkv']
    sh['na_bias'] = na_bias_tables(I['l1_rpb'], flip)
    sh['w_out1'] = I['l1_w_out']
    sh['mlp_w1_1'] = I['l1_mlp_w1']
    sh['mlp_w2_1'] = I['l1_mlp_w2']
    _SHARED[key] = sh
    m.update(sh)
    return m


_PROG = {}


def kernel(**inputs):
    I = {k: np.asarray(v) for k, v in inputs.items()}
    if 'p' not in _PROG:
        _PROG['p'] = build()
    P = _PROG['p']
    _SHARED.clear()
    in_maps = []
    for b in range(4):
        for half in range(2):
            m = host_inputs(I, b, half)
            in_maps.append({k: v for k, v in m.items() if k in P.inp})
    res = run_bass_kernel_spmd(P.nc, in_maps, core_ids=list(range(8)))
    out = np.empty((4, T, D), np.float32)
    for b in range(4):
        o0 = np.asarray(res.results[2 * b]['outT'])
        o1 = np.asarray(res.results[2 * b + 1]['outT'])
        out[b, :NOWN] = o0.T
        out[b, NOWN:] = o1.T[::-1]
    _SHARED.clear()
    return out
```
